# Optimizing a Trainium2 kernel written in Bass

```python
import jax, jax.numpy as jnp
from jax import lax
import numpy as np

D_MODEL = 1024
BATCH = 4
SEQ = 4096
DEPTH = 4

GRID_W = 64
CTX_LEN = 256
N_MOD = 6
GLA_HEADS = 4
GLA_DK = 64
GLA_DV = 128
GLA_GATE_RANK = 16
GLA_GATE_TEMP = 16.0
GLA_CHUNK = 64
MLA_HEADS = 8
MLA_Q_RANK = 384
MLA_KV_RANK = 256
MLA_NOPE = 64
MLA_ROPE = 32
MLA_DV = 64
MLA_SCALE = (MLA_NOPE + MLA_ROPE) ** -0.5
ATTN_BLOCK = 128
ROPE_BASE = 10000.0
RWKV_HEADS = 8
RWKV_HEAD = 64
RWKV_DECAY_RANK = 64
RWKV_AAA_RANK = 64
RWKV_GATE_RANK = 128
RWKV_GN_EPS = 64e-5
BRANCH_W = 512
N_BRANCH = 3
D_FF = 4 * D_MODEL
GLA_COLS = 2 * GLA_HEADS * GLA_DK + 2 * GLA_HEADS * GLA_DV + 2 * GLA_GATE_RANK
MLA_COLS = MLA_Q_RANK + MLA_KV_RANK + MLA_ROPE
RWKV_COLS = 3 * RWKV_HEADS * RWKV_HEAD + 2 * RWKV_DECAY_RANK + 2 * RWKV_AAA_RANK + RWKV_GATE_RANK
GATE_COLS = N_BRANCH * D_MODEL
N_IN = GLA_COLS + MLA_COLS + RWKV_COLS + GATE_COLS

kernel_name = "hybrid_gla_mla_rwkv7_dit_block"


def _split(z, sizes):
    idx = [int(i) for i in np.cumsum(sizes)[:-1]]
    return jnp.split(z, idx, axis=-1)


def _to_heads(a, n_heads):
    a = a.reshape(a.shape[:-1] + (n_heads, a.shape[-1] // n_heads))
    return jnp.swapaxes(a, -2, -3)


def _from_heads(a):
    a = jnp.swapaxes(a, -2, -3)
    return a.reshape(a.shape[:-2] + (-1,))


def _normalize(x, eps=1e-5):
    xf = x.astype(jnp.float32)
    mu = jnp.mean(xf, axis=-1, keepdims=True)
    var = jnp.mean(jnp.square(xf - mu), axis=-1, keepdims=True)
    return (xf - mu) * lax.rsqrt(var + eps)


def layer_norm(x, g, b):
    return (_normalize(x) * g + b).astype(x.dtype)


def modulate(x, shift, scale):
    return (_normalize(x, 1e-6) * (1.0 + scale) + shift).astype(x.dtype)


def rms_norm(x, g, eps=1e-6):
    xf = x.astype(jnp.float32)
    return (xf * lax.rsqrt(jnp.mean(xf * xf, axis=-1, keepdims=True) + eps) * g).astype(x.dtype)


def _stack_dirs(a, axis):
    return jnp.stack([a, jnp.flip(a, axis)])


def _flip_second(a2, axis):
    return jnp.stack([a2[0], jnp.flip(a2[1], axis)])


def _merge_dirs(y2, axis):
    return y2[0] + jnp.flip(y2[1], axis)


def _axial_rope(rows):
    n_axis = MLA_ROPE // 4
    inv = ROPE_BASE ** (-jnp.arange(n_axis, dtype=jnp.float32) / n_axis)
    row = jnp.repeat(jnp.arange(rows, dtype=jnp.float32), GRID_W)
    col = jnp.tile(jnp.arange(GRID_W, dtype=jnp.float32), rows)
    ang = jnp.concatenate([row[:, None] * inv, col[:, None] * inv], axis=-1)
    return jnp.cos(ang), jnp.sin(ang)


def _rope(x, cos, sin):
    x1, x2 = jnp.split(x, 2, axis=-1)
    return jnp.concatenate([x1 * cos - x2 * sin, x1 * sin + x2 * cos], axis=-1).astype(x.dtype)


def _gla_scan(q, k, v, logg, s0):
    T = q.shape[-2]
    n = T // GLA_CHUNK

    def chunk(a):
        return a.astype(jnp.float32).reshape(a.shape[:-2] + (n, GLA_CHUNK, a.shape[-1]))

    qc, kc, vc = chunk(q), chunk(k), chunk(v)
    b = jnp.cumsum(chunk(logg), axis=-2)
    b_last = b[..., -1:, :]
    q_in = qc * jnp.exp(b)
    k_in = kc * jnp.exp(-b)
    k_st = kc * jnp.exp(b_last - b)
    mask = jnp.tril(jnp.ones((GLA_CHUNK, GLA_CHUNK), dtype=bool))
    att = jnp.where(mask, jnp.einsum("...cd,...sd->...cs", q_in, k_in), 0.0)
    o_intra = jnp.einsum("...cs,...sv->...cv", att, vc)
    u = jnp.einsum("...cd,...cv->...dv", k_st, vc)
    decay = jnp.exp(b_last[..., 0, :])

    def step(s, xs):
        d, u_n = xs
        return d[..., None] * s + u_n, s

    s_fin, s_prev = lax.scan(step, s0, (jnp.moveaxis(decay, -2, 0), jnp.moveaxis(u, -3, 0)))
    o_inter = jnp.einsum("...cd,...dv->...cv", q_in, jnp.moveaxis(s_prev, 0, -3))
    o = (o_intra + o_inter).reshape(v.shape)
    return o.astype(v.dtype), s_fin


def _gla_prep(z, gk_up, gk_b):
    hk, hv, r = GLA_HEADS * GLA_DK, GLA_HEADS * GLA_DV, GLA_GATE_RANK
    q, k, v, gd_f, gd_b, og = _split(z, [hk, hk, hv, r, r, hv])
    gd = jnp.stack([gd_f, gd_b])
    logg = jax.nn.log_sigmoid((jnp.einsum("gbtr,grk->gbtk", gd, gk_up) + gk_b[:, None, None, :]).astype(jnp.float32)) / GLA_GATE_TEMP
    q = _to_heads(q, GLA_HEADS) * GLA_DK ** -0.5
    return (_stack_dirs(q, -2), _stack_dirs(_to_heads(k, GLA_HEADS), -2),
            _stack_dirs(_to_heads(v, GLA_HEADS), -2), _flip_second(_to_heads(logg, GLA_HEADS), -2), og)


def gla_branch(z, zc, gk_up, gk_b, norm_g, need_ctx):
    qc, kc, vc, gc, ogc = _gla_prep(zc, gk_up, gk_b)
    s0 = jnp.zeros((2, zc.shape[0], GLA_HEADS, GLA_DK, GLA_DV), jnp.float32)
    oc, s_ctx = _gla_scan(qc, kc, vc, gc, s0)
    q, k, v, g, og = _gla_prep(z, gk_up, gk_b)
    o, _ = _gla_scan(q, k, v, g, s_ctx)

    def post(o2, gate):
        o_h = rms_norm(_merge_dirs(o2, -2), norm_g)
        return (_from_heads(o_h) * jax.nn.silu(gate)).astype(z.dtype)

    return post(o, og), (post(oc, ogc) if need_ctx else None)


def _mla_proj(z, q_norm_g, kv_norm_g, w_uq, w_ukv):
    B, T, _ = z.shape
    qd, kvd, kr = _split(z, [MLA_Q_RANK, MLA_KV_RANK, MLA_ROPE])
    q = (rms_norm(qd, q_norm_g) @ w_uq).reshape(B, T, MLA_HEADS, MLA_NOPE + MLA_ROPE)
    kv = (rms_norm(kvd, kv_norm_g) @ w_ukv).reshape(B, T, MLA_HEADS, MLA_NOPE + MLA_DV)
    return q[..., :MLA_NOPE], q[..., MLA_NOPE:], kv[..., :MLA_NOPE], kr, kv[..., MLA_NOPE:]


def _mla_keys(k_nope, k_rope):
    B, T, H, _ = k_nope.shape
    return jnp.concatenate([k_nope, jnp.broadcast_to(k_rope[:, :, None, :], (B, T, H, MLA_ROPE))], axis=-1)


def _attend(q, k, v):
    s = jnp.einsum("bqhd,bkhd->bhqk", q, k).astype(jnp.float32) * MLA_SCALE
    p = jax.nn.softmax(s, axis=-1).astype(v.dtype)
    return jnp.einsum("bhqk,bkhv->bqhv", p, v)


def mla_branch(z, zc, q_norm_g, kv_norm_g, w_uq, w_ukv, cos, sin, need_ctx):
    B, T, _ = z.shape
    qn_c, qr_c, kn_c, kr_c, v_c = _mla_proj(zc, q_norm_g, kv_norm_g, w_uq, w_ukv)
    k_c = _mla_keys(kn_c, kr_c)
    qn, qr, kn, kr, v = _mla_proj(z, q_norm_g, kv_norm_g, w_uq, w_ukv)
    q = jnp.concatenate([qn, _rope(qr, cos[:, None, :], sin[:, None, :])], axis=-1)
    k = _mla_keys(kn, _rope(kr, cos, sin))
    k_all = jnp.concatenate([k_c, k], axis=1)
    v_all = jnp.concatenate([v_c, v], axis=1)
    nb = T // ATTN_BLOCK
    q_blocks = jnp.swapaxes(q.reshape(B, nb, ATTN_BLOCK, MLA_HEADS, MLA_NOPE + MLA_ROPE), 0, 1)
    o = lax.map(lambda qb: _attend(qb, k_all, v_all), q_blocks)
    y = jnp.swapaxes(o, 0, 1).reshape(B, T, MLA_HEADS * MLA_DV)
    yc = None
    if need_ctx:
        q_c = jnp.concatenate([qn_c, qr_c], axis=-1)
        yc = _attend(q_c, k_c, v_c).reshape(B, zc.shape[1], MLA_HEADS * MLA_DV)
    return y, yc


def _token_shift(z, mu):
    prev = jnp.pad(z[:, :-1], ((0, 0), (1, 0), (0, 0)))
    nxt = jnp.pad(z[:, 1:], ((0, 0), (0, 1), (0, 0)))
    return z + mu[0] * (prev - z) + mu[1] * (nxt - z)


def _rh(a):
    return a.reshape(a.shape[:-1] + (RWKV_HEADS, RWKV_HEAD))


def _rwkv_prep(z, shift_mu, w0, w_up, a0, a_up, g_up, k_k, k_a):
    W = RWKV_HEADS * RWKV_HEAD
    zs = _token_shift(z, shift_mu)
    r, k, v, wd_f, wd_b, ad_f, ad_b, gd = _split(
        zs, [W, W, W, RWKV_DECAY_RANK, RWKV_DECAY_RANK, RWKV_AAA_RANK, RWKV_AAA_RANK, RWKV_GATE_RANK])
    wd = jnp.stack([wd_f, wd_b])
    ad = jnp.stack([ad_f, ad_b])
    w_raw = (w0[:, None, None, :] + jnp.einsum("gbtr,grc->gbtc", jnp.tanh(wd), w_up)).astype(jnp.float32)
    decay = jnp.exp(-jnp.exp(-jax.nn.softplus(-w_raw) - 0.5))
    a = jax.nn.sigmoid(a0[:, None, None, :] + jnp.einsum("gbtr,grc->gbtc", ad, a_up))
    g = jax.nn.sigmoid(gd) @ g_up
    kk = _rh(k * k_k)
    kk = kk * lax.rsqrt(jnp.sum(kk * kk, axis=-1, keepdims=True) + 1e-12)
    k_eff = k * (1.0 + (a - 1.0) * k_a)
    return _rh(r), _rh(decay), _rh(k_eff), _rh(v), kk, _rh(a), g


def _rwkv_scan(r, w, k, v, kk, b, s0):
    xs = tuple(jnp.moveaxis(a.astype(jnp.float32), 2, 0) for a in (r, w, k, v, kk, b))

    def step(s, xt):
        r_t, w_t, k_t, v_t, kk_t, b_t = xt
        sa = -jnp.einsum("...vk,...k->...v", s, kk_t)
        s = s * w_t[..., None, :] + sa[..., :, None] * b_t[..., None, :] + v_t[..., :, None] * k_t[..., None, :]
        return s, jnp.einsum("...vk,...k->...v", s, r_t)

    s_fin, y = lax.scan(step, s0, xs)
    return jnp.moveaxis(y, 0, 2), s_fin


def rwkv_branch(z, zc, shift_mu, w0, w_up, a0, a_up, g_up, k_k, k_a, r_k, ln_g, ln_b, need_ctx):
    r_k_h = r_k.reshape(RWKV_HEADS, RWKV_HEAD)

    def run(zz, s0):
        r, w, k_eff, v, kk, a, g = _rwkv_prep(zz, shift_mu, w0, w_up, a0, a_up, g_up, k_k, k_a)
        y2, s_fin = _rwkv_scan(_stack_dirs(r, 1), _flip_second(w, 1), _flip_second(k_eff, 1),
                               _stack_dirs(v, 1), _stack_dirs(kk, 1), _flip_second(kk[None] * a, 1), s0)
        return y2, s_fin, (r, k_eff, v, g)

    def post(y2, r, k_eff, v, g):
        B, T = y2.shape[1], y2.shape[2]
        y = _merge_dirs(y2, 1)
        mu = jnp.mean(y, axis=-1, keepdims=True)
        var = jnp.mean(jnp.square(y - mu), axis=-1, keepdims=True)
        yn = ((y - mu) * lax.rsqrt(var + RWKV_GN_EPS)).reshape(B, T, -1) * ln_g + ln_b
        bonus = jnp.sum(jnp.sum(r[None] * k_eff * r_k_h, axis=-1, keepdims=True) * v[None], axis=0)
        return ((yn + bonus.reshape(B, T, -1)) * g).astype(z.dtype)

    s0 = jnp.zeros((2, zc.shape[0], RWKV_HEADS, RWKV_HEAD, RWKV_HEAD), jnp.float32)
    y2c, s_ctx, aux_c = run(zc, s0)
    y2, _, aux = run(z, s_ctx)
    return post(y2, *aux), (post(y2c, *aux_c) if need_ctx else None)


def _merge(ya, yb, yr, zg, w_branch, w_out):
    u = jnp.einsum("btnw,nwd->btnd", jnp.stack([ya, yb, yr], axis=2), w_branch)
    gates = jax.nn.sigmoid(zg.reshape(zg.shape[:-1] + (N_BRANCH, D_MODEL)))
    return jnp.sum(gates * u, axis=2) @ w_out


def _mlp(h, w1, w2):
    return jnp.square(jax.nn.relu(h @ w1)) @ w2


def setup_inputs(seed: int = 0) -> dict:
    key = jax.random.key(seed)
    ks = iter(jax.random.split(key, 40))

    def nrm(shape, scale=1.0):
        return scale * jax.random.normal(next(ks), shape, jnp.float32)

    def unif(shape, lo, hi):
        return jax.random.uniform(next(ks), shape, jnp.float32, lo, hi)

    D, L, W = D_MODEL, DEPTH, BRANCH_W
    beta = (8.0 * DEPTH) ** -0.25
    hk = GLA_HEADS * GLA_DK
    return {
        "x": nrm((BATCH, SEQ, D)),
        "c": nrm((BATCH, D)),
        "ctx": nrm((BATCH, CTX_LEN, D)),
        "c_ctx": nrm((D,)),
        "ada_w": nrm((L, D, N_MOD * D), 0.5 * D ** -0.5),
        "ada_b": nrm((L, N_MOD * D), 0.02),
        "w_in": nrm((L, D, N_IN), D ** -0.5),
        "gla_gk_up": nrm((L, 2, GLA_GATE_RANK, hk), GLA_GATE_RANK ** -0.5),
        "gla_gk_b": nrm((L, 2, hk), 0.5),
        "gla_norm_g": 1.0 + nrm((L, GLA_DV), 0.05),
        "mla_q_norm_g": 1.0 + nrm((L, MLA_Q_RANK), 0.05),
        "mla_kv_norm_g": 1.0 + nrm((L, MLA_KV_RANK), 0.05),
        "mla_w_uq": nrm((L, MLA_Q_RANK, MLA_HEADS * (MLA_NOPE + MLA_ROPE)), MLA_Q_RANK ** -0.5),
        "mla_w_ukv": nrm((L, MLA_KV_RANK, MLA_HEADS * (MLA_NOPE + MLA_DV)), MLA_KV_RANK ** -0.5),
        "rwkv_shift_mu": unif((L, 2, RWKV_COLS), 0.0, 0.5),
        "rwkv_w0": unif((L, 2, W), -4.0, 0.0),
        "rwkv_w_up": nrm((L, 2, RWKV_DECAY_RANK, W), 0.5 * RWKV_DECAY_RANK ** -0.5),
        "rwkv_a0": nrm((L, 2, W), 0.1),
        "rwkv_a_up": nrm((L, 2, RWKV_AAA_RANK, W), 0.5 * RWKV_AAA_RANK ** -0.5),
        "rwkv_g_up": nrm((L, RWKV_GATE_RANK, W), RWKV_GATE_RANK ** -0.5),
        "rwkv_k_k": 0.85 + nrm((L, W), 0.05),
        "rwkv_k_a": 1.0 + nrm((L, W), 0.05),
        "rwkv_r_k": nrm((L, W), 0.1),
        "rwkv_ln_g": 1.0 + nrm((L, W), 0.05),
        "rwkv_ln_b": nrm((L, W), 0.02),
        "w_branch": nrm((L, N_BRANCH, W, D), W ** -0.5),
        "w_out": nrm((L, D, D), beta * D ** -0.5),
        "ln1_g": 1.0 + nrm((L, D), 0.05),
        "ln1_b": nrm((L, D), 0.02),
        "mlp_w1": nrm((L, D, D_FF), D ** -0.5),
        "mlp_w2": nrm((L, D_FF, D), beta * D_FF ** -0.5),
        "ln2_g": 1.0 + nrm((L, D), 0.05),
        "ln2_b": nrm((L, D), 0.02),
    }


def reference(x, c, ctx, c_ctx, ada_w, ada_b, w_in, gla_gk_up, gla_gk_b, gla_norm_g,
              mla_q_norm_g, mla_kv_norm_g, mla_w_uq, mla_w_ukv, rwkv_shift_mu, rwkv_w0, rwkv_w_up,
              rwkv_a0, rwkv_a_up, rwkv_g_up, rwkv_k_k, rwkv_k_a, rwkv_r_k, rwkv_ln_g, rwkv_ln_b,
              w_branch, w_out, ln1_g, ln1_b, mlp_w1, mlp_w2, ln2_g, ln2_b):
    B, T, D = x.shape
    rows = T // GRID_W
    cos, sin = _axial_rope(rows)
    alpha = (2.0 * DEPTH) ** 0.25
    col_sizes = [GLA_COLS, MLA_COLS, RWKV_COLS, GATE_COLS]
    xc = ctx
    for l in range(DEPTH):
        need_ctx = l < DEPTH - 1
        mod = (jax.nn.silu(c) @ ada_w[l] + ada_b[l]).reshape(B, N_MOD, 1, D)
        modc = (jax.nn.silu(c_ctx) @ ada_w[l] + ada_b[l]).reshape(N_MOD, 1, D)
        z = modulate(x, mod[:, 0], mod[:, 1]) @ w_in[l]
        zc = modulate(xc, modc[0], modc[1]) @ w_in[l]
        za, zb, zr, zg = _split(z, col_sizes)
        zac, zbc, zrc, zgc = _split(zc, col_sizes)
        ya, yac = gla_branch(za, zac, gla_gk_up[l], gla_gk_b[l], gla_norm_g[l], need_ctx)
        yb, ybc = mla_branch(zb, zbc, mla_q_norm_g[l], mla_kv_norm_g[l], mla_w_uq[l], mla_w_ukv[l],
                             cos, sin, need_ctx)
        yr, yrc = rwkv_branch(zr, zrc, rwkv_shift_mu[l], rwkv_w0[l], rwkv_w_up[l], rwkv_a0[l],
                              rwkv_a_up[l], rwkv_g_up[l], rwkv_k_k[l], rwkv_k_a[l], rwkv_r_k[l],
                              rwkv_ln_g[l], rwkv_ln_b[l], need_ctx)
        x = layer_norm(alpha * x + mod[:, 2] * _merge(ya, yb, yr, zg, w_branch[l], w_out[l]),
                       ln1_g[l], ln1_b[l])
        x = layer_norm(alpha * x + mod[:, 5] * _mlp(modulate(x, mod[:, 3], mod[:, 4]), mlp_w1[l], mlp_w2[l]),
                       ln2_g[l], ln2_b[l])
        if need_ctx:
            xc = layer_norm(alpha * xc + modc[2] * _merge(yac, ybc, yrc, zgc, w_branch[l], w_out[l]),
                            ln1_g[l], ln1_b[l])
            xc = layer_norm(alpha * xc + modc[5] * _mlp(modulate(xc, modc[3], modc[4]), mlp_w1[l], mlp_w2[l]),
                            ln2_g[l], ln2_b[l])
    return x
```

```python
import numpy as np
import concourse.bass as bass
import concourse.mybir as mybir

F32 = mybir.dt.float32
BF16 = mybir.dt.bfloat16
AF = mybir.ActivationFunctionType
ALU = mybir.AluOpType
AX = mybir.AxisListType

ENGS = ("pe", "act", "dve", "pool", "sp")


class Buf:
    def __init__(self, prog, t, name, space):
        self.prog = prog
        self.t = t
        self.name = name
        self.space = space
        self.last_write = None
        self.reads = {}
        self.war = {}
        self.dsem = None
        self.dcount = 0

    def __getitem__(self, idx):
        return View(self, self.t.__getitem__(idx))

    @property
    def ap(self):
        return View(self, self.t[:] if self.space != "dram" else self.t.ap())


class View:
    def __init__(self, buf, ap):
        self.buf = buf
        self.ap = ap

    def __getitem__(self, idx):
        return View(self.buf, self.ap.__getitem__(idx))

    def rearrange(self, *a, **k):
        return View(self.buf, self.ap.rearrange(*a, **k))

    def bitcast(self, dt):
        return View(self.buf, self.ap.bitcast(dt))


class Prog:
    def __init__(self, same_engine_sync=True):
        self.nc = bass.Bass("TRN2", target_bir_lowering=False)
        self.ops = {e: [] for e in ENGS}
        self.count = {e: 0 for e in ENGS}
        self.sems = {}
        self.waited = {e: {} for e in ENGS}
        self.same_engine_sync = same_engine_sync
        self.nbuf = 0
        for e in ("pe", "act", "dve", "pool"):
            self.sems[e] = self.nc.alloc_semaphore(name="s_" + e)
        self.final_tokens = []

    def sbuf(self, name, shape, dt=F32):
        t = self.nc.alloc_sbuf_tensor(name, list(shape), dt)
        return Buf(self, t, name, "sbuf")

    def psum(self, name, shape, dt=F32):
        t = self.nc.alloc_psum_tensor(name, list(shape), dt)
        return Buf(self, t, name, "psum")

    def dram(self, name, shape, dt=F32, kind="Internal"):
        t = self.nc.dram_tensor(name, list(shape), dt, kind=kind)
        return Buf(self, t, name, "dram")

    def sub(self, parent, name):
        b = Buf(self, parent.t, name, parent.space)
        return b

    def _need(self, eng, tok, waits):
        if tok is None:
            return
        key, val, teng = tok
        if teng == eng:
            if eng in ("pe", "sp"):
                return
            if not self.same_engine_sync:
                return
        if self.waited[eng].get(key, 0) >= val:
            return
        cur = waits.get(key, 0)
        if val > cur:
            waits[key] = val

    def _deps(self, eng, reads, writes, pe_accum=False):
        waits = {}
        for v in reads:
            self._need(eng, v.buf.last_write, waits)
        for v in writes:
            b = v.buf
            if not (pe_accum and b.last_write is not None and b.last_write[2] == "pe"):
                self._need(eng, b.last_write, waits)
            for src in (b.reads, b.war):
                for reng, tok in src.items():
                    if reng == eng and eng != "sp":
                        continue
                    self._need(eng, tok, waits)
        for k, v in waits.items():
            self.waited[eng][k] = v
        return list(waits.items())

    def _commit(self, eng, tok, reads, writes):
        for v in reads:
            v.buf.reads[eng if eng != "sp" else ("sp", tok[0])] = tok
        for v in writes:
            v.buf.last_write = tok
            if v.buf.reads:
                v.buf.war = v.buf.reads
                v.buf.reads = {}

    def op(self, eng, fn, reads, writes, pe_accum=False):
        waits = self._deps(eng, reads, writes, pe_accum)
        self.count[eng] += 1
        tok = (eng, self.count[eng], eng)
        self.ops[eng].append((waits, fn, (eng, 1)))
        self._commit(eng, tok, reads, writes)
        return tok

    def dma(self, out, in_, queue="sp", nowaw=False, **kw):
        b = out.buf
        if b.dsem is None:
            b.dsem = "d_" + b.name + str(self.nbuf)
            self.nbuf += 1
            self.sems[b.dsem] = self.nc.alloc_semaphore(name=b.dsem[:30])
        waits = {}
        self._need(queue, in_.buf.last_write, waits)
        if not (nowaw and b.last_write is not None and b.last_write[0] == b.dsem):
            self._need(queue, b.last_write, waits)
        for src in (b.reads, b.war):
            for reng, tok in src.items():
                self._need(queue, tok, waits)
        for k, v in waits.items():
            self.waited[queue][k] = v
        b.dcount += 16
        tok = (b.dsem, b.dcount, "dma")
        oap, iap = out.ap, in_.ap
        fn = lambda e: e.dma_start(out=oap, in_=iap, **kw)
        if queue != "sp":
            self.count[queue] += 0
        self.ops[queue].append((list(waits.items()), fn, (b.dsem, 16)))
        in_.buf.reads[("dma", b.dsem)] = tok
        b.last_write = tok
        if b.reads:
            b.war = b.reads
            b.reads = {}
        return tok

    def matmul(self, out, lhsT, rhs, start=True, stop=True, **kw):
        o, l, r = out.ap, lhsT.ap, rhs.ap
        return self.op("pe", lambda e: e.matmul(o, l, r, start=start, stop=stop, **kw),
                       [lhsT, rhs], [out], pe_accum=not start)

    def transpose(self, out, in_, ident):
        o, i, d = out.ap, in_.ap, ident.ap
        return self.op("pe", lambda e: e.transpose(o, i, d), [in_, ident], [out])

    def act(self, out, in_, func, bias=None, scale=None, accum_out=None, eng="act"):
        reads = [in_]
        kw = {}
        if bias is not None:
            if isinstance(bias, View):
                reads.append(bias); kw["bias"] = bias.ap
            else:
                kw["bias"] = bias
        if scale is not None:
            if isinstance(scale, View):
                reads.append(scale); kw["scale"] = scale.ap
            else:
                kw["scale"] = scale
        writes = [out]
        if accum_out is not None:
            writes.append(accum_out); kw["accum_out"] = accum_out.ap
        o, i = out.ap, in_.ap
        return self.op(eng, lambda e: e.activation(o, i, func, **kw), reads, writes)

    def tt(self, out, in0, in1, op, eng="dve"):
        o, a, b = out.ap, in0.ap, in1.ap
        return self.op(eng, lambda e: e.tensor_tensor(o, a, b, op), [in0, in1], [out])

    def ts(self, out, in0, s1, s2, op0, op1=None, eng="dve", accum_out=None):
        reads = [in0]
        a1 = s1.ap if isinstance(s1, View) else s1
        a2 = s2.ap if isinstance(s2, View) else s2
        if isinstance(s1, View): reads.append(s1)
        if isinstance(s2, View): reads.append(s2)
        o, a = out.ap, in0.ap
        writes = [out]
        kw = {}
        if accum_out is not None:
            writes.append(accum_out); kw["accum_out"] = accum_out.ap
        if op1 is None:
            return self.op(eng, lambda e: e.tensor_scalar(o, a, a1, None, op0, **kw), reads, writes)
        return self.op(eng, lambda e: e.tensor_scalar(o, a, a1, a2, op0, op1, **kw), reads, writes)

    def stt(self, out, in0, scalar, in1, op0, op1, eng="dve"):
        reads = [in0, in1]
        s = scalar.ap if isinstance(scalar, View) else scalar
        if isinstance(scalar, View): reads.append(scalar)
        o, a, b = out.ap, in0.ap, in1.ap
        return self.op(eng, lambda e: e.scalar_tensor_tensor(o, a, s, b, op0, op1), reads, [out])

    def copy(self, out, in_, eng="dve"):
        o, i = out.ap, in_.ap
        if eng == "act":
            return self.op(eng, lambda e: e.copy(o, i), [in_], [out])
        return self.op(eng, lambda e: e.tensor_copy(o, i), [in_], [out])

    def memset(self, out, val, eng="dve"):
        o = out.ap
        return self.op(eng, lambda e: e.memset(o, val), [], [out])

    def reduce(self, out, in_, op, axis=AX.X, eng="dve"):
        o, i = out.ap, in_.ap
        return self.op(eng, lambda e: e.tensor_reduce(o, i, axis, op), [in_], [out])

    def recip(self, out, in_):
        o, i = out.ap, in_.ap
        return self.op("dve", lambda e: e.reciprocal(o, i), [in_], [out])

    def finish(self, out_bufs):
        for b in out_bufs:
            self.final_tokens.append(b.last_write)
        nc = self.nc
        with nc.Block() as block:
            def emit(eng_name):
                def body(e):
                    for waits, fn, inc in self.ops[eng_name]:
                        for k, v in waits:
                            e.wait_ge(self.sems[k], v)
                        ins = fn(e)
                        ins.then_inc(self.sems[inc[0]], inc[1])
                    if eng_name == "sp":
                        for tok in self.final_tokens:
                            e.wait_ge(self.sems[tok[0]], tok[1])
                return body
            block.tensor(emit("pe"))
            block.scalar(emit("act"))
            block.vector(emit("dve"))
            block.gpsimd(emit("pool"))
            block.sync(emit("sp"))
        return nc


from concourse.bass_utils import run_bass_kernel_spmd
import ml_dtypes

NPBF = ml_dtypes.bfloat16
D = 1024
NT = 2176
BLKS = [(0, 128, 1)] + [(128 + 512 * i, 512, 0) for i in range(4)]
ALPHA = 8.0 ** 0.25


def bc(view, shape, axis):
    return View(view.buf, view.ap.unsqueeze(axis).broadcast_to(list(shape)))


class PsPool:
    def __init__(self, P, n=8):
        self.t = [P.psum("psb%d" % i, [128, 512], F32) for i in range(n)]
        self.i = 0

    def get(self):
        b = self.t[self.i % len(self.t)]
        self.i += 1
        return b


def fm_stats(P, pp, xb, n, eps, ones32, sqb, mean, rstd, tmp):
    P.act(sqb[:, :, 0:n], xb, AF.Square)
    ps_s = pp.get(); ps_q = pp.get()
    for c in range(8):
        P.matmul(ps_s[:, 0:n], ones32.ap, xb[:, c, :], start=(c == 0), stop=(c == 7))
    for c in range(8):
        P.matmul(ps_q[:, 0:n], ones32.ap, sqb[:, c, 0:n], start=(c == 0), stop=(c == 7))
    P.ts(mean[:, 0:n], ps_s[:, 0:n], 1.0 / D, None, ALU.mult)
    P.tt(tmp[:, 0:n], mean[:, 0:n], mean[:, 0:n], ALU.mult)
    P.stt(tmp[:, 0:n], ps_q[:, 0:n], 1.0 / D, tmp[:, 0:n], ALU.mult, ALU.subtract)
    P.ts(tmp[:, 0:n], tmp[:, 0:n], eps, None, ALU.add)
    P.act(tmp[:, 0:n], tmp[:, 0:n], AF.Sqrt)
    P.recip(rstd[:, 0:n], tmp[:, 0:n])


def fm_norm_apply(P, xb, n, mean, rstd, s1, s2, out):
    P.tt(xb, xb, bc(mean[:, 0:n], [128, 8, n], 1), ALU.subtract)
    P.tt(xb, xb, bc(rstd[:, 0:n], [128, 8, n], 1), ALU.mult)
    for c in range(8):
        P.ts(out[:, c, :], xb[:, c, :], s1[:, c:c + 1], s2[:, c:c + 1], ALU.mult, ALU.add,
             eng="pool" if c % 2 else "dve")


def build_A():
    P = Prog()
    X = lambda n, s, dt=F32: P.dram(n, s, dt, kind="ExternalInput")
    xT = X("xT", [D, NT]); c2d = X("c2", [128, 8, 2]); ada_w = X("ada_w", [D, 6144])
    ada_bT = X("ada_bT", [128, 48]); w_in = X("w_in", [D, 7232])
    zT = P.dram("zT", [7232, NT], BF16, kind="ExternalOutput")
    modo = P.dram("mod", [128, 48, 2], F32, kind="ExternalOutput")
    pp = PsPool(P)
    ones32 = P.sbuf("ones32", [128, 128]); P.memset(ones32.ap, 1.0)
    c2 = P.sbuf("c2s", [128, 8, 2]); P.dma(c2.ap, c2d.ap); P.act(c2.ap, c2.ap, AF.Silu)
    bias = P.sbuf("adab", [128, 48]); P.dma(bias.ap, ada_bT.ap)
    mod = P.sbuf("modsb", [128, 48, 2])
    wA = [P.sbuf("wA%d" % i, [128, 8, 512]) for i in range(2)]
    awv = ada_w.ap.rearrange("(kc p) n -> p kc n", p=128)
    for g in range(12):
        w = wA[g % 2]
        for kc in range(8):
            P.dma(w[:, kc, :], awv[:, kc, g * 512:(g + 1) * 512], queue="sp" if kc % 2 else "pool", nowaw=True)
        ps = pp.get()
        for oc in range(4):
            for kc in range(8):
                P.matmul(ps[:, oc * 16:oc * 16 + 2], w[:, kc, oc * 128:(oc + 1) * 128], c2[:, kc, :],
                         start=(kc == 0), stop=(kc == 7))
        P.tt(mod[:, g * 4:(g + 1) * 4, :], ps[:, 0:64].rearrange("p (a b) -> p a b", b=16)[:, :, 0:2],
             bc(bias[:, g * 4:(g + 1) * 4], [128, 4, 2], 2), ALU.add)
    P.dma(modo.ap, mod.ap, queue="pool")
    opsc = P.sbuf("opsc", [128, 8, 2]); P.ts(opsc.ap, mod[:, 8:16, :], 1.0, None, ALU.add)
    wI = P.sbuf("wI", [128, 8, 7232], BF16)
    wiv = w_in.ap.rearrange("(kc p) n -> p kc n", p=128)
    for kc in range(8):
        P.dma(wI[:, kc, :], wiv[:, kc, :], queue="pool", nowaw=True)
    xv = xT.ap.rearrange("(c p) t -> p c t", p=128)
    xbs = [P.sbuf("xb%d" % i, [128, 8, 512]) for i in range(1)]
    sqb = P.sbuf("sqb", [128, 8, 512]); mean = P.sbuf("mean", [128, 512]); rstd = P.sbuf("rstd", [128, 512])
    tmp = P.sbuf("tmp", [128, 512])
    xms = [P.sbuf("xm%d" % i, [128, 8, 512], BF16) for i in range(1)]
    zst = [P.sbuf("zst%d" % i, [128, 512], BF16) for i in range(4)]
    k = 0
    for bi, (t0, n, col) in enumerate(BLKS):
        xb = xbs[0]; xm = xms[0]
        for c in range(8):
            P.dma(xb[:, c, 0:n], xv[:, c, t0:t0 + n], queue="sp", nowaw=True)
        fm_stats(P, pp, xb[:, :, 0:n], n, 1e-6, ones32, sqb, mean, rstd, tmp)
        fm_norm_apply(P, xb[:, :, 0:n], n, mean, rstd, opsc[:, :, col], mod[:, 0:8, col], xm[:, :, 0:n])
        for j in range(57):
            m = 128 if j < 56 else 64
            ps = pp.get()
            for kc in range(8):
                P.matmul(ps[0:m, 0:n], wI[:, kc, j * 128:j * 128 + m], xm[:, kc, 0:n], start=(kc == 0), stop=(kc == 7))
            st = zst[k % 4]; k += 1
            if j == 32:
                P.copy(st[0:64, 0:n], ps[0:64, 0:n], eng="dve")
                P.act(st[64:128, 0:n], ps[64:128, 0:n], AF.Sigmoid)
            elif j * 128 >= 4160:
                P.act(st[0:m, 0:n], ps[0:m, 0:n], AF.Sigmoid)
            elif k % 2:
                P.copy(st[0:m, 0:n], ps[0:m, 0:n], eng="act")
            else:
                P.copy(st[0:m, 0:n], ps[0:m, 0:n], eng="dve")
            P.dma(zT[j * 128:j * 128 + m, t0:t0 + n], st[0:m, 0:n], queue="sp", nowaw=True)
    return P.finish([zT, modo])


def build_C1():
    P = Prog()
    X = lambda n, s, dt=F32: P.dram(n, s, dt, kind="ExternalInput")
    xT = X("xT", [D, NT]); yT = X("yT", [1536, NT], BF16); gT = X("gT", [3072, NT], BF16)
    modd = X("mod", [128, 48, 2]); w_br = X("w_branch", [1536, D]); w_out = X("w_out", [D, D])
    lng = X("ln_g", [128, 8]); lnb = X("ln_b", [128, 8])
    oT = P.dram("x1T", [D, NT], F32, kind="ExternalOutput")
    pp = PsPool(P)
    ones32 = P.sbuf("ones32", [128, 128]); P.memset(ones32.ap, 1.0)
    mod = P.sbuf("modsb", [128, 48, 2]); P.dma(mod.ap, modd.ap)
    g1 = P.sbuf("lng", [128, 8]); b1 = P.sbuf("lnb", [128, 8]); P.dma(g1.ap, lng.ap); P.dma(b1.ap, lnb.ap)
    wb = P.sbuf("wb", [128, 12, D], BF16); wo = P.sbuf("wo", [128, 8, D], BF16)
    wbv = w_br.ap.rearrange("(kc p) n -> p kc n", p=128); wov = w_out.ap.rearrange("(kc p) n -> p kc n", p=128)
    for kc in range(12):
        P.dma(wb[:, kc, :], wbv[:, kc, :], queue="pool", nowaw=True)
    for kc in range(8):
        P.dma(wo[:, kc, :], wov[:, kc, :], queue="pool", nowaw=True)
    xv = xT.ap.rearrange("(c p) t -> p c t", p=128); yv = yT.ap.rearrange("(c p) t -> p c t", p=128)
    gv = gT.ap.rearrange("(c p) t -> p c t", p=128); ov = oT.ap.rearrange("(c p) t -> p c t", p=128)
    xb = P.sbuf("xb", [128, 8, 512]); yb = P.sbuf("yb", [128, 12, 512], BF16); gb = P.sbuf("gb", [128, 24, 512], BF16)
    mb = P.sbuf("mb", [128, 8, 512], BF16); t1 = P.sbuf("t1", [128, 512]); t2 = P.sbuf("t2", [128, 512])
    sqb = P.sbuf("sqb", [128, 8, 512]); mean = P.sbuf("mean", [128, 512]); rstd = P.sbuf("rstd", [128, 512])
    tmp = P.sbuf("tmp", [128, 512]); ob = P.sbuf("ob", [128, 8, 512])
    for bi, (t0, n, col) in enumerate(BLKS):
        for c in range(8):
            P.dma(xb[:, c, 0:n], xv[:, c, t0:t0 + n], queue="sp", nowaw=True)
        for c in range(12):
            P.dma(yb[:, c, 0:n], yv[:, c, t0:t0 + n], queue="sp", nowaw=True)
        for c in range(24):
            P.dma(gb[:, c, 0:n], gv[:, c, t0:t0 + n], queue="pool", nowaw=True)
        for oc in range(8):
            for nb in range(3):
                ps = pp.get()
                for kc in range(4):
                    P.matmul(ps[:, 0:n], wb[:, nb * 4 + kc, oc * 128:(oc + 1) * 128], yb[:, nb * 4 + kc, 0:n],
                             start=(kc == 0), stop=(kc == 3))
                dst = t1 if nb == 0 else t2
                P.tt(dst[:, 0:n], ps[:, 0:n], gb[:, nb * 8 + oc, 0:n], ALU.mult)
                if nb == 1:
                    P.tt(t1[:, 0:n], t1[:, 0:n], t2[:, 0:n], ALU.add, eng="pool")
                if nb == 2:
                    P.tt(mb[:, oc, 0:n], t1[:, 0:n], t2[:, 0:n], ALU.add, eng="pool")
        for oc in range(8):
            ps = pp.get()
            for kc in range(8):
                P.matmul(ps[:, 0:n], wo[:, kc, oc * 128:(oc + 1) * 128], mb[:, kc, 0:n], start=(kc == 0), stop=(kc == 7))
            P.ts(t1[:, 0:n], ps[:, 0:n], mod[:, 16 + oc, col:col + 1], None, ALU.mult)
            P.stt(xb[:, oc, 0:n], xb[:, oc, 0:n], ALPHA, t1[:, 0:n], ALU.mult, ALU.add)
        fm_stats(P, pp, xb[:, :, 0:n], n, 1e-5, ones32, sqb, mean, rstd, tmp)
        fm_norm_apply(P, xb[:, :, 0:n], n, mean, rstd, g1.ap, b1.ap, ob[:, :, 0:n])
        for c in range(8):
            P.dma(ov[:, c, t0:t0 + n], ob[:, c, 0:n], queue="sp", nowaw=True)
    return P.finish([oT])


def build_C2():
    P = Prog()
    X = lambda n, s, dt=F32: P.dram(n, s, dt, kind="ExternalInput")
    xT = X("xT", [D, NT]); modd = X("mod", [128, 48, 2]); w1d = X("w1", [D, 4096]); w2d = X("w2", [4096, D])
    lng = X("ln_g", [128, 8]); lnb = X("ln_b", [128, 8])
    oT = P.dram("x2T", [D, NT], F32, kind="ExternalOutput")
    pp = PsPool(P)
    ones32 = P.sbuf("ones32", [128, 128]); P.memset(ones32.ap, 1.0)
    mod = P.sbuf("modsb", [128, 48, 2]); P.dma(mod.ap, modd.ap)
    opsc = P.sbuf("opsc", [128, 8, 2]); P.ts(opsc.ap, mod[:, 32:40, :], 1.0, None, ALU.add)
    g1 = P.sbuf("lng", [128, 8]); b1 = P.sbuf("lnb", [128, 8]); P.dma(g1.ap, lng.ap); P.dma(b1.ap, lnb.ap)
    w1 = P.sbuf("w1s", [128, 8, 4096], BF16); w2 = P.sbuf("w2s", [128, 32, D], BF16)
    w1v = w1d.ap.rearrange("(kc p) n -> p kc n", p=128); w2v = w2d.ap.rearrange("(kc p) n -> p kc n", p=128)
    for kc in range(8):
        P.dma(w1[:, kc, :], w1v[:, kc, :], queue="pool", nowaw=True)
    for kc in range(32):
        P.dma(w2[:, kc, :], w2v[:, kc, :], queue="pool", nowaw=True)
    xv = xT.ap.rearrange("(c p) t -> p c t", p=128); ov = oT.ap.rearrange("(c p) t -> p c t", p=128)
    NB = 256
    xb = P.sbuf("xb", [128, 8, NB]); xr = P.sbuf("xr", [128, 8, NB]); xm = P.sbuf("xm", [128, 8, NB], BF16)
    hb = P.sbuf("hb", [128, 32, NB], BF16); hr = P.sbuf("hr", [128, NB]); t1 = P.sbuf("t1", [128, NB])
    sqb = P.sbuf("sqb", [128, 8, NB]); mean = P.sbuf("mean", [128, NB]); rstd = P.sbuf("rstd", [128, NB])
    tmp = P.sbuf("tmp", [128, NB]); ob = P.sbuf("ob", [128, 8, NB])
    blks = []
    for (t0, n, col) in BLKS:
        for s in range(0, n, NB):
            blks.append((t0 + s, min(NB, n - s), col))
    for bi, (t0, n, col) in enumerate(blks):
        for c in range(8):
            P.dma(xb[:, c, 0:n], xv[:, c, t0:t0 + n], queue="sp", nowaw=True)
        P.copy(xr[:, :, 0:n], xb[:, :, 0:n], eng="pool")
        fm_stats(P, pp, xb[:, :, 0:n], n, 1e-6, ones32, sqb, mean, rstd, tmp)
        fm_norm_apply(P, xb[:, :, 0:n], n, mean, rstd, opsc[:, :, col], mod[:, 24:32, col], xm[:, :, 0:n])
        for fc in range(32):
            ps = pp.get()
            for kc in range(8):
                P.matmul(ps[:, 0:n], w1[:, kc, fc * 128:(fc + 1) * 128], xm[:, kc, 0:n], start=(kc == 0), stop=(kc == 7))
            P.act(hr[:, 0:n], ps[:, 0:n], AF.Relu)
            P.tt(hb[:, fc, 0:n], hr[:, 0:n], hr[:, 0:n], ALU.mult)
        for oc in range(8):
            ps = pp.get()
            for fc in range(32):
                P.matmul(ps[:, 0:n], w2[:, fc, oc * 128:(oc + 1) * 128], hb[:, fc, 0:n], start=(fc == 0), stop=(fc == 31))
            P.ts(t1[:, 0:n], ps[:, 0:n], mod[:, 40 + oc, col:col + 1], None, ALU.mult)
            P.stt(xr[:, oc, 0:n], xr[:, oc, 0:n], ALPHA, t1[:, 0:n], ALU.mult, ALU.add)
        fm_stats(P, pp, xr[:, :, 0:n], n, 1e-5, ones32, sqb, mean, rstd, tmp)
        fm_norm_apply(P, xr[:, :, 0:n], n, mean, rstd, g1.ap, b1.ap, ob[:, :, 0:n])
        for c in range(8):
            P.dma(ov[:, c, t0:t0 + n], ob[:, c, 0:n], queue="sp", nowaw=True)
    return P.finish([oT])


TT = 4352
QBLK = [(0, 256, 2)] + [(256 + 512 * i, 512, 34) for i in range(8)]


def build_B1():
    P = Prog()
    X = lambda n, s, dt=F32: P.dram(n, s, dt, kind="ExternalInput")
    qdT = X("qdT", [384, TT], BF16); kvdT = X("kvdT", [256, TT], BF16)
    krT = X("krT", [32, TT], BF16); krrT = X("krrT", [32, TT], BF16)
    wq = X("wq", [384, 384]); wqr = X("wqr", [384, 384]); wkn = X("wkn", [256, 256]); wkv = X("wkv", [256, 256])
    qg = X("qg", [128, 3]); kg = X("kg", [128, 2]); Cd = X("ropeC", [96, TT]); Sd = X("ropeS", [96, TT])
    yo = P.dram("ybT", [256, TT], BF16, kind="ExternalOutput")
    pss = [P.psum("pS%d" % i, [128, 512]) for i in range(6)]
    pso = [P.psum("pO%d" % i, [128, 512]) for i in range(2)]
    psi = [0]
    def gps():
        psi[0] += 1
        return pss[psi[0] % 6]
    ones16 = P.sbuf("ones16", [128, 128], BF16); P.memset(ones16.ap, 1.0)
    qgs = P.sbuf("qgs", [128, 3]); kgs = P.sbuf("kgs", [128, 2]); P.dma(qgs.ap, qg.ap); P.dma(kgs.ap, kg.ap)
    Ct = P.sbuf("Ct", [96, TT]); St = P.sbuf("St", [96, TT]); P.dma(Ct.ap, Cd.ap); P.dma(St.ap, Sd.ap, queue="pool")
    qd = P.sbuf("qd", [128, 3, TT], BF16); kvd = P.sbuf("kvd", [128, 2, TT], BF16)
    for c in range(3):
        P.dma(qd[:, c, :], qdT[c * 128:(c + 1) * 128, :], nowaw=True)
    for c in range(2):
        P.dma(kvd[:, c, :], kvdT[c * 128:(c + 1) * 128, :], queue="pool", nowaw=True)
    kr = P.sbuf("kr", [96, TT], BF16); krr = P.sbuf("krr", [96, TT], BF16)
    P.dma(kr[64:96, :], krT.ap); P.dma(krr[64:96, :], krrT.ap, queue="pool")
    wqs = P.sbuf("wqs", [128, 3, 384], BF16); wqrs = P.sbuf("wqrs", [128, 3, 384], BF16)
    wkns = P.sbuf("wkns", [128, 2, 256], BF16); wkvs = P.sbuf("wkvs", [128, 2, 256], BF16)
    P.dma(wqs.ap, wq.ap.rearrange("(c p) n -> p c n", p=128), queue="pool")
    P.dma(wqrs.ap, wqr.ap.rearrange("(c p) n -> p c n", p=128), queue="pool")
    P.dma(wkns.ap, wkn.ap.rearrange("(c p) n -> p c n", p=128), queue="pool")
    P.dma(wkvs.ap, wkv.ap.rearrange("(c p) n -> p c n", p=128), queue="pool")
    rq = P.sbuf("rq", [128, TT]); rk = P.sbuf("rk", [128, TT]); rkc = P.sbuf("rkc", [128, 34])
    sq = P.sbuf("sq", [128, 3, 512], BF16)
    for (t0, n, _) in QBLK:
        for (src, nc_, dst, dim) in ((qd, 3, rq, 384.0), (kvd, 2, rk, 256.0)):
            P.tt(sq[:, 0:nc_, 0:n], src[:, 0:nc_, t0:t0 + n], src[:, 0:nc_, t0:t0 + n], ALU.mult)
            ps = gps()
            for c in range(nc_):
                P.matmul(ps[:, 0:n], ones16.ap, sq[:, c, 0:n], start=(c == 0), stop=(c == nc_ - 1))
            P.ts(dst[:, t0:t0 + n], ps[:, 0:n], 1.0 / dim, 1e-6, ALU.mult, ALU.add)
            P.act(dst[:, t0:t0 + n], dst[:, t0:t0 + n], AF.Sqrt)
            P.recip(dst[:, t0:t0 + n], dst[:, t0:t0 + n])
            if src is kvd:
                for tl in range(n // 128):
                    kt = t0 // 128 + tl
                    ps2 = gps()
                    for c in range(2):
                        P.matmul(ps2[:, 0:1], sq[:, c, tl * 128:(tl + 1) * 128], ones16[:, 0:1], start=(c == 0), stop=(c == 1))
                    P.ts(rkc[:, kt:kt + 1], ps2[:, 0:1], 1.0 / 256.0, 1e-6, ALU.mult, ALU.add)
    P.act(rkc.ap, rkc.ap, AF.Sqrt); P.recip(rkc.ap, rkc.ap)
    P.ts(rq.ap, rq.ap, 96.0 ** -0.5, None, ALU.mult)
    for c in range(3):
        P.ts(qd[:, c, :], qd[:, c, :], qgs[:, c:c + 1], None, ALU.mult, eng="pool")
    for c in range(2):
        P.ts(kvd[:, c, :], kvd[:, c, :], kgs[:, c:c + 1], None, ALU.mult, eng="pool")
    Rk = P.sbuf("Rk", [96, TT], BF16); tk = P.sbuf("tk", [96, TT])
    P.tt(tk[64:96, :], kr[64:96, :], Ct[64:96, :], ALU.mult)
    P.tt(kr[64:96, :], krr[64:96, :], St[64:96, :], ALU.mult)
    P.tt(Rk[64:96, :], tk[64:96, :], kr[64:96, :], ALU.add)
    Qh = P.sbuf("Qh", [96, TT], BF16); Kh = P.sbuf("Kh", [96, TT], BF16); Vh = P.sbuf("Vh", [128, 34, 128], BF16)
    P.memset(Vh.ap, 1.0)
    t1 = P.sbuf("t1", [96, 512]); t2 = P.sbuf("t2", [96, 512])
    es = [P.sbuf("es%d" % i, [128, 512], BF16) for i in range(4)]
    den = P.sbuf("den", [64, 512]); ost = [P.sbuf("ost%d" % i, [64, 512], BF16) for i in range(2)]
    ei = 0
    for h in range(4):
        for (t0, n, _) in QBLK:
            ps1 = gps(); ps2 = gps()
            for c in range(3):
                P.matmul(ps1[0:96, 0:n], wqs[:, c, h * 96:(h + 1) * 96], qd[:, c, t0:t0 + n], start=(c == 0), stop=(c == 2))
            for c in range(3):
                P.matmul(ps2[0:96, 0:n], wqrs[:, c, h * 96:(h + 1) * 96], qd[:, c, t0:t0 + n], start=(c == 0), stop=(c == 2))
            P.tt(t1[:, 0:n], ps1[0:96, 0:n], Ct[:, t0:t0 + n], ALU.mult)
            P.tt(t2[:, 0:n], ps2[0:96, 0:n], St[:, t0:t0 + n], ALU.mult)
            P.tt(t1[:, 0:n], t1[:, 0:n], t2[:, 0:n], ALU.add, eng="pool")
            P.tt(Qh[:, t0:t0 + n], t1[:, 0:n], rq[0:96, t0:t0 + n], ALU.mult)
            ps3 = gps()
            for c in range(2):
                P.matmul(ps3[0:64, 0:n], wkns[:, c, h * 64:(h + 1) * 64], kvd[:, c, t0:t0 + n], start=(c == 0), stop=(c == 1))
            P.tt(Kh[0:64, t0:t0 + n], ps3[0:64, 0:n], rk[0:64, t0:t0 + n], ALU.mult)
            P.copy(Kh[64:96, t0:t0 + n], Rk[64:96, t0:t0 + n], eng="pool")
        for kt in range(34):
            ps4 = gps()
            for c in range(2):
                P.matmul(ps4[:, 0:64], kvd[:, c, kt * 128:(kt + 1) * 128], wkvs[:, c, h * 64:(h + 1) * 64], start=(c == 0), stop=(c == 1))
            P.ts(Vh[:, kt, 0:64], ps4[:, 0:64], rkc[:, kt:kt + 1], None, ALU.mult)
        for qi, (q0, nq, nk) in enumerate(QBLK):
            po = pso[qi % 2]
            for kt in range(nk):
                ps = gps()
                P.matmul(ps[:, 0:nq], Kh[:, kt * 128:(kt + 1) * 128], Qh[:, q0:q0 + nq])
                e = es[ei % 4]; ei += 1
                P.act(e[:, 0:nq], ps[:, 0:nq], AF.Exp)
                P.matmul(po[:, 0:nq], Vh[:, kt, :], e[:, 0:nq], start=(kt == 0), stop=(kt == nk - 1))
            P.copy(den[:, 0:nq], po[64:128, 0:nq])
            P.recip(den[:, 0:nq], den[:, 0:nq])
            o = ost[qi % 2]
            P.tt(o[:, 0:nq], po[0:64, 0:nq], den[:, 0:nq], ALU.mult)
            P.dma(yo[h * 64:(h + 1) * 64, q0:q0 + nq], o[:, 0:nq], queue="sp", nowaw=True)
    return P.finish([yo])


def build_B2():
    P = Prog()
    X = lambda n, s, dt=F32: P.dram(n, s, dt, kind="ExternalInput")
    qT = X("qT", [128, TT], BF16); kT = X("kT", [128, TT], BF16); kM = X("kM", [TT, 128], BF16)
    vM = X("vM", [TT, 256], BF16); ogM = X("ogM", [TT, 256], BF16)
    gdT = [X("gdT%d" % i, [16, TT], BF16) for i in range(2)]
    gup = [X("gup%d" % i, [16, 128]) for i in range(2)]; gkb = [X("gkb%d" % i, [1, 128]) for i in range(2)]
    ngd = X("ng", [1, 128]); mk = X("masks", [128, 4, 128])
    yo = P.dram("ya", [TT, 256], F32, kind="ExternalOutput")
    pp = PsPool(P)
    masks = P.sbuf("masksb", [128, 4, 128]); P.dma(masks.ap, mk.ap)
    m16 = P.sbuf("m16", [128, 2, 128], BF16); P.copy(m16.ap, masks[:, 0:2, :])
    ones1 = P.sbuf("ones1", [1, 128], BF16); P.memset(ones1.ap, 1.0)
    q = P.sbuf("q", [128, TT], BF16); k = P.sbuf("k", [128, TT], BF16)
    P.dma(q.ap, qT.ap); P.dma(k.ap, kT.ap, queue="pool")
    km = P.sbuf("km", [128, 34, 128], BF16); vm = P.sbuf("vm", [128, 34, 256], BF16)
    P.dma(km.ap, kM.ap.rearrange("(n p) c -> p n c", p=128)); P.dma(vm.ap, vM.ap.rearrange("(n p) c -> p n c", p=128), queue="pool")
    gd = [P.sbuf("gd%d" % i, [16, TT], BF16) for i in range(2)]
    gu = [P.sbuf("gu%d" % i, [16, 128], BF16) for i in range(2)]; gb = [P.sbuf("gb%d" % i, [1, 128], BF16) for i in range(2)]
    for i in range(2):
        P.dma(gd[i].ap, gdT[i].ap); P.dma(gu[i].ap, gup[i].ap, queue="pool"); P.dma(gb[i].ap, gkb[i].ap, queue="pool")
    ng = P.sbuf("ngsb", [128, 128]); P.dma(ng.ap, View(ngd, ngd.t.ap()[0, :].partition_broadcast(128)))
    oacc = P.sbuf("oacc", [128, 34, 256])
    S32 = P.sbuf("S32", [128, 2, 128]); S16 = P.sbuf("S16", [128, 2, 128], BF16)
    sp = P.sbuf("sp", [128, 128]); E1 = P.sbuf("E1", [128, 128]); E2 = P.sbuf("E2", [128, 128]); E3 = P.sbuf("E3", [128, 128])
    dec = P.sbuf("dec", [128, 1]); qin = P.sbuf("qin", [128, 128], BF16); kin = P.sbuf("kin", [128, 128], BF16)
    kst = P.sbuf("kst", [128, 128], BF16); attm = [P.sbuf("attm%d" % i, [128, 128], BF16) for i in range(2)]
    for dr in range(2):
        order = list(range(34)) if dr == 0 else [1, 0] + list(range(33, 1, -1))
        last = 127 if dr == 0 else 0
        P.memset(S32[:, 0, :], 0.0); P.memset(S16[:, 0, :], 0.0)
        Sf = S32[:, 0, :]; Sb = S16[:, 0, :]
        for ti in order:
            tsl = slice(ti * 128, (ti + 1) * 128)
            ps = pp.get()
            P.matmul(ps[:, 0:128], gd[dr][:, tsl], gu[dr].ap, start=True, stop=False)
            P.matmul(ps[:, 0:128], ones1.ap, gb[dr].ap, start=False, stop=True)
            P.act(sp.ap, ps[:, 0:128], AF.Exp, scale=-1.0)
            P.act(sp.ap, sp.ap, AF.Ln, bias=1.0)
            pb = pp.get(); pf = pp.get()
            P.matmul(pb[:, 0:128], sp.ap, masks[:, dr, :])
            P.matmul(pf[:, 0:128], masks[:, 2 + dr, :], sp.ap)
            P.act(E1.ap, pb[:, 0:128], AF.Exp, scale=-1.0 / 16.0)
            P.act(E2.ap, pb[:, 0:128], AF.Exp, scale=1.0 / 16.0)
            P.act(E3.ap, pf[:, 0:128], AF.Exp, scale=-1.0 / 16.0)
            P.copy(dec.ap, E1[:, last:last + 1])
            P.stt(qin.ap, q[:, tsl], 0.125, E1.ap, ALU.mult, ALU.mult)
            P.tt(kin.ap, k[:, tsl], E2.ap, ALU.mult, eng="pool")
            P.tt(kst.ap, km[:, ti, :], E3.ap, ALU.mult, eng="pool")
            for hh in range(2):
                hs = slice(hh * 64, (hh + 1) * 64); vs = slice(hh * 128, (hh + 1) * 128)
                pa = pp.get()
                P.matmul(pa[:, 0:128], kin[hs, :], qin[hs, :])
                am = attm[hh]
                P.tt(am.ap, pa[:, 0:128], m16[:, dr, :], ALU.mult)
                po = pp.get()
                P.matmul(po[:, 0:128], am.ap, vm[:, ti, vs], start=True, stop=False)
                P.matmul(po[:, 0:128], qin[hs, :], Sb[hs, :], start=False, stop=True)
                if dr == 0:
                    P.copy(oacc[:, ti, vs], po[:, 0:128], eng="act")
                else:
                    P.tt(oacc[:, ti, vs], oacc[:, ti, vs], po[:, 0:128], ALU.add)
                pu = pp.get()
                P.matmul(pu[:, 0:128], kst.ap, vm[:, ti, vs])
                P.stt(Sf[hs, :], Sf[hs, :], dec[hs, 0:1], pu[hs, 0:128], ALU.mult, ALU.add)
                P.copy(Sb[hs, :], Sf[hs, :], eng="act")
    og = P.sbuf("og", [128, 256], BF16); sg = P.sbuf("sg", [128, 256]); sq = P.sbuf("sq2", [128, 256])
    ss = P.sbuf("ss", [128, 2]); yt = [P.sbuf("yt%d" % i, [128, 256]) for i in range(2)]
    for ti in range(34):
        P.dma(og.ap, ogM[ti * 128:(ti + 1) * 128, :])
        P.act(sg.ap, og.ap, AF.Silu)
        o = oacc[:, ti, :]
        P.tt(sq.ap, o, o, ALU.mult)
        P.reduce(ss.ap, sq.ap.rearrange("p (h v) -> p h v", h=2), ALU.add)
        P.ts(ss.ap, ss.ap, 1.0 / 128.0, 1e-6, ALU.mult, ALU.add)
        P.act(ss.ap, ss.ap, AF.Sqrt); P.recip(ss.ap, ss.ap)
        y = yt[ti % 2]
        P.tt(y.ap.rearrange("p (h v) -> p h v", h=2), o.rearrange("p (h v) -> p h v", h=2), bc(ss.ap, [128, 2, 128], 2), ALU.mult)
        P.tt(y.ap.rearrange("p (h v) -> p h v", h=2), y.ap.rearrange("p (h v) -> p h v", h=2), bc(ng.ap, [128, 2, 128], 1), ALU.mult, eng="pool")
        P.tt(y.ap, y.ap, sg.ap, ALU.mult)
        P.dma(yo[ti * 128:(ti + 1) * 128, :], y.ap, queue="pool", nowaw=True)
    return P.finish([yo])


CW = 0.6065306597126334


def build_B3():
    P = Prog()
    X = lambda n, s, dt=F32: P.dram(n, s, dt, kind="ExternalInput")
    zrkv = X("zrkv", [TT, 3, 768], BF16)
    lrd = X("lr", [3, 3, 128, TT], BF16)
    murkv = X("murkv", [2, 768]); mulrd = X("mulr", [128, 3, 2])
    wupd = X("wup", [128, 256]); w0d = X("w0", [2, 1, 256]); aupd = X("aup", [128, 256]); a0d = X("a0", [2, 1, 256])
    gupd = X("gup", [128, 256]); vecsd = X("vecs", [5, 256]); mk = X("masks", [128, 4, 128]); idd = X("ident", [128, 128])
    yo = P.dram("yr", [TT, 256], F32, kind="ExternalOutput")
    pp = PsPool(P)
    masks = P.sbuf("masksb", [128, 4, 128]); P.dma(masks.ap, mk.ap)
    id32 = P.sbuf("id32", [128, 128]); P.dma(id32.ap, idd.ap)
    id16 = P.sbuf("id16", [128, 128], BF16); P.copy(id16.ap, id32.ap)
    ones1 = P.sbuf("ones1", [1, 128], BF16); P.memset(ones1.ap, 1.0)
    onec = P.sbuf("onec", [128, 1]); P.memset(onec.ap, 1.0)
    M_I = (2, 3); QS_I = (3, 2); QN_I = (0, 1); INC_I = (0, 1); EXC_I = (3, 2); SUF_I = (2, 3)
    maskM = P.sbuf("maskM", [128, 2, 128], BF16); maskQN = P.sbuf("maskQN", [128, 2, 256], BF16)
    for d in range(2):
        P.copy(maskM[:, d, :], masks[:, M_I[d], :])
        P.copy(maskQN[:, d, 0:128], masks[:, QS_I[d], :]); P.copy(maskQN[:, d, 128:256], masks[:, QN_I[d], :])
    murep = P.sbuf("murep", [128, 2, 768])
    P.dma(murep.ap, View(murkv, murkv.t.ap().partition_broadcast(128)))
    c0rep = P.sbuf("c0rep", [128, 768])
    P.tt(c0rep.ap, murep[:, 0, :], murep[:, 1, :], ALU.add)
    P.ts(c0rep.ap, c0rep.ap, -1.0, 1.0, ALU.mult, ALU.add)
    vrep = P.sbuf("vrep", [128, 5, 256]); P.dma(vrep.ap, View(vecsd, vecsd.t.ap().partition_broadcast(128)), queue="pool")
    wup = P.sbuf("wup16", [128, 256], BF16); aup = P.sbuf("aup16", [128, 256], BF16)
    w0 = P.sbuf("w016", [1, 2, 256], BF16); a0 = P.sbuf("a016", [1, 2, 256], BF16); gup = P.sbuf("gup16", [128, 256], BF16)
    P.dma(wup.ap, wupd.ap, queue="pool"); P.dma(aup.ap, aupd.ap, queue="pool")
    P.dma(w0.ap, w0d.ap.rearrange("d r c -> r d c"), queue="pool"); P.dma(a0.ap, a0d.ap.rearrange("d r c -> r d c"), queue="pool")
    P.dma(gup.ap, gupd.ap, queue="pool")
    mulr = P.sbuf("mulrs", [128, 3, 2]); P.dma(mulr.ap, mulrd.ap)
    clr = P.sbuf("clr", [128, 3])
    P.tt(clr.ap, mulr[:, :, 0], mulr[:, :, 1], ALU.add); P.ts(clr.ap, clr.ap, -1.0, 1.0, ALU.mult, ALU.add)
    st = [P.sbuf("lrst%d" % i, [128, 1088], BF16) for i in range(6)]
    LRs = [P.sbuf("LRs%d" % i, [128, TT], BF16) for i in range(3)]
    LRg = LRs[2]
    tl = P.sbuf("lrt", [128, 1088])
    kk_ = 0
    for i in range(3):
        fn = (AF.Tanh, AF.Identity, AF.Sigmoid)[i]
        for b0 in range(0, TT, 1088):
            bs = slice(b0, b0 + 1088)
            sb = [st[(kk_ % 2) * 3 + j] for j in range(3)]; kk_ += 1
            for j in range(3):
                P.dma(sb[j].ap, lrd[i, j][:, bs], queue="sp" if j % 2 else "pool")
            P.ts(tl.ap, sb[0].ap, clr[:, i:i + 1], None, ALU.mult)
            P.stt(tl.ap, sb[1].ap, mulr[:, i, 0:1], tl.ap, ALU.mult, ALU.add)
            P.stt(tl.ap, sb[2].ap, mulr[:, i, 1:2], tl.ap, ALU.mult, ALU.add)
            P.act(LRs[i][:, bs], tl.ap, fn)
    yacc = P.sbuf("yacc", [128, 34, 256]); racc = P.sbuf("racc", [128, 34, 4])
    P.memset(yacc.ap, 0.0, eng="pool"); P.memset(racc.ap, 0.0, eng="pool")
    S16 = P.sbuf("S16", [64, 8, 64], BF16); P.memset(S16.ap, 0.0)
    zb = [P.sbuf("zb%d" % i, [128, 3, 768], BF16) for i in range(2)]
    zv = zrkv.ap.rearrange("(n p) j c -> n p j c", p=128)

    def mk_prep(tag):
        f = lambda nm, sh, dt=F32: P.sbuf(nm + tag, sh, dt)
        return dict(m1=f("m1", [128, 768]), m2=f("m2", [128, 768]), V16=f("V16", [128, 256], BF16),
                    sw=f("sw", [128, 256]), aa=f("aa", [128, 256]), kk=f("kk", [128, 256]), t1=f("t1", [128, 256]),
                    keff=f("keff", [128, 256]), bb=f("bb", [128, 256]), ss=f("ss", [128, 4]),
                    EA=f("EA", [128, 256]), ER=f("ER", [128, 256]), EI=f("EI", [128, 256]), ES=f("ES", [128, 256]),
                    Ap32=f("Ap32", [128, 256]), Ap16=f("Ap16", [128, 256], BF16), Rp16=f("Rp16", [128, 256], BF16),
                    Bm16=f("Bm16", [128, 256], BF16), Km16=f("Km16", [128, 256], BF16), Bc16=f("Bc16", [128, 256], BF16),
                    Kc16=f("Kc16", [128, 256], BF16), Wc=f("Wc", [64, 4]),
                    FTAR=f("FTAR", [128, 2, 256], BF16), FTBK=f("FTBK", [128, 2, 256], BF16))
    PB = {d: mk_prep("_%d" % d) for d in range(2)}

    def prep(d, ti, par):
        T_ = PB[d]
        zin = zb[(d + par) % 2]
        P.dma(zin.ap, zv[ti])
        m1, m2 = T_["m1"], T_["m2"]
        P.tt(m1.ap, zin[:, 0, :], c0rep.ap, ALU.mult)
        P.tt(m2.ap, zin[:, 1, :], murep[:, 0, :], ALU.mult, eng="pool")
        P.tt(m1.ap, m1.ap, m2.ap, ALU.add)
        P.tt(m2.ap, zin[:, 2, :], murep[:, 1, :], ALU.mult, eng="pool")
        P.tt(m1.ap, m1.ap, m2.ap, ALU.add)
        r = m1[:, 0:256]; k = m1[:, 256:512]; v = m1[:, 512:768]
        P.copy(T_["V16"].ap, v, eng="pool")
        tsl = slice(ti * 128, (ti + 1) * 128)
        pw = pp.get()
        ds = slice(64 * d, 64 * d + 64)
        P.matmul(pw[:, 0:256], LRs[0][ds, tsl], wup[ds, :], start=True, stop=False)
        P.matmul(pw[:, 0:256], ones1.ap, w0[:, d, :], start=False, stop=True)
        P.act(T_["sw"].ap, pw[:, 0:256], AF.Sigmoid)
        pa = pp.get()
        P.matmul(pa[:, 0:256], LRs[1][ds, tsl], aup[ds, :], start=True, stop=False)
        P.matmul(pa[:, 0:256], ones1.ap, a0[:, d, :], start=False, stop=True)
        P.act(T_["aa"].ap, pa[:, 0:256], AF.Sigmoid)
        kk, t1, ss = T_["kk"], T_["t1"], T_["ss"]
        P.tt(kk.ap, k, vrep[:, 0, :], ALU.mult)
        P.tt(t1.ap, kk.ap, kk.ap, ALU.mult, eng="pool")
        P.reduce(ss.ap, t1.ap.rearrange("p (h c) -> p h c", h=4), ALU.add)
        P.ts(ss.ap, ss.ap, 1e-12, None, ALU.add)
        P.act(ss.ap, ss.ap, AF.Sqrt); P.recip(ss.ap, ss.ap)
        P.tt(kk.ap.rearrange("p (h c) -> p h c", h=4), kk.ap.rearrange("p (h c) -> p h c", h=4), bc(ss.ap, [128, 4, 64], 2), ALU.mult)
        keff, bb = T_["keff"], T_["bb"]
        P.stt(t1.ap, T_["aa"].ap, -1.0, vrep[:, 1, :], ALU.add, ALU.mult)
        P.stt(keff.ap, t1.ap, 1.0, k, ALU.add, ALU.mult)
        P.tt(bb.ap, kk.ap, T_["aa"].ap, ALU.mult, eng="pool")
        P.tt(t1.ap, r, keff.ap, ALU.mult)
        P.tt(t1.ap, t1.ap, vrep[:, 2, :], ALU.mult)
        P.reduce(ss.ap, t1.ap.rearrange("p (h c) -> p h c", h=4), ALU.add)
        P.tt(racc[:, ti, :], racc[:, ti, :], ss.ap, ALU.add)
        sw = T_["sw"]
        pci = pp.get(); pce = pp.get(); pcs = pp.get()
        P.matmul(pci[:, 0:256], masks[:, INC_I[d], :], sw.ap)
        P.matmul(pce[:, 0:256], masks[:, EXC_I[d], :], sw.ap)
        P.matmul(pcs[:, 0:256], masks[:, SUF_I[d], :], sw.ap)
        P.act(T_["EA"].ap, pce[:, 0:256], AF.Exp, scale=-CW)
        P.act(T_["ER"].ap, pci[:, 0:256], AF.Exp, scale=-CW)
        P.act(T_["EI"].ap, pci[:, 0:256], AF.Exp, scale=CW)
        P.act(T_["ES"].ap, pcs[:, 0:256], AF.Exp, scale=-CW)
        P.stt(T_["Ap32"].ap, kk.ap, -1.0, T_["EA"].ap, ALU.mult, ALU.mult)
        P.copy(T_["Ap16"].ap, T_["Ap32"].ap, eng="pool")
        P.tt(T_["Rp16"].ap, r, T_["ER"].ap, ALU.mult)
        P.tt(T_["Bm16"].ap, bb.ap, T_["EI"].ap, ALU.mult, eng="pool")
        P.tt(T_["Km16"].ap, keff.ap, T_["EI"].ap, ALU.mult)
        P.tt(T_["Bc16"].ap, bb.ap, T_["ES"].ap, ALU.mult, eng="pool")
        P.tt(T_["Kc16"].ap, keff.ap, T_["ES"].ap, ALU.mult)
        pwc = pp.get()
        for h in range(4):
            P.matmul(pwc[0:64, h * 8:h * 8 + 1], sw[:, h * 64:(h + 1) * 64], onec.ap)
        P.act(T_["Wc"].ap, pwc[0:64, 0:32].rearrange("p (h e) -> p h e", e=8)[:, :, 0], AF.Exp, scale=-CW)
        for (dstn, srcs) in (("FTAR", ("Ap16", "Rp16")), ("FTBK", ("Bm16", "Km16"))):
            pt = pp.get(); ptv = pt.ap.bitcast(BF16)
            for pair in range(2):
                for qi in range(2):
                    c0 = (pair * 2 + qi) * 128
                    P.transpose(ptv[:, c0:c0 + 128], T_[srcs[qi]][:, pair * 128:(pair + 1) * 128], id16.ap)
            P.copy(T_[dstn].ap.rearrange("p a c -> p (a c)"), ptv[:, 0:512], eng="act")
        return T_

    UB = []
    for u in range(8):
        f = lambda nm, sh, dt=F32: P.sbuf("%s_u%d" % (nm, u), sh, dt)
        UB.append(dict(Mab=f("Mab", [128, 128]), Q32=f("Q32", [128, 128]), Nrb=f("Nrb", [128, 128], BF16),
                       T3=f("T3", [128, 256], BF16), Z32=f("Z32", [128, 128]), Z16=f("Z16", [128, 128], BF16),
                       MQ=[f("MQ%d" % i, [128, 256]) for i in range(2)],
                       G16=f("G16", [64, 64], BF16), RT=f("RT", [64, 128], BF16)))
    orders = [list(range(34)), [1, 0] + list(range(33, 1, -1))]
    for step in range(34):
        par = step % 2
        units = []
        for d in range(2):
            ti = orders[d][step]
            T_ = prep(d, ti, par)
            for h in range(4):
                units.append((d, ti, h, T_, UB[d * 4 + h]))
        for (d, ti, h, T_, U) in units:
            pair, hs = h // 2, slice(64 * (h % 2), 64 * (h % 2) + 64)
            AR, BK = T_["FTAR"], T_["FTBK"]
            p1 = pp.get(); p2 = pp.get(); p3 = pp.get()
            P.matmul(p1[:, 0:128], AR[hs, pair, 0:128], BK[hs, pair, 0:128])
            P.matmul(p2[:, 0:256], BK[hs, pair, 0:128], AR[hs, pair, 0:256])
            P.matmul(p3[:, 0:256], BK[hs, pair, 128:256], AR[hs, pair, 0:256])
            P.tt(U["Mab"].ap, p1[:, 0:128], masks[:, M_I[d], :], ALU.mult)
            P.tt(U["Q32"].ap, p2[:, 0:128], masks[:, QS_I[d], :], ALU.mult)
            P.tt(U["Nrb"].ap, p2[:, 128:256], maskQN[:, d, 128:256], ALU.mult)
            P.tt(U["T3"].ap, p3[:, 0:256], maskQN[:, d, :], ALU.mult)
        for (d, ti, h, T_, U) in units:
            hc = slice(h * 64, (h + 1) * 64)
            px = pp.get()
            P.matmul(px[:, 0:64], U["T3"][:, 0:128], T_["V16"][:, hc])
            P.copy(U["Z32"][:, 0:64], px[:, 0:64], eng="act")
            P.copy(U["Z32"][:, 64:128], T_["Ap32"][:, hc], eng="pool")
        for lv in range(7):
            for (d, ti, h, T_, U) in units:
                if lv == 0:
                    Mv, Qv = U["Mab"].ap, U["Q32"].ap
                else:
                    mq = U["MQ"][(lv - 1) % 2]
                    Mv, Qv = mq[:, 0:128], mq[:, 128:256]
                pa = pp.get()
                P.matmul(pa[:, 0:128], Qv, U["Z32"].ap)
                if lv < 6:
                    pq = pp.get()
                    P.matmul(pq[:, 0:128], Qv, Mv)
                    P.matmul(pq[:, 128:256], Mv, Qv)
                P.tt(U["Z32"].ap, U["Z32"].ap, pa[:, 0:128], ALU.add)
                if lv < 6:
                    P.copy(U["MQ"][lv % 2].ap, pq[:, 0:256], eng="act")
        for (d, ti, h, T_, U) in units:
            P.copy(U["Z16"].ap, U["Z32"].ap, eng="pool")
        for (d, ti, h, T_, U) in units:
            hc = slice(h * 64, (h + 1) * 64)
            pg = pp.get(); pr = pp.get()
            P.matmul(pg[0:64, 0:64], U["Z16"][:, 64:128], T_["Bc16"][:, hc])
            P.matmul(pr[0:64, 0:128], U["Z16"][:, 64:128], U["Nrb"].ap, start=True, stop=False)
            P.matmul(pr[0:64, 0:128], T_["Rp16"][:, hc], id16.ap, start=False, stop=True)
            P.stt(U["G16"].ap, id32[0:64, 0:64], T_["Wc"][:, h:h + 1], pg[0:64, 0:64], ALU.mult, ALU.add)
            P.copy(U["RT"].ap, pr[0:64, 0:128], eng="act")
        for (d, ti, h, T_, U) in units:
            hc = slice(h * 64, (h + 1) * 64); ch = d * 4 + h
            ph = pp.get(); py = pp.get()
            P.matmul(ph[0:64, 0:64], T_["Bc16"][:, hc], U["Z16"][:, 0:64], start=True, stop=False)
            P.matmul(ph[0:64, 0:64], T_["Kc16"][:, hc], T_["V16"][:, hc], start=False, stop=False)
            P.matmul(ph[0:64, 0:64], U["G16"].ap, S16[:, ch, :], start=False, stop=True)
            P.matmul(py[:, 0:64], U["Nrb"].ap, U["Z16"][:, 0:64], start=True, stop=False)
            P.matmul(py[:, 0:64], U["T3"][:, 128:256], T_["V16"][:, hc], start=False, stop=False)
            P.matmul(py[:, 0:64], U["RT"].ap, S16[:, ch, :], start=False, stop=True)
            P.copy(S16[:, ch, :], ph[0:64, 0:64], eng="act")
            P.tt(yacc[:, ti, hc], yacc[:, ti, hc], py[:, 0:64], ALU.add)
    zv2 = [P.sbuf("zv2_%d" % i, [128, 3, 256], BF16) for i in range(2)]
    e1 = P.sbuf("e1", [128, 256]); e2 = P.sbuf("e2", [128, 256]); es = P.sbuf("es_", [128, 4]); er = P.sbuf("er_", [128, 4])
    yt = [P.sbuf("yrt%d" % i, [128, 256]) for i in range(2)]
    H4 = lambda vw: vw.rearrange("p (h c) -> p h c", h=4)
    for ti in range(34):
        zin = zv2[ti % 2]
        P.dma(zin.ap, zv[ti][:, :, 512:768])
        P.tt(e1.ap, zin[:, 0, :], c0rep[:, 512:768], ALU.mult, eng="pool")
        P.tt(e2.ap, zin[:, 1, :], murep[:, 0, 512:768], ALU.mult, eng="pool")
        P.tt(e1.ap, e1.ap, e2.ap, ALU.add, eng="pool")
        P.tt(e2.ap, zin[:, 2, :], murep[:, 1, 512:768], ALU.mult, eng="pool")
        P.tt(e1.ap, e1.ap, e2.ap, ALU.add, eng="pool")
        P.tt(H4(e1.ap), H4(e1.ap), bc(racc[:, ti, :], [128, 4, 64], 2), ALU.mult, eng="pool")
        y = yacc[:, ti, :]
        P.reduce(es.ap, H4(y), ALU.add)
        P.ts(es.ap, es.ap, 1.0 / 64.0, None, ALU.mult)
        P.tt(H4(y), H4(y), bc(es.ap, [128, 4, 64], 2), ALU.subtract)
        P.tt(e2.ap, y, y, ALU.mult)
        P.reduce(er.ap, H4(e2.ap), ALU.add)
        P.ts(er.ap, er.ap, 1.0 / 64.0, 64e-5, ALU.mult, ALU.add)
        P.act(er.ap, er.ap, AF.Sqrt); P.recip(er.ap, er.ap)
        o = yt[ti % 2]
        P.tt(H4(o.ap), H4(y), bc(er.ap, [128, 4, 64], 2), ALU.mult)
        P.tt(o.ap, o.ap, vrep[:, 3, :], ALU.mult)
        P.tt(o.ap, o.ap, vrep[:, 4, :], ALU.add)
        P.tt(o.ap, o.ap, e1.ap, ALU.add)
        pg = pp.get()
        P.matmul(pg[:, 0:256], LRg[:, ti * 128:(ti + 1) * 128], gup.ap)
        P.tt(o.ap, o.ap, pg[:, 0:256], ALU.mult)
        P.dma(yo[ti * 128:(ti + 1) * 128, :], o.ap, queue="pool", nowaw=True)
    return P.finish([yo])


_PROGS = {}


_BUILDERS = {"A": build_A, "B1": build_B1, "B2": build_B2, "B3": build_B3, "C1": build_C1, "C2": build_C2}
_TIMES = []


def _run(name, in_maps):
    import time as _t
    t0 = _t.time()
    nc = _BUILDERS[name]()
    t1 = _t.time()
    res = run_bass_kernel_spmd(nc, in_maps, core_ids=list(range(8)))
    _TIMES.append((name, round(t1 - t0, 1), round(_t.time() - t1, 1)))
    return res.results


def _shift3(a):
    p = np.zeros_like(a); n = np.zeros_like(a)
    p[1:256] = a[0:255]; p[257:] = a[256:-1]
    n[0:255] = a[1:256]; n[256:-1] = a[257:]
    return np.stack([a, p, n], 1)


def _colv(v, n):
    return np.ascontiguousarray(np.asarray(v, np.float32).reshape(n, 128).T)


def _rope_tables():
    inv = (10000.0 ** (-np.arange(8, dtype=np.float32) / 8)).astype(np.float32)
    row = np.repeat(np.arange(64, dtype=np.float32), 64); col = np.tile(np.arange(64, dtype=np.float32), 64)
    ang = np.concatenate([row[:, None] * inv, col[:, None] * inv], -1).astype(np.float32)
    cos, sin = np.cos(ang).astype(np.float32), np.sin(ang).astype(np.float32)
    C = np.ones((96, TT), np.float32); S = np.zeros((96, TT), np.float32)
    C[64:80, 256:] = cos.T; C[80:96, 256:] = cos.T
    S[64:80, 256:] = -sin.T; S[80:96, 256:] = sin.T
    return C, S


def _tok(h):
    return np.concatenate([np.arange(128 * h, 128 * h + 128), 256 + np.arange(2048 * h, 2048 * h + 2048)])


def kernel(x, c, ctx, c_ctx, ada_w, ada_b, w_in, gla_gk_up, gla_gk_b, gla_norm_g, mla_q_norm_g, mla_kv_norm_g,
           mla_w_uq, mla_w_ukv, rwkv_shift_mu, rwkv_w0, rwkv_w_up, rwkv_a0, rwkv_a_up, rwkv_g_up, rwkv_k_k,
           rwkv_k_a, rwkv_r_k, rwkv_ln_g, rwkv_ln_b, w_branch, w_out, ln1_g, ln1_b, mlp_w1, mlp_w2, ln2_g, ln2_b,
           _layers=4):
    f32 = lambda a: np.ascontiguousarray(np.asarray(a, np.float32))
    xf = np.concatenate([f32(ctx), f32(x)], 1)
    C, S = _rope_tables()
    s_ = np.arange(128)[:, None]; t_ = np.arange(128)[None, :]
    masks = np.stack([s_ <= t_, s_ >= t_, s_ > t_, s_ < t_], 1).astype(np.float32)
    toks = [_tok(0), _tok(1)]
    for l in range(_layers):
        ims = []
        for cid in range(8):
            b, h = cid // 2, cid % 2
            c2 = np.stack([f32(c)[b], f32(c_ctx)], -1).reshape(8, 128, 2).transpose(1, 0, 2).copy()
            ims.append({"xT": np.ascontiguousarray(xf[b][toks[h]].T), "c2": c2, "ada_w": f32(ada_w[l]),
                        "ada_bT": _colv(ada_b[l], 48), "w_in": f32(w_in[l])})
        rA = _run("A", ims)
        z = np.empty((4, TT, 7232), NPBF)
        for cid in range(8):
            z[cid // 2][toks[cid % 2]] = rA[cid]["zT"].T
        mods = [rA[2 * b]["mod"] for b in range(4)]
        ims = []
        for cid in range(8):
            b, h0 = cid // 2, 4 * (cid % 2)
            zb = z[b]; kr = zb[:, 2208:2240]
            wq = f32(mla_w_uq[l])[:, h0 * 96:(h0 + 4) * 96]; w3 = wq.reshape(384, 4, 96)
            wqr = np.concatenate([w3[:, :, 0:64], w3[:, :, 80:96], w3[:, :, 64:80]], -1).reshape(384, 384)
            k3 = f32(mla_w_ukv[l]).reshape(256, 8, 128)[:, h0:h0 + 4]
            ims.append({"qdT": np.ascontiguousarray(zb[:, 1568:1952].T), "kvdT": np.ascontiguousarray(zb[:, 1952:2208].T),
                        "krT": np.ascontiguousarray(kr.T), "krrT": np.ascontiguousarray(np.concatenate([kr[:, 16:32], kr[:, 0:16]], 1).T),
                        "wq": np.ascontiguousarray(wq), "wqr": np.ascontiguousarray(wqr),
                        "wkn": np.ascontiguousarray(k3[:, :, 0:64].reshape(256, 256)),
                        "wkv": np.ascontiguousarray(k3[:, :, 64:128].reshape(256, 256)),
                        "qg": _colv(mla_q_norm_g[l], 3), "kg": _colv(mla_kv_norm_g[l], 2), "ropeC": C, "ropeS": S})
        rB1 = _run("B1", ims)
        yb = np.empty((4, TT, 512), NPBF)
        for cid in range(8):
            yb[cid // 2][:, 256 * (cid % 2):256 * (cid % 2) + 256] = rB1[cid]["ybT"].T
        ims = []
        for cid in range(8):
            b, h0 = cid // 2, 2 * (cid % 2)
            zb = z[b]; cs = slice(h0 * 64, h0 * 64 + 128)
            gu = f32(gla_gk_up[l]); gbv = f32(gla_gk_b[l])
            ims.append({"qT": np.ascontiguousarray(zb[:, 0:256][:, cs].T), "kT": np.ascontiguousarray(zb[:, 256:512][:, cs].T),
                        "kM": np.ascontiguousarray(zb[:, 256:512][:, cs]),
                        "vM": np.ascontiguousarray(zb[:, 512 + h0 * 128:512 + h0 * 128 + 256]),
                        "ogM": np.ascontiguousarray(zb[:, 1056 + h0 * 128:1056 + h0 * 128 + 256]),
                        "gdT0": np.ascontiguousarray(zb[:, 1024:1040].T), "gdT1": np.ascontiguousarray(zb[:, 1040:1056].T),
                        "gup0": np.ascontiguousarray(gu[0][:, cs]), "gup1": np.ascontiguousarray(gu[1][:, cs]),
                        "gkb0": np.ascontiguousarray(gbv[0][None, cs]), "gkb1": np.ascontiguousarray(gbv[1][None, cs]),
                        "ng": f32(gla_norm_g[l])[None], "masks": masks})
        rB2 = _run("B2", ims)
        ya = np.empty((4, TT, 512), NPBF)
        for cid in range(8):
            ya[cid // 2][:, 256 * (cid % 2):256 * (cid % 2) + 256] = rB2[cid]["ya"].astype(NPBF)
        ims = []
        mu = f32(rwkv_shift_mu[l]); ident = np.eye(128, dtype=np.float32)
        for cid in range(8):
            b, h0 = cid // 2, 4 * (cid % 2)
            zb = z[b]; base = 2240; cs = slice(h0 * 64, h0 * 64 + 256)
            sel = np.concatenate([np.arange(base + o + h0 * 64, base + o + h0 * 64 + 256) for o in (0, 512, 1024)])
            lr = np.stack([_shift3(zb[:, base + o:base + o + 128]).transpose(1, 2, 0) for o in (1536, 1664, 1792)], 0)
            vecs = np.stack([f32(v_[l])[cs] for v_ in (rwkv_k_k, rwkv_k_a, rwkv_r_k, rwkv_ln_g, rwkv_ln_b)], 0)
            ims.append({"zrkv": np.ascontiguousarray(_shift3(zb[:, sel])), "lr": np.ascontiguousarray(lr),
                        "murkv": np.ascontiguousarray(mu[:, sel - base]),
                        "mulr": np.ascontiguousarray(np.stack([mu[:, o:o + 128].T for o in (1536, 1664, 1792)], 1)),
                        "wup": np.ascontiguousarray(np.concatenate([f32(rwkv_w_up[l])[0][:, cs], f32(rwkv_w_up[l])[1][:, cs]], 0)),
                        "w0": np.ascontiguousarray(f32(rwkv_w0[l])[:, None, cs]),
                        "aup": np.ascontiguousarray(np.concatenate([f32(rwkv_a_up[l])[0][:, cs], f32(rwkv_a_up[l])[1][:, cs]], 0)),
                        "a0": np.ascontiguousarray(f32(rwkv_a0[l])[:, None, cs]),
                        "gup": np.ascontiguousarray(f32(rwkv_g_up[l])[:, cs]), "vecs": np.ascontiguousarray(vecs),
                        "masks": masks, "ident": ident})
        rB3 = _run("B3", ims)
        yr = np.empty((4, TT, 512), NPBF)
        for cid in range(8):
            yr[cid // 2][:, 256 * (cid % 2):256 * (cid % 2) + 256] = rB3[cid]["yr"].astype(NPBF)
        ims = []
        for cid in range(8):
            b, h = cid // 2, cid % 2
            y = np.concatenate([ya[b], yb[b], yr[b]], 1)[toks[h]]
            ims.append({"xT": np.ascontiguousarray(xf[b][toks[h]].T), "yT": np.ascontiguousarray(y.T),
                        "gT": np.ascontiguousarray(rA[cid]["zT"][4160:]), "mod": mods[b],
                        "w_branch": f32(w_branch[l]).reshape(1536, 1024), "w_out": f32(w_out[l]),
                        "ln_g": _colv(ln1_g[l], 8), "ln_b": _colv(ln1_b[l], 8)})
        rC1 = _run("C1", ims)
        ims = [{"xT": rC1[cid]["x1T"], "mod": mods[cid // 2], "w1": f32(mlp_w1[l]), "w2": f32(mlp_w2[l]),
                "ln_g": _colv(ln2_g[l], 8), "ln_b": _colv(ln2_b[l], 8)} for cid in range(8)]
        rC2 = _run("C2", ims)
        for cid in range(8):
            xf[cid // 2][toks[cid % 2]] = rC2[cid]["x2T"].T
    return np.ascontiguousarray(xf[:, 256:, :])
```

```python
import numpy as np
import concourse.bass as bass
import concourse.mybir as mybir

F32 = mybir.dt.float32
BF16 = mybir.dt.bfloat16
AF = mybir.ActivationFunctionType
ALU = mybir.AluOpType
AX = mybir.AxisListType

ENGS = ("pe", "act", "dve", "pool", "sp")


class Buf:
    def __init__(self, prog, t, name, space):
        self.prog = prog
        self.t = t
        self.name = name
        self.space = space
        self.last_write = None
        self.reads = {}
        self.war = {}
        self.dsem = None
        self.dcount = 0

    def __getitem__(self, idx):
        return View(self, self.t.__getitem__(idx))

    @property
    def ap(self):
        return View(self, self.t[:] if self.space != "dram" else self.t.ap())


class View:
    def __init__(self, buf, ap):
        self.buf = buf
        self.ap = ap

    def __getitem__(self, idx):
        return View(self.buf, self.ap.__getitem__(idx))

    def rearrange(self, *a, **k):
        return View(self.buf, self.ap.rearrange(*a, **k))

    def bitcast(self, dt):
        return View(self.buf, self.ap.bitcast(dt))


class Prog:
    def __init__(self, same_engine_sync=True):
        self.nc = bass.Bass("TRN2", target_bir_lowering=False)
        self.ops = {e: [] for e in ENGS}
        self.count = {e: 0 for e in ENGS}
        self.sems = {}
        self.waited = {e: {} for e in ENGS}
        self.same_engine_sync = same_engine_sync
        self.nbuf = 0
        for e in ("pe", "act", "dve", "pool"):
            self.sems[e] = self.nc.alloc_semaphore(name="s_" + e)
        self.final_tokens = []
        self.tag = ""

    def sbuf(self, name, shape, dt=F32):
        t = self.nc.alloc_sbuf_tensor(name, list(shape), dt)
        return Buf(self, t, name, "sbuf")

    def psum(self, name, shape, dt=F32):
        t = self.nc.alloc_psum_tensor(name, list(shape), dt)
        return Buf(self, t, name, "psum")

    def dram(self, name, shape, dt=F32, kind="Internal"):
        t = self.nc.dram_tensor(name, list(shape), dt, kind=kind)
        return Buf(self, t, name, "dram")

    def sub(self, parent, name):
        b = Buf(self, parent.t, name, parent.space)
        return b

    def _need(self, eng, tok, waits):
        if tok is None:
            return
        key, val, teng = tok
        if teng == eng:
            if eng in ("pe", "sp"):
                return
            if not self.same_engine_sync:
                return
        if self.waited[eng].get(key, 0) >= val:
            return
        cur = waits.get(key, 0)
        if val > cur:
            waits[key] = val

    def _deps(self, eng, reads, writes, pe_accum=False):
        waits = {}
        for v in reads:
            self._need(eng, v.buf.last_write, waits)
        for v in writes:
            b = v.buf
            if not (pe_accum and b.last_write is not None and b.last_write[2] == "pe"):
                self._need(eng, b.last_write, waits)
            for src in (b.reads, b.war):
                for reng, tok in src.items():
                    if reng == eng and eng != "sp":
                        continue
                    self._need(eng, tok, waits)
        for k, v in waits.items():
            self.waited[eng][k] = v
        return list(waits.items())

    def _commit(self, eng, tok, reads, writes):
        for v in reads:
            v.buf.reads[eng if eng != "sp" else ("sp", tok[0])] = tok
        for v in writes:
            v.buf.last_write = tok
            if v.buf.reads:
                v.buf.war = v.buf.reads
                v.buf.reads = {}

    def op(self, eng, fn, reads, writes, pe_accum=False):
        waits = self._deps(eng, reads, writes, pe_accum)
        self.count[eng] += 1
        tok = (eng, self.count[eng], eng)
        self.ops[eng].append((waits, fn, (eng, 1), self.tag))
        self._commit(eng, tok, reads, writes)
        return tok

    def dma(self, out, in_, queue="sp", nowaw=False, **kw):
        b = out.buf
        if b.dsem is None:
            b.dsem = "d_" + b.name + str(self.nbuf)
            self.nbuf += 1
            self.sems[b.dsem] = self.nc.alloc_semaphore(name=b.dsem[:30])
        waits = {}
        self._need(queue, in_.buf.last_write, waits)
        if not (nowaw and b.last_write is not None and b.last_write[0] == b.dsem):
            self._need(queue, b.last_write, waits)
        for src in (b.reads, b.war):
            for reng, tok in src.items():
                self._need(queue, tok, waits)
        for k, v in waits.items():
            self.waited[queue][k] = v
        b.dcount += 16
        tok = (b.dsem, b.dcount, "dma")
        oap, iap = out.ap, in_.ap
        fn = lambda e: e.dma_start(out=oap, in_=iap, **kw)
        if queue != "sp":
            self.count[queue] += 0
        self.ops[queue].append((list(waits.items()), fn, (b.dsem, 16), self.tag))
        in_.buf.reads[("dma", b.dsem)] = tok
        b.last_write = tok
        if b.reads:
            b.war = b.reads
            b.reads = {}
        return tok

    def matmul(self, out, lhsT, rhs, start=True, stop=True, **kw):
        o, l, r = out.ap, lhsT.ap, rhs.ap
        return self.op("pe", lambda e: e.matmul(o, l, r, start=start, stop=stop, **kw),
                       [lhsT, rhs], [out], pe_accum=not start)

    def transpose(self, out, in_, ident):
        o, i, d = out.ap, in_.ap, ident.ap
        return self.op("pe", lambda e: e.transpose(o, i, d), [in_, ident], [out])

    def act(self, out, in_, func, bias=None, scale=None, accum_out=None, eng="act"):
        reads = [in_]
        kw = {}
        if bias is not None:
            if isinstance(bias, View):
                reads.append(bias); kw["bias"] = bias.ap
            else:
                kw["bias"] = bias
        if scale is not None:
            if isinstance(scale, View):
                reads.append(scale); kw["scale"] = scale.ap
            else:
                kw["scale"] = scale
        writes = [out]
        if accum_out is not None:
            writes.append(accum_out); kw["accum_out"] = accum_out.ap
        o, i = out.ap, in_.ap
        return self.op(eng, lambda e: e.activation(o, i, func, **kw), reads, writes)

    def tt(self, out, in0, in1, op, eng="dve"):
        o, a, b = out.ap, in0.ap, in1.ap
        return self.op(eng, lambda e: e.tensor_tensor(o, a, b, op), [in0, in1], [out])

    def ts(self, out, in0, s1, s2, op0, op1=None, eng="dve", accum_out=None):
        reads = [in0]
        a1 = s1.ap if isinstance(s1, View) else s1
        a2 = s2.ap if isinstance(s2, View) else s2
        if isinstance(s1, View): reads.append(s1)
        if isinstance(s2, View): reads.append(s2)
        o, a = out.ap, in0.ap
        writes = [out]
        kw = {}
        if accum_out is not None:
            writes.append(accum_out); kw["accum_out"] = accum_out.ap
        if op1 is None:
            return self.op(eng, lambda e: e.tensor_scalar(o, a, a1, None, op0, **kw), reads, writes)
        return self.op(eng, lambda e: e.tensor_scalar(o, a, a1, a2, op0, op1, **kw), reads, writes)

    def stt(self, out, in0, scalar, in1, op0, op1, eng="dve"):
        reads = [in0, in1]
        s = scalar.ap if isinstance(scalar, View) else scalar
        if isinstance(scalar, View): reads.append(scalar)
        o, a, b = out.ap, in0.ap, in1.ap
        return self.op(eng, lambda e: e.scalar_tensor_tensor(o, a, s, b, op0, op1), reads, [out])

    def copy(self, out, in_, eng="dve"):
        o, i = out.ap, in_.ap
        if eng == "act":
            return self.op(eng, lambda e: e.copy(o, i), [in_], [out])
        return self.op(eng, lambda e: e.tensor_copy(o, i), [in_], [out])

    def memset(self, out, val, eng="dve"):
        o = out.ap
        return self.op(eng, lambda e: e.memset(o, val), [], [out])

    def reduce(self, out, in_, op, axis=AX.X, eng="dve"):
        o, i = out.ap, in_.ap
        return self.op(eng, lambda e: e.tensor_reduce(o, i, axis, op), [in_], [out])

    def recip(self, out, in_):
        o, i = out.ap, in_.ap
        return self.op("dve", lambda e: e.reciprocal(o, i), [in_], [out])

    def finish(self, out_bufs):
        for b in out_bufs:
            self.final_tokens.append(b.last_write)
        nc = self.nc
        with nc.Block() as block:
            def emit(eng_name):
                def body(e):
                    for waits, fn, inc, tag in self.ops[eng_name]:
                        for k, v in waits:
                            w = e.wait_ge(self.sems[k], v)
                            if tag:
                                w.annotate("W:" + tag + ":" + str(k))
                        ins = fn(e)
                        ins.then_inc(self.sems[inc[0]], inc[1])
                        if tag:
                            ins.annotate(tag)
                    if eng_name == "sp":
                        for tok in self.final_tokens:
                            e.wait_ge(self.sems[tok[0]], tok[1])
                return body
            block.tensor(emit("pe"))
            block.scalar(emit("act"))
            block.vector(emit("dve"))
            block.gpsimd(emit("pool"))
            block.sync(emit("sp"))
        return nc


from concourse.bass_utils import run_bass_kernel_spmd
import ml_dtypes

NPBF = ml_dtypes.bfloat16
D = 1024
NT = 2176
BLKS = [(0, 128, 1)] + [(128 + 512 * i, 512, 0) for i in range(4)]
ALPHA = 8.0 ** 0.25


def bc(view, shape, axis):
    return View(view.buf, view.ap.unsqueeze(axis).broadcast_to(list(shape)))


class PsPool:
    def __init__(self, P, n=8):
        self.t = [P.psum("psb%d" % i, [128, 512], F32) for i in range(n)]
        self.i = 0

    def get(self):
        b = self.t[self.i % len(self.t)]
        self.i += 1
        return b


def fm_stats(P, pp, xb, n, eps, ones32, sqb, mean, rstd, tmp):
    P.act(sqb[:, :, 0:n], xb, AF.Square)
    ps_s = pp.get(); ps_q = pp.get()
    for c in range(8):
        P.matmul(ps_s[:, 0:n], ones32.ap, xb[:, c, :], start=(c == 0), stop=(c == 7))
    for c in range(8):
        P.matmul(ps_q[:, 0:n], ones32.ap, sqb[:, c, 0:n], start=(c == 0), stop=(c == 7))
    P.ts(mean[:, 0:n], ps_s[:, 0:n], 1.0 / D, None, ALU.mult)
    P.tt(tmp[:, 0:n], mean[:, 0:n], mean[:, 0:n], ALU.mult)
    P.stt(tmp[:, 0:n], ps_q[:, 0:n], 1.0 / D, tmp[:, 0:n], ALU.mult, ALU.subtract)
    P.ts(tmp[:, 0:n], tmp[:, 0:n], eps, None, ALU.add)
    P.act(tmp[:, 0:n], tmp[:, 0:n], AF.Sqrt)
    P.recip(rstd[:, 0:n], tmp[:, 0:n])


def fm_norm_apply(P, xb, n, mean, rstd, s1, s2, out):
    P.tt(xb, xb, bc(mean[:, 0:n], [128, 8, n], 1), ALU.subtract)
    P.tt(xb, xb, bc(rstd[:, 0:n], [128, 8, n], 1), ALU.mult)
    for c in range(8):
        P.ts(out[:, c, :], xb[:, c, :], s1[:, c:c + 1], s2[:, c:c + 1], ALU.mult, ALU.add,
             eng="pool" if c % 2 else "dve")


def build_A():
    P = Prog()
    X = lambda n, s, dt=F32: P.dram(n, s, dt, kind="ExternalInput")
    xT = X("xT", [D, NT]); c2d = X("c2", [128, 8, 2]); ada_w = X("ada_w", [D, 6144])
    ada_bT = X("ada_bT", [128, 48]); w_in = X("w_in", [D, 7232])
    zT = P.dram("zT", [7232, NT], BF16, kind="ExternalOutput")
    modo = P.dram("mod", [128, 48, 2], F32, kind="ExternalOutput")
    pp = PsPool(P)
    ones32 = P.sbuf("ones32", [128, 128]); P.memset(ones32.ap, 1.0)
    c2 = P.sbuf("c2s", [128, 8, 2]); P.dma(c2.ap, c2d.ap); P.act(c2.ap, c2.ap, AF.Silu)
    bias = P.sbuf("adab", [128, 48]); P.dma(bias.ap, ada_bT.ap)
    mod = P.sbuf("modsb", [128, 48, 2])
    wA = [P.sbuf("wA%d" % i, [128, 8, 512]) for i in range(2)]
    awv = ada_w.ap.rearrange("(kc p) n -> p kc n", p=128)
    for g in range(12):
        w = wA[g % 2]
        for kc in range(8):
            P.dma(w[:, kc, :], awv[:, kc, g * 512:(g + 1) * 512], queue="sp" if kc % 2 else "pool", nowaw=True)
        ps = pp.get()
        for oc in range(4):
            for kc in range(8):
                P.matmul(ps[:, oc * 16:oc * 16 + 2], w[:, kc, oc * 128:(oc + 1) * 128], c2[:, kc, :],
                         start=(kc == 0), stop=(kc == 7))
        P.tt(mod[:, g * 4:(g + 1) * 4, :], ps[:, 0:64].rearrange("p (a b) -> p a b", b=16)[:, :, 0:2],
             bc(bias[:, g * 4:(g + 1) * 4], [128, 4, 2], 2), ALU.add)
    P.dma(modo.ap, mod.ap, queue="pool")
    opsc = P.sbuf("opsc", [128, 8, 2]); P.ts(opsc.ap, mod[:, 8:16, :], 1.0, None, ALU.add)
    wI = P.sbuf("wI", [128, 8, 7232], BF16)
    wiv = w_in.ap.rearrange("(kc p) n -> p kc n", p=128)
    for kc in range(8):
        P.dma(wI[:, kc, :], wiv[:, kc, :], queue="pool", nowaw=True)
    xv = xT.ap.rearrange("(c p) t -> p c t", p=128)
    xbs = [P.sbuf("xb%d" % i, [128, 8, 512]) for i in range(1)]
    sqb = P.sbuf("sqb", [128, 8, 512]); mean = P.sbuf("mean", [128, 512]); rstd = P.sbuf("rstd", [128, 512])
    tmp = P.sbuf("tmp", [128, 512])
    xms = [P.sbuf("xm%d" % i, [128, 8, 512], BF16) for i in range(1)]
    zst = [P.sbuf("zst%d" % i, [128, 512], BF16) for i in range(4)]
    k = 0
    for bi, (t0, n, col) in enumerate(BLKS):
        xb = xbs[0]; xm = xms[0]
        for c in range(8):
            P.dma(xb[:, c, 0:n], xv[:, c, t0:t0 + n], queue="sp", nowaw=True)
        fm_stats(P, pp, xb[:, :, 0:n], n, 1e-6, ones32, sqb, mean, rstd, tmp)
        fm_norm_apply(P, xb[:, :, 0:n], n, mean, rstd, opsc[:, :, col], mod[:, 0:8, col], xm[:, :, 0:n])
        for j in range(57):
            m = 128 if j < 56 else 64
            ps = pp.get()
            for kc in range(8):
                P.matmul(ps[0:m, 0:n], wI[:, kc, j * 128:j * 128 + m], xm[:, kc, 0:n], start=(kc == 0), stop=(kc == 7))
            st = zst[k % 4]; k += 1
            if j == 32:
                P.copy(st[0:64, 0:n], ps[0:64, 0:n], eng="dve")
                P.act(st[64:128, 0:n], ps[64:128, 0:n], AF.Sigmoid)
            elif j * 128 >= 4160:
                P.act(st[0:m, 0:n], ps[0:m, 0:n], AF.Sigmoid)
            elif k % 2:
                P.copy(st[0:m, 0:n], ps[0:m, 0:n], eng="act")
            else:
                P.copy(st[0:m, 0:n], ps[0:m, 0:n], eng="dve")
            P.dma(zT[j * 128:j * 128 + m, t0:t0 + n], st[0:m, 0:n], queue="sp", nowaw=True)
    return P.finish([zT, modo])


def build_C1():
    P = Prog()
    X = lambda n, s, dt=F32: P.dram(n, s, dt, kind="ExternalInput")
    xT = X("xT", [D, NT]); yT = X("yT", [1536, NT], BF16); gT = X("gT", [3072, NT], BF16)
    modd = X("mod", [128, 48, 2]); w_br = X("w_branch", [1536, D]); w_out = X("w_out", [D, D])
    lng = X("ln_g", [128, 8]); lnb = X("ln_b", [128, 8])
    oT = P.dram("x1T", [D, NT], F32, kind="ExternalOutput")
    pp = PsPool(P)
    ones32 = P.sbuf("ones32", [128, 128]); P.memset(ones32.ap, 1.0)
    mod = P.sbuf("modsb", [128, 48, 2]); P.dma(mod.ap, modd.ap)
    g1 = P.sbuf("lng", [128, 8]); b1 = P.sbuf("lnb", [128, 8]); P.dma(g1.ap, lng.ap); P.dma(b1.ap, lnb.ap)
    wb = P.sbuf("wb", [128, 12, D], BF16); wo = P.sbuf("wo", [128, 8, D], BF16)
    wbv = w_br.ap.rearrange("(kc p) n -> p kc n", p=128); wov = w_out.ap.rearrange("(kc p) n -> p kc n", p=128)
    for kc in range(12):
        P.dma(wb[:, kc, :], wbv[:, kc, :], queue="pool", nowaw=True)
    for kc in range(8):
        P.dma(wo[:, kc, :], wov[:, kc, :], queue="pool", nowaw=True)
    xv = xT.ap.rearrange("(c p) t -> p c t", p=128); yv = yT.ap.rearrange("(c p) t -> p c t", p=128)
    gv = gT.ap.rearrange("(c p) t -> p c t", p=128); ov = oT.ap.rearrange("(c p) t -> p c t", p=128)
    xb = P.sbuf("xb", [128, 8, 512]); yb = P.sbuf("yb", [128, 12, 512], BF16); gb = P.sbuf("gb", [128, 24, 512], BF16)
    mb = P.sbuf("mb", [128, 8, 512], BF16); t1 = P.sbuf("t1", [128, 512]); t2 = P.sbuf("t2", [128, 512])
    sqb = P.sbuf("sqb", [128, 8, 512]); mean = P.sbuf("mean", [128, 512]); rstd = P.sbuf("rstd", [128, 512])
    tmp = P.sbuf("tmp", [128, 512]); ob = P.sbuf("ob", [128, 8, 512])
    for bi, (t0, n, col) in enumerate(BLKS):
        for c in range(8):
            P.dma(xb[:, c, 0:n], xv[:, c, t0:t0 + n], queue="sp", nowaw=True)
        for c in range(12):
            P.dma(yb[:, c, 0:n], yv[:, c, t0:t0 + n], queue="sp", nowaw=True)
        for c in range(24):
            P.dma(gb[:, c, 0:n], gv[:, c, t0:t0 + n], queue="pool", nowaw=True)
        for oc in range(8):
            for nb in range(3):
                ps = pp.get()
                for kc in range(4):
                    P.matmul(ps[:, 0:n], wb[:, nb * 4 + kc, oc * 128:(oc + 1) * 128], yb[:, nb * 4 + kc, 0:n],
                             start=(kc == 0), stop=(kc == 3))
                dst = t1 if nb == 0 else t2
                P.tt(dst[:, 0:n], ps[:, 0:n], gb[:, nb * 8 + oc, 0:n], ALU.mult)
                if nb == 1:
                    P.tt(t1[:, 0:n], t1[:, 0:n], t2[:, 0:n], ALU.add, eng="pool")
                if nb == 2:
                    P.tt(mb[:, oc, 0:n], t1[:, 0:n], t2[:, 0:n], ALU.add, eng="pool")
        for oc in range(8):
            ps = pp.get()
            for kc in range(8):
                P.matmul(ps[:, 0:n], wo[:, kc, oc * 128:(oc + 1) * 128], mb[:, kc, 0:n], start=(kc == 0), stop=(kc == 7))
            P.ts(t1[:, 0:n], ps[:, 0:n], mod[:, 16 + oc, col:col + 1], None, ALU.mult)
            P.stt(xb[:, oc, 0:n], xb[:, oc, 0:n], ALPHA, t1[:, 0:n], ALU.mult, ALU.add)
        fm_stats(P, pp, xb[:, :, 0:n], n, 1e-5, ones32, sqb, mean, rstd, tmp)
        fm_norm_apply(P, xb[:, :, 0:n], n, mean, rstd, g1.ap, b1.ap, ob[:, :, 0:n])
        for c in range(8):
            P.dma(ov[:, c, t0:t0 + n], ob[:, c, 0:n], queue="sp", nowaw=True)
    return P.finish([oT])


def build_C2():
    P = Prog()
    X = lambda n, s, dt=F32: P.dram(n, s, dt, kind="ExternalInput")
    xT = X("xT", [D, NT]); modd = X("mod", [128, 48, 2]); w1d = X("w1", [D, 4096]); w2d = X("w2", [4096, D])
    lng = X("ln_g", [128, 8]); lnb = X("ln_b", [128, 8])
    oT = P.dram("x2T", [D, NT], F32, kind="ExternalOutput")
    pp = PsPool(P)
    ones32 = P.sbuf("ones32", [128, 128]); P.memset(ones32.ap, 1.0)
    mod = P.sbuf("modsb", [128, 48, 2]); P.dma(mod.ap, modd.ap)
    opsc = P.sbuf("opsc", [128, 8, 2]); P.ts(opsc.ap, mod[:, 32:40, :], 1.0, None, ALU.add)
    g1 = P.sbuf("lng", [128, 8]); b1 = P.sbuf("lnb", [128, 8]); P.dma(g1.ap, lng.ap); P.dma(b1.ap, lnb.ap)
    w1 = P.sbuf("w1s", [128, 8, 4096], BF16); w2 = P.sbuf("w2s", [128, 32, D], BF16)
    w1v = w1d.ap.rearrange("(kc p) n -> p kc n", p=128); w2v = w2d.ap.rearrange("(kc p) n -> p kc n", p=128)
    for kc in range(8):
        P.dma(w1[:, kc, :], w1v[:, kc, :], queue="pool", nowaw=True)
    for kc in range(32):
        P.dma(w2[:, kc, :], w2v[:, kc, :], queue="pool", nowaw=True)
    xv = xT.ap.rearrange("(c p) t -> p c t", p=128); ov = oT.ap.rearrange("(c p) t -> p c t", p=128)
    NB = 256
    xb = P.sbuf("xb", [128, 8, NB]); xr = P.sbuf("xr", [128, 8, NB]); xm = P.sbuf("xm", [128, 8, NB], BF16)
    hb = P.sbuf("hb", [128, 32, NB], BF16); hr = P.sbuf("hr", [128, NB]); t1 = P.sbuf("t1", [128, NB])
    sqb = P.sbuf("sqb", [128, 8, NB]); mean = P.sbuf("mean", [128, NB]); rstd = P.sbuf("rstd", [128, NB])
    tmp = P.sbuf("tmp", [128, NB]); ob = P.sbuf("ob", [128, 8, NB])
    blks = []
    for (t0, n, col) in BLKS:
        for s in range(0, n, NB):
            blks.append((t0 + s, min(NB, n - s), col))
    for bi, (t0, n, col) in enumerate(blks):
        for c in range(8):
            P.dma(xb[:, c, 0:n], xv[:, c, t0:t0 + n], queue="sp", nowaw=True)
        P.copy(xr[:, :, 0:n], xb[:, :, 0:n], eng="pool")
        fm_stats(P, pp, xb[:, :, 0:n], n, 1e-6, ones32, sqb, mean, rstd, tmp)
        fm_norm_apply(P, xb[:, :, 0:n], n, mean, rstd, opsc[:, :, col], mod[:, 24:32, col], xm[:, :, 0:n])
        for fc in range(32):
            ps = pp.get()
            for kc in range(8):
                P.matmul(ps[:, 0:n], w1[:, kc, fc * 128:(fc + 1) * 128], xm[:, kc, 0:n], start=(kc == 0), stop=(kc == 7))
            P.act(hr[:, 0:n], ps[:, 0:n], AF.Relu)
            P.tt(hb[:, fc, 0:n], hr[:, 0:n], hr[:, 0:n], ALU.mult)
        for oc in range(8):
            ps = pp.get()
            for fc in range(32):
                P.matmul(ps[:, 0:n], w2[:, fc, oc * 128:(oc + 1) * 128], hb[:, fc, 0:n], start=(fc == 0), stop=(fc == 31))
            P.ts(t1[:, 0:n], ps[:, 0:n], mod[:, 40 + oc, col:col + 1], None, ALU.mult)
            P.stt(xr[:, oc, 0:n], xr[:, oc, 0:n], ALPHA, t1[:, 0:n], ALU.mult, ALU.add)
        fm_stats(P, pp, xr[:, :, 0:n], n, 1e-5, ones32, sqb, mean, rstd, tmp)
        fm_norm_apply(P, xr[:, :, 0:n], n, mean, rstd, g1.ap, b1.ap, ob[:, :, 0:n])
        for c in range(8):
            P.dma(ov[:, c, t0:t0 + n], ob[:, c, 0:n], queue="sp", nowaw=True)
    return P.finish([oT])


TT = 4352
QBLK = [(0, 256, 2)] + [(256 + 512 * i, 512, 34) for i in range(8)]


def build_B1():
    P = Prog()
    X = lambda n, s, dt=F32: P.dram(n, s, dt, kind="ExternalInput")
    qdT = X("qdT", [384, TT], BF16); kvdT = X("kvdT", [256, TT], BF16)
    krT = X("krT", [32, TT], BF16); krrT = X("krrT", [32, TT], BF16)
    wq = X("wq", [384, 384]); wqr = X("wqr", [384, 384]); wkn = X("wkn", [256, 256]); wkv = X("wkv", [256, 256])
    qg = X("qg", [128, 3]); kg = X("kg", [128, 2]); Cd = X("ropeC", [96, TT]); Sd = X("ropeS", [96, TT])
    yo = P.dram("ybT", [256, TT], BF16, kind="ExternalOutput")
    pss = [P.psum("pS%d" % i, [128, 512]) for i in range(6)]
    pso = [P.psum("pO%d" % i, [128, 512]) for i in range(2)]
    psi = [0]
    def gps():
        psi[0] += 1
        return pss[psi[0] % 6]
    ones16 = P.sbuf("ones16", [128, 128], BF16); P.memset(ones16.ap, 1.0)
    qgs = P.sbuf("qgs", [128, 3]); kgs = P.sbuf("kgs", [128, 2]); P.dma(qgs.ap, qg.ap); P.dma(kgs.ap, kg.ap)
    Ct = P.sbuf("Ct", [96, TT]); St = P.sbuf("St", [96, TT]); P.dma(Ct.ap, Cd.ap); P.dma(St.ap, Sd.ap, queue="pool")
    qd = P.sbuf("qd", [128, 3, TT], BF16); kvd = P.sbuf("kvd", [128, 2, TT], BF16)
    for c in range(3):
        P.dma(qd[:, c, :], qdT[c * 128:(c + 1) * 128, :], nowaw=True)
    for c in range(2):
        P.dma(kvd[:, c, :], kvdT[c * 128:(c + 1) * 128, :], queue="pool", nowaw=True)
    kr = P.sbuf("kr", [96, TT], BF16); krr = P.sbuf("krr", [96, TT], BF16)
    P.dma(kr[64:96, :], krT.ap); P.dma(krr[64:96, :], krrT.ap, queue="pool")
    wqs = P.sbuf("wqs", [128, 3, 384], BF16); wqrs = P.sbuf("wqrs", [128, 3, 384], BF16)
    wkns = P.sbuf("wkns", [128, 2, 256], BF16); wkvs = P.sbuf("wkvs", [128, 2, 256], BF16)
    P.dma(wqs.ap, wq.ap.rearrange("(c p) n -> p c n", p=128), queue="pool")
    P.dma(wqrs.ap, wqr.ap.rearrange("(c p) n -> p c n", p=128), queue="pool")
    P.dma(wkns.ap, wkn.ap.rearrange("(c p) n -> p c n", p=128), queue="pool")
    P.dma(wkvs.ap, wkv.ap.rearrange("(c p) n -> p c n", p=128), queue="pool")
    rq = P.sbuf("rq", [128, TT]); rk = P.sbuf("rk", [128, TT]); rkc = P.sbuf("rkc", [128, 34])
    sq = P.sbuf("sq", [128, 3, 512], BF16)
    for (t0, n, _) in QBLK:
        for (src, nc_, dst, dim) in ((qd, 3, rq, 384.0), (kvd, 2, rk, 256.0)):
            P.tt(sq[:, 0:nc_, 0:n], src[:, 0:nc_, t0:t0 + n], src[:, 0:nc_, t0:t0 + n], ALU.mult)
            ps = gps()
            for c in range(nc_):
                P.matmul(ps[:, 0:n], ones16.ap, sq[:, c, 0:n], start=(c == 0), stop=(c == nc_ - 1))
            P.ts(dst[:, t0:t0 + n], ps[:, 0:n], 1.0 / dim, 1e-6, ALU.mult, ALU.add)
            P.act(dst[:, t0:t0 + n], dst[:, t0:t0 + n], AF.Sqrt)
            P.recip(dst[:, t0:t0 + n], dst[:, t0:t0 + n])
            if src is kvd:
                for tl in range(n // 128):
                    kt = t0 // 128 + tl
                    ps2 = gps()
                    for c in range(2):
                        P.matmul(ps2[:, 0:1], sq[:, c, tl * 128:(tl + 1) * 128], ones16[:, 0:1], start=(c == 0), stop=(c == 1))
                    P.ts(rkc[:, kt:kt + 1], ps2[:, 0:1], 1.0 / 256.0, 1e-6, ALU.mult, ALU.add)
    P.act(rkc.ap, rkc.ap, AF.Sqrt); P.recip(rkc.ap, rkc.ap)
    P.ts(rq.ap, rq.ap, 96.0 ** -0.5, None, ALU.mult)
    for c in range(3):
        P.ts(qd[:, c, :], qd[:, c, :], qgs[:, c:c + 1], None, ALU.mult, eng="pool")
    for c in range(2):
        P.ts(kvd[:, c, :], kvd[:, c, :], kgs[:, c:c + 1], None, ALU.mult, eng="pool")
    Rk = P.sbuf("Rk", [96, TT], BF16); tk = P.sbuf("tk", [96, TT])
    P.tt(tk[64:96, :], kr[64:96, :], Ct[64:96, :], ALU.mult)
    P.tt(kr[64:96, :], krr[64:96, :], St[64:96, :], ALU.mult)
    P.tt(Rk[64:96, :], tk[64:96, :], kr[64:96, :], ALU.add)
    Qh = P.sbuf("Qh", [96, TT], BF16); Kh = P.sbuf("Kh", [96, TT], BF16); Vh = P.sbuf("Vh", [128, 34, 128], BF16)
    P.memset(Vh.ap, 1.0)
    t1 = P.sbuf("t1", [96, 512]); t2 = P.sbuf("t2", [96, 512])
    es = [P.sbuf("es%d" % i, [128, 512], BF16) for i in range(6)]
    den = P.sbuf("den", [64, 512]); ost = [P.sbuf("ost%d" % i, [64, 512], BF16) for i in range(2)]
    ei = 0
    for h in range(4):
        for (t0, n, _) in QBLK:
            ps1 = gps(); ps2 = gps()
            for c in range(3):
                P.matmul(ps1[0:96, 0:n], wqs[:, c, h * 96:(h + 1) * 96], qd[:, c, t0:t0 + n], start=(c == 0), stop=(c == 2))
            for c in range(3):
                P.matmul(ps2[0:96, 0:n], wqrs[:, c, h * 96:(h + 1) * 96], qd[:, c, t0:t0 + n], start=(c == 0), stop=(c == 2))
            P.tt(t1[:, 0:n], ps1[0:96, 0:n], Ct[:, t0:t0 + n], ALU.mult)
            P.tt(t2[:, 0:n], ps2[0:96, 0:n], St[:, t0:t0 + n], ALU.mult)
            P.tt(t1[:, 0:n], t1[:, 0:n], t2[:, 0:n], ALU.add, eng="pool")
            P.tt(Qh[:, t0:t0 + n], t1[:, 0:n], rq[0:96, t0:t0 + n], ALU.mult)
            ps3 = gps()
            for c in range(2):
                P.matmul(ps3[0:64, 0:n], wkns[:, c, h * 64:(h + 1) * 64], kvd[:, c, t0:t0 + n], start=(c == 0), stop=(c == 1))
            P.tt(Kh[0:64, t0:t0 + n], ps3[0:64, 0:n], rk[0:64, t0:t0 + n], ALU.mult)
            P.copy(Kh[64:96, t0:t0 + n], Rk[64:96, t0:t0 + n], eng="pool")
        for kt in range(34):
            ps4 = gps()
            for c in range(2):
                P.matmul(ps4[:, 0:64], kvd[:, c, kt * 128:(kt + 1) * 128], wkvs[:, c, h * 64:(h + 1) * 64], start=(c == 0), stop=(c == 1))
            P.ts(Vh[:, kt, 0:64], ps4[:, 0:64], rkc[:, kt:kt + 1], None, ALU.mult)
        for qi, (q0, nq, nk) in enumerate(QBLK):
            po = pso[qi % 2]
            pend = []
            for kt in range(nk):
                ps = gps()
                P.matmul(ps[:, 0:nq], Kh[:, kt * 128:(kt + 1) * 128], Qh[:, q0:q0 + nq])
                e = es[ei % 6]; ei += 1
                P.act(e[:, 0:nq], ps[:, 0:nq], AF.Exp)
                pend.append((kt, e))
                if len(pend) > 2:
                    k0, e0 = pend.pop(0)
                    P.matmul(po[:, 0:nq], Vh[:, k0, :], e0[:, 0:nq], start=(k0 == 0), stop=(k0 == nk - 1))
            for (k0, e0) in pend:
                P.matmul(po[:, 0:nq], Vh[:, k0, :], e0[:, 0:nq], start=(k0 == 0), stop=(k0 == nk - 1))
            P.copy(den[:, 0:nq], po[64:128, 0:nq])
            P.recip(den[:, 0:nq], den[:, 0:nq])
            o = ost[qi % 2]
            P.tt(o[:, 0:nq], po[0:64, 0:nq], den[:, 0:nq], ALU.mult)
            P.dma(yo[h * 64:(h + 1) * 64, q0:q0 + nq], o[:, 0:nq], queue="sp", nowaw=True)
    return P.finish([yo])


def build_B2():
    P = Prog()
    X = lambda n, s, dt=F32: P.dram(n, s, dt, kind="ExternalInput")
    qT = X("qT", [128, TT], BF16); kT = X("kT", [128, TT], BF16); kM = X("kM", [TT, 128], BF16)
    vM = X("vM", [TT, 256], BF16); ogM = X("ogM", [TT, 256], BF16)
    gdT = [X("gdT%d" % i, [16, TT], BF16) for i in range(2)]
    gup = [X("gup%d" % i, [16, 128]) for i in range(2)]; gkb = [X("gkb%d" % i, [1, 128]) for i in range(2)]
    ngd = X("ng", [1, 128]); mk = X("masks", [128, 4, 128])
    yo = P.dram("ya", [TT, 256], F32, kind="ExternalOutput")
    pp = PsPool(P)
    masks = P.sbuf("masksb", [128, 4, 128]); P.dma(masks.ap, mk.ap)
    m16 = P.sbuf("m16", [128, 2, 128], BF16); P.copy(m16.ap, masks[:, 0:2, :])
    ones1 = P.sbuf("ones1", [1, 128], BF16); P.memset(ones1.ap, 1.0)
    q = P.sbuf("q", [128, TT], BF16); k = P.sbuf("k", [128, TT], BF16)
    P.dma(q.ap, qT.ap); P.dma(k.ap, kT.ap, queue="pool")
    km = P.sbuf("km", [128, 34, 128], BF16); vm = P.sbuf("vm", [128, 34, 256], BF16)
    P.dma(km.ap, kM.ap.rearrange("(n p) c -> p n c", p=128)); P.dma(vm.ap, vM.ap.rearrange("(n p) c -> p n c", p=128), queue="pool")
    gd = [P.sbuf("gd%d" % i, [16, TT], BF16) for i in range(2)]
    gu = [P.sbuf("gu%d" % i, [16, 128], BF16) for i in range(2)]; gb = [P.sbuf("gb%d" % i, [1, 128], BF16) for i in range(2)]
    for i in range(2):
        P.dma(gd[i].ap, gdT[i].ap); P.dma(gu[i].ap, gup[i].ap, queue="pool"); P.dma(gb[i].ap, gkb[i].ap, queue="pool")
    ng = P.sbuf("ngsb", [128, 128]); P.dma(ng.ap, View(ngd, ngd.t.ap()[0, :].partition_broadcast(128)))
    oacc = P.sbuf("oacc", [128, 34, 256])
    S32 = P.sbuf("S32", [128, 2, 128]); S16 = P.sbuf("S16", [128, 2, 128], BF16)
    sp = P.sbuf("sp", [128, 128]); E1 = P.sbuf("E1", [128, 128]); E2 = P.sbuf("E2", [128, 128]); E3 = P.sbuf("E3", [128, 128])
    dec = P.sbuf("dec", [128, 1]); qin = P.sbuf("qin", [128, 128], BF16); kin = P.sbuf("kin", [128, 128], BF16)
    kst = P.sbuf("kst", [128, 128], BF16); attm = [P.sbuf("attm%d" % i, [128, 128], BF16) for i in range(2)]
    for dr in range(2):
        order = list(range(34)) if dr == 0 else [1, 0] + list(range(33, 1, -1))
        last = 127 if dr == 0 else 0
        P.memset(S32[:, 0, :], 0.0); P.memset(S16[:, 0, :], 0.0)
        Sf = S32[:, 0, :]; Sb = S16[:, 0, :]
        for ti in order:
            tsl = slice(ti * 128, (ti + 1) * 128)
            ps = pp.get()
            P.matmul(ps[:, 0:128], gd[dr][:, tsl], gu[dr].ap, start=True, stop=False)
            P.matmul(ps[:, 0:128], ones1.ap, gb[dr].ap, start=False, stop=True)
            P.act(sp.ap, ps[:, 0:128], AF.Exp, scale=-1.0)
            P.act(sp.ap, sp.ap, AF.Ln, bias=1.0)
            pb = pp.get(); pf = pp.get()
            P.matmul(pb[:, 0:128], sp.ap, masks[:, dr, :])
            P.matmul(pf[:, 0:128], masks[:, 2 + dr, :], sp.ap)
            P.act(E1.ap, pb[:, 0:128], AF.Exp, scale=-1.0 / 16.0)
            P.act(E2.ap, pb[:, 0:128], AF.Exp, scale=1.0 / 16.0)
            P.act(E3.ap, pf[:, 0:128], AF.Exp, scale=-1.0 / 16.0)
            P.copy(dec.ap, E1[:, last:last + 1])
            P.stt(qin.ap, q[:, tsl], 0.125, E1.ap, ALU.mult, ALU.mult)
            P.tt(kin.ap, k[:, tsl], E2.ap, ALU.mult, eng="pool")
            P.tt(kst.ap, km[:, ti, :], E3.ap, ALU.mult, eng="pool")
            for hh in range(2):
                hs = slice(hh * 64, (hh + 1) * 64); vs = slice(hh * 128, (hh + 1) * 128)
                pa = pp.get()
                P.matmul(pa[:, 0:128], kin[hs, :], qin[hs, :])
                am = attm[hh]
                P.tt(am.ap, pa[:, 0:128], m16[:, dr, :], ALU.mult)
                po = pp.get()
                P.matmul(po[:, 0:128], am.ap, vm[:, ti, vs], start=True, stop=False)
                P.matmul(po[:, 0:128], qin[hs, :], Sb[hs, :], start=False, stop=True)
                if dr == 0:
                    P.copy(oacc[:, ti, vs], po[:, 0:128], eng="act")
                else:
                    P.tt(oacc[:, ti, vs], oacc[:, ti, vs], po[:, 0:128], ALU.add)
                pu = pp.get()
                P.matmul(pu[:, 0:128], kst.ap, vm[:, ti, vs])
                P.stt(Sf[hs, :], Sf[hs, :], dec[hs, 0:1], pu[hs, 0:128], ALU.mult, ALU.add)
                P.copy(Sb[hs, :], Sf[hs, :], eng="act")
    og = P.sbuf("og", [128, 256], BF16); sg = P.sbuf("sg", [128, 256]); sq = P.sbuf("sq2", [128, 256])
    ss = P.sbuf("ss", [128, 2]); yt = [P.sbuf("yt%d" % i, [128, 256]) for i in range(2)]
    for ti in range(34):
        P.dma(og.ap, ogM[ti * 128:(ti + 1) * 128, :])
        P.act(sg.ap, og.ap, AF.Silu)
        o = oacc[:, ti, :]
        P.tt(sq.ap, o, o, ALU.mult)
        P.reduce(ss.ap, sq.ap.rearrange("p (h v) -> p h v", h=2), ALU.add)
        P.ts(ss.ap, ss.ap, 1.0 / 128.0, 1e-6, ALU.mult, ALU.add)
        P.act(ss.ap, ss.ap, AF.Sqrt); P.recip(ss.ap, ss.ap)
        y = yt[ti % 2]
        P.tt(y.ap.rearrange("p (h v) -> p h v", h=2), o.rearrange("p (h v) -> p h v", h=2), bc(ss.ap, [128, 2, 128], 2), ALU.mult)
        P.tt(y.ap.rearrange("p (h v) -> p h v", h=2), y.ap.rearrange("p (h v) -> p h v", h=2), bc(ng.ap, [128, 2, 128], 1), ALU.mult, eng="pool")
        P.tt(y.ap, y.ap, sg.ap, ALU.mult)
        P.dma(yo[ti * 128:(ti + 1) * 128, :], y.ap, queue="pool", nowaw=True)
    return P.finish([yo])


CW = 0.6065306597126334


def build_B3():
    P = Prog()
    X = lambda n, s, dt=F32: P.dram(n, s, dt, kind="ExternalInput")
    zrkv = X("zrkv", [TT, 3, 768], BF16)
    lrd = X("lr", [3, 3, 128, TT], BF16)
    murkv = X("murkv", [2, 768]); mulrd = X("mulr", [128, 3, 2])
    wupd = X("wup", [128, 256]); w0d = X("w0", [2, 1, 256]); aupd = X("aup", [128, 256]); a0d = X("a0", [2, 1, 256])
    gupd = X("gup", [128, 256]); vecsd = X("vecs", [5, 256]); mk = X("masks", [128, 4, 128]); idd = X("ident", [128, 128])
    yo = P.dram("yr", [TT, 256], F32, kind="ExternalOutput")
    pp = PsPool(P)
    masks = P.sbuf("masksb", [128, 4, 128]); P.dma(masks.ap, mk.ap)
    id32 = P.sbuf("id32", [128, 128]); P.dma(id32.ap, idd.ap)
    id16 = P.sbuf("id16", [128, 128], BF16); P.copy(id16.ap, id32.ap)
    ones1 = P.sbuf("ones1", [1, 128], BF16); P.memset(ones1.ap, 1.0)
    onec = P.sbuf("onec", [128, 1]); P.memset(onec.ap, 1.0)
    M_I = (2, 3); QS_I = (3, 2); QN_I = (0, 1); INC_I = (0, 1); EXC_I = (3, 2); SUF_I = (2, 3)
    maskM = P.sbuf("maskM", [128, 2, 128], BF16); maskQN = P.sbuf("maskQN", [128, 2, 256], BF16)
    for d in range(2):
        P.copy(maskM[:, d, :], masks[:, M_I[d], :])
        P.copy(maskQN[:, d, 0:128], masks[:, QS_I[d], :]); P.copy(maskQN[:, d, 128:256], masks[:, QN_I[d], :])
    murep = P.sbuf("murep", [128, 2, 768])
    P.dma(murep.ap, View(murkv, murkv.t.ap().partition_broadcast(128)))
    c0rep = P.sbuf("c0rep", [128, 768])
    P.tt(c0rep.ap, murep[:, 0, :], murep[:, 1, :], ALU.add)
    P.ts(c0rep.ap, c0rep.ap, -1.0, 1.0, ALU.mult, ALU.add)
    vrep = P.sbuf("vrep", [128, 5, 256]); P.dma(vrep.ap, View(vecsd, vecsd.t.ap().partition_broadcast(128)), queue="pool")
    wup = P.sbuf("wup16", [128, 256], BF16); aup = P.sbuf("aup16", [128, 256], BF16)
    w0 = P.sbuf("w016", [1, 2, 256], BF16); a0 = P.sbuf("a016", [1, 2, 256], BF16); gup = P.sbuf("gup16", [128, 256], BF16)
    P.dma(wup.ap, wupd.ap, queue="pool"); P.dma(aup.ap, aupd.ap, queue="pool")
    P.dma(w0.ap, w0d.ap.rearrange("d r c -> r d c"), queue="pool"); P.dma(a0.ap, a0d.ap.rearrange("d r c -> r d c"), queue="pool")
    P.dma(gup.ap, gupd.ap, queue="pool")
    P.tag = "lowrank"
    mulr = P.sbuf("mulrs", [128, 3, 2]); P.dma(mulr.ap, mulrd.ap)
    clr = P.sbuf("clr", [128, 3])
    P.tt(clr.ap, mulr[:, :, 0], mulr[:, :, 1], ALU.add); P.ts(clr.ap, clr.ap, -1.0, 1.0, ALU.mult, ALU.add)
    st = [P.sbuf("lrst%d" % i, [128, 544], BF16) for i in range(6)]
    LRs = [P.sbuf("LRs%d" % i, [128, TT], BF16) for i in range(3)]
    LRg = LRs[2]
    tl = P.sbuf("lrt", [128, 544])
    kk_ = 0
    for i in range(3):
        fn = (AF.Tanh, AF.Identity, AF.Sigmoid)[i]
        for b0 in range(0, TT, 544):
            bs = slice(b0, b0 + 544)
            sb = [st[(kk_ % 2) * 3 + j] for j in range(3)]; kk_ += 1
            for j in range(3):
                P.dma(sb[j].ap, lrd[i, j][:, bs], queue="sp" if j % 2 else "pool")
            P.ts(tl.ap, sb[0].ap, clr[:, i:i + 1], None, ALU.mult)
            P.stt(tl.ap, sb[1].ap, mulr[:, i, 0:1], tl.ap, ALU.mult, ALU.add)
            P.stt(tl.ap, sb[2].ap, mulr[:, i, 1:2], tl.ap, ALU.mult, ALU.add)
            P.act(LRs[i][:, bs], tl.ap, fn)
    yacc = P.sbuf("yacc", [128, 34, 256]); racc = P.sbuf("racc", [128, 34, 4])
    P.memset(yacc.ap, 0.0, eng="pool"); P.memset(racc.ap, 0.0, eng="pool")
    S16 = P.sbuf("S16", [64, 8, 64], BF16); P.memset(S16.ap, 0.0)
    zb = [P.sbuf("zb%d" % i, [128, 3, 768], BF16) for i in range(2)]
    zv = zrkv.ap.rearrange("(n p) j c -> n p j c", p=128)

    def mk_tmp(tag):
        f = lambda nm, sh, dt=F32: P.sbuf(nm + tag, sh, dt)
        return dict(m1=f("m1", [128, 768]), m2=f("m2", [128, 768]),
                    sw=f("sw", [128, 256]), aa=f("aa", [128, 256]), kk=f("kk", [128, 256]), t1=f("t1", [128, 256]),
                    keff=f("keff", [128, 256]), bb=f("bb", [128, 256]), ss=f("ss", [128, 4]),
                    EA=f("EA", [128, 256]), ER=f("ER", [128, 256]), EI=f("EI", [128, 256]), ES=f("ES", [128, 256]),
                    Ap16=f("Ap16", [128, 256], BF16), Bm16=f("Bm16", [128, 256], BF16), Km16=f("Km16", [128, 256], BF16))

    def mk_out(tag):
        f = lambda nm, sh, dt=F32: P.sbuf(nm + tag, sh, dt)
        return dict(V16=f("V16", [128, 256], BF16), Ap32=f("Ap32", [128, 256]), Rp16=f("Rp16", [128, 256], BF16),
                    Bc16=f("Bc16", [128, 256], BF16), Kc16=f("Kc16", [128, 256], BF16), Wc=f("Wc", [64, 4]),
                    FTAR=f("FTAR", [128, 2, 256], BF16), FTBK=f("FTBK", [128, 2, 256], BF16))
    PBt = {d: mk_tmp("_%d" % d) for d in range(2)}
    PBo = {(d, par): mk_out("_%d%d" % (d, par)) for d in range(2) for par in range(2)}
    PB = {(d, par): {**PBt[d], **PBo[(d, par)]} for d in range(2) for par in range(2)}

    def prep_gen(d, ti, par):
        T_ = PB[(d, par)]
        P.tag = "prep0"
        zin = zb[(d + par) % 2]
        P.dma(zin.ap, zv[ti])
        m1, m2 = T_["m1"], T_["m2"]
        P.tt(m1.ap, zin[:, 0, :], c0rep.ap, ALU.mult, eng="pool")
        P.tt(m2.ap, zin[:, 1, :], murep[:, 0, :], ALU.mult, eng="pool")
        P.tt(m1.ap, m1.ap, m2.ap, ALU.add, eng="pool")
        P.tt(m2.ap, zin[:, 2, :], murep[:, 1, :], ALU.mult, eng="pool")
        P.tt(m1.ap, m1.ap, m2.ap, ALU.add, eng="pool")
        r = m1[:, 0:256]; k = m1[:, 256:512]; v = m1[:, 512:768]
        P.copy(T_["V16"].ap, v, eng="pool")
        tsl = slice(ti * 128, (ti + 1) * 128)
        pw = pp.get()
        ds = slice(64 * d, 64 * d + 64)
        P.matmul(pw[:, 0:256], LRs[0][ds, tsl], wup[ds, :], start=True, stop=False)
        P.matmul(pw[:, 0:256], ones1.ap, w0[:, d, :], start=False, stop=True)
        P.act(T_["sw"].ap, pw[:, 0:256], AF.Sigmoid)
        pa = pp.get()
        P.matmul(pa[:, 0:256], LRs[1][ds, tsl], aup[ds, :], start=True, stop=False)
        P.matmul(pa[:, 0:256], ones1.ap, a0[:, d, :], start=False, stop=True)
        P.act(T_["aa"].ap, pa[:, 0:256], AF.Sigmoid)
        yield
        P.tag = "prep1"
        sw = T_["sw"]
        pci = pp.get(); pce = pp.get(); pcs = pp.get()
        P.matmul(pci[:, 0:256], masks[:, INC_I[d], :], sw.ap)
        P.matmul(pce[:, 0:256], masks[:, EXC_I[d], :], sw.ap)
        P.matmul(pcs[:, 0:256], masks[:, SUF_I[d], :], sw.ap)
        P.act(T_["EA"].ap, pce[:, 0:256], AF.Exp, scale=-CW)
        P.act(T_["ER"].ap, pci[:, 0:256], AF.Exp, scale=-CW)
        P.act(T_["EI"].ap, pci[:, 0:256], AF.Exp, scale=CW)
        P.act(T_["ES"].ap, pcs[:, 0:256], AF.Exp, scale=-CW)
        pwc = pp.get()
        for h in range(4):
            P.matmul(pwc[0:64, h * 8:h * 8 + 1], sw[:, h * 64:(h + 1) * 64], onec.ap)
        P.act(T_["Wc"].ap, pwc[0:64, 0:32].rearrange("p (h e) -> p h e", e=8)[:, :, 0], AF.Exp, scale=-CW)
        kk, t1, ss = T_["kk"], T_["t1"], T_["ss"]
        P.tt(kk.ap, k, vrep[:, 0, :], ALU.mult)
        P.tt(t1.ap, kk.ap, kk.ap, ALU.mult)
        P.reduce(ss.ap, t1.ap.rearrange("p (h c) -> p h c", h=4), ALU.add)
        P.ts(ss.ap, ss.ap, 1e-12, None, ALU.add)
        P.act(ss.ap, ss.ap, AF.Sqrt)
        yield
        P.tag = "prep2"
        P.recip(ss.ap, ss.ap)
        P.tt(kk.ap.rearrange("p (h c) -> p h c", h=4), kk.ap.rearrange("p (h c) -> p h c", h=4), bc(ss.ap, [128, 4, 64], 2), ALU.mult)
        keff, bb = T_["keff"], T_["bb"]
        P.stt(t1.ap, T_["aa"].ap, -1.0, vrep[:, 1, :], ALU.add, ALU.mult)
        P.stt(keff.ap, t1.ap, 1.0, k, ALU.add, ALU.mult)
        P.tt(bb.ap, kk.ap, T_["aa"].ap, ALU.mult)
        yield
        P.tag = "prep3"
        P.stt(T_["Ap32"].ap, kk.ap, -1.0, T_["EA"].ap, ALU.mult, ALU.mult)
        P.copy(T_["Ap16"].ap, T_["Ap32"].ap, eng="pool")
        P.tt(T_["Rp16"].ap, r, T_["ER"].ap, ALU.mult)
        P.tt(T_["Bm16"].ap, bb.ap, T_["EI"].ap, ALU.mult, eng="pool")
        P.tt(T_["Km16"].ap, keff.ap, T_["EI"].ap, ALU.mult)
        yield
        P.tag = "prep4"
        P.tt(T_["Bc16"].ap, bb.ap, T_["ES"].ap, ALU.mult, eng="pool")
        P.tt(T_["Kc16"].ap, keff.ap, T_["ES"].ap, ALU.mult)
        P.tt(t1.ap, r, keff.ap, ALU.mult)
        P.tt(t1.ap, t1.ap, vrep[:, 2, :], ALU.mult)
        P.reduce(ss.ap, t1.ap.rearrange("p (h c) -> p h c", h=4), ALU.add)
        P.tt(racc[:, ti, :], racc[:, ti, :], ss.ap, ALU.add)
        yield
        P.tag = "prep5"
        for (dstn, srcs) in (("FTAR", ("Ap16", "Rp16")), ("FTBK", ("Bm16", "Km16"))):
            pt = pp.get(); ptv = pt.ap.bitcast(BF16)
            for pair in range(2):
                for qi in range(2):
                    c0 = (pair * 2 + qi) * 128
                    P.transpose(ptv[:, c0:c0 + 128], T_[srcs[qi]][:, pair * 128:(pair + 1) * 128], id16.ap)
            P.copy(T_[dstn].ap.rearrange("p a c -> p (a c)"), ptv[:, 0:512], eng="act")

    UB = []
    for u in range(8):
        f = lambda nm, sh, dt=F32: P.sbuf("%s_u%d" % (nm, u), sh, dt)
        UB.append(dict(Mab=f("Mab", [128, 128]), Q32=f("Q32", [128, 128]), Nrb=f("Nrb", [128, 128], BF16),
                       T3=f("T3", [128, 256], BF16), Z32=f("Z32", [128, 128]), Z16=f("Z16", [128, 128], BF16),
                       MQ=[f("MQ%d" % i, [128, 256]) for i in range(2)],
                       G16=f("G16", [64, 64], BF16), RT=f("RT", [64, 128], BF16)))
    e1s = [P.sbuf("e1_%d" % i, [128, 256]) for i in range(2)]; e2 = P.sbuf("e2_", [128, 256])
    es = P.sbuf("es_", [128, 4]); er = P.sbuf("er_", [128, 4])
    yt = [P.sbuf("yrt%d" % i, [128, 256]) for i in range(2)]
    zv2 = [P.sbuf("zv2_%d" % i, [128, 3, 256], BF16) for i in range(2)]
    H4 = lambda vw: vw.rearrange("p (h c) -> p h c", h=4)
    epi_n = [0]

    def epilogue(ti):
        P.tag = "epi"
        i2 = epi_n[0] % 2; epi_n[0] += 1
        zin = zv2[i2]; e1 = e1s[i2]
        P.dma(zin.ap, zv[ti][:, :, 512:768])
        P.tt(e1.ap, zin[:, 0, :], c0rep[:, 512:768], ALU.mult, eng="pool")
        P.tt(e2.ap, zin[:, 1, :], murep[:, 0, 512:768], ALU.mult, eng="pool")
        P.tt(e1.ap, e1.ap, e2.ap, ALU.add, eng="pool")
        P.tt(e2.ap, zin[:, 2, :], murep[:, 1, 512:768], ALU.mult, eng="pool")
        P.tt(e1.ap, e1.ap, e2.ap, ALU.add, eng="pool")
        P.tt(H4(e1.ap), H4(e1.ap), bc(racc[:, ti, :], [128, 4, 64], 2), ALU.mult, eng="pool")
        y = yacc[:, ti, :]
        P.reduce(es.ap, H4(y), ALU.add)
        P.ts(es.ap, es.ap, 1.0 / 64.0, None, ALU.mult)
        P.tt(H4(y), H4(y), bc(es.ap, [128, 4, 64], 2), ALU.subtract)
        o = yt[i2]
        P.tt(o.ap, y, y, ALU.mult)
        P.reduce(er.ap, H4(o.ap), ALU.add)
        P.ts(er.ap, er.ap, 1.0 / 64.0, 64e-5, ALU.mult, ALU.add)
        P.act(er.ap, er.ap, AF.Sqrt); P.recip(er.ap, er.ap)
        P.tt(H4(o.ap), H4(y), bc(er.ap, [128, 4, 64], 2), ALU.mult)
        P.tt(o.ap, o.ap, vrep[:, 3, :], ALU.mult)
        P.tt(o.ap, o.ap, vrep[:, 4, :], ALU.add)
        P.tt(o.ap, o.ap, e1.ap, ALU.add)
        pg = pp.get()
        P.matmul(pg[:, 0:256], LRg[:, ti * 128:(ti + 1) * 128], gup.ap)
        P.tt(o.ap, o.ap, pg[:, 0:256], ALU.mult)
        P.dma(yo[ti * 128:(ti + 1) * 128, :], o.ap, queue="sp", nowaw=True)

    orders = [list(range(34)), [1, 0] + list(range(33, 1, -1))]

    def start_prep(step):
        return [prep_gen(d, orders[d][step], step % 2) for d in range(2)]

    def advance(gens):
        for g in gens:
            next(g, None)
    for g in start_prep(0):
        for _ in g:
            pass
    for step in range(34):
        par = step % 2
        units = []
        for d in range(2):
            ti = orders[d][step]
            for h in range(4):
                units.append((d, ti, h, PB[(d, par)], UB[d * 4 + h]))
        gens = start_prep(step + 1) if step + 1 < 34 else []
        P.tag = "s1"
        for (d, ti, h, T_, U) in units:
            pair, hs = h // 2, slice(64 * (h % 2), 64 * (h % 2) + 64)
            AR, BK = T_["FTAR"], T_["FTBK"]
            p1 = pp.get(); p2 = pp.get(); p3 = pp.get()
            P.matmul(p1[:, 0:128], AR[hs, pair, 0:128], BK[hs, pair, 0:128])
            P.matmul(p2[:, 0:256], BK[hs, pair, 0:128], AR[hs, pair, 0:256])
            P.matmul(p3[:, 0:256], BK[hs, pair, 128:256], AR[hs, pair, 0:256])
            P.tt(U["Mab"].ap, p1[:, 0:128], masks[:, M_I[d], :], ALU.mult)
            P.tt(U["Q32"].ap, p2[:, 0:128], masks[:, QS_I[d], :], ALU.mult)
            P.tt(U["Nrb"].ap, p2[:, 128:256], maskQN[:, d, 128:256], ALU.mult)
            P.tt(U["T3"].ap, p3[:, 0:256], maskQN[:, d, :], ALU.mult)
        P.tag = "s2"
        for (d, ti, h, T_, U) in units:
            hc = slice(h * 64, (h + 1) * 64)
            px = pp.get()
            P.matmul(px[:, 0:64], U["T3"][:, 0:128], T_["V16"][:, hc])
            P.copy(U["Z32"][:, 0:64], px[:, 0:64], eng="act")
            P.copy(U["Z32"][:, 64:128], T_["Ap32"][:, hc], eng="pool")
        for lv in range(7):
            P.tag = "lv%d" % lv
            for (d, ti, h, T_, U) in units:
                if lv == 0:
                    Mv, Qv = U["Mab"].ap, U["Q32"].ap
                else:
                    mq = U["MQ"][(lv - 1) % 2]
                    Mv, Qv = mq[:, 0:128], mq[:, 128:256]
                pa = pp.get()
                P.matmul(pa[:, 0:128], Qv, U["Z32"].ap)
                if lv < 6:
                    pq = pp.get()
                    P.matmul(pq[:, 0:128], Qv, Mv)
                    P.matmul(pq[:, 128:256], Mv, Qv)
                P.tt(U["Z32"].ap, U["Z32"].ap, pa[:, 0:128], ALU.add)
                if lv < 6:
                    P.copy(U["MQ"][lv % 2].ap, pq[:, 0:256], eng="act")
            advance(gens)
        P.tag = "z16"
        for (d, ti, h, T_, U) in units:
            P.copy(U["Z16"].ap, U["Z32"].ap, eng="pool")
        P.tag = "s3"
        for (d, ti, h, T_, U) in units:
            hc = slice(h * 64, (h + 1) * 64)
            pg = pp.get(); pr = pp.get()
            P.matmul(pg[0:64, 0:64], U["Z16"][:, 64:128], T_["Bc16"][:, hc])
            P.matmul(pr[0:64, 0:128], U["Z16"][:, 64:128], U["Nrb"].ap, start=True, stop=False)
            P.matmul(pr[0:64, 0:128], T_["Rp16"][:, hc], id16.ap, start=False, stop=True)
            P.stt(U["G16"].ap, id32[0:64, 0:64], T_["Wc"][:, h:h + 1], pg[0:64, 0:64], ALU.mult, ALU.add)
            P.copy(U["RT"].ap, pr[0:64, 0:128], eng="act")
        P.tag = "post"
        for (d, ti, h, T_, U) in units:
            hc = slice(h * 64, (h + 1) * 64); ch = d * 4 + h
            ph = pp.get(); py = pp.get()
            P.matmul(ph[0:64, 0:64], T_["Bc16"][:, hc], U["Z16"][:, 0:64], start=True, stop=False)
            P.matmul(ph[0:64, 0:64], T_["Kc16"][:, hc], T_["V16"][:, hc], start=False, stop=False)
            P.matmul(ph[0:64, 0:64], U["G16"].ap, S16[:, ch, :], start=False, stop=True)
            P.matmul(py[:, 0:64], U["Nrb"].ap, U["Z16"][:, 0:64], start=True, stop=False)
            P.matmul(py[:, 0:64], U["T3"][:, 128:256], T_["V16"][:, hc], start=False, stop=False)
            P.matmul(py[:, 0:64], U["RT"].ap, S16[:, ch, :], start=False, stop=True)
            P.copy(S16[:, ch, :], ph[0:64, 0:64], eng="act")
            P.tt(yacc[:, ti, hc], yacc[:, ti, hc], py[:, 0:64], ALU.add)
        for t_done in range(34):
            bpos = (1, 0)[t_done] if t_done < 2 else 35 - t_done
            if max(t_done, bpos) == step:
                epilogue(t_done)
    return P.finish([yo])


_PROGS = {}


_BUILDERS = {"A": build_A, "B1": build_B1, "B2": build_B2, "B3": build_B3, "C1": build_C1, "C2": build_C2}
_TIMES = []


def _run(name, in_maps):
    import time as _t
    t0 = _t.time()
    nc = _BUILDERS[name]()
    t1 = _t.time()
    res = run_bass_kernel_spmd(nc, in_maps, core_ids=list(range(8)))
    _TIMES.append((name, round(t1 - t0, 1), round(_t.time() - t1, 1)))
    return res.results


def _shift3(a):
    p = np.zeros_like(a); n = np.zeros_like(a)
    p[1:256] = a[0:255]; p[257:] = a[256:-1]
    n[0:255] = a[1:256]; n[256:-1] = a[257:]
    return np.stack([a, p, n], 1)


def _colv(v, n):
    return np.ascontiguousarray(np.asarray(v, np.float32).reshape(n, 128).T)


def _rope_tables():
    inv = (10000.0 ** (-np.arange(8, dtype=np.float32) / 8)).astype(np.float32)
    row = np.repeat(np.arange(64, dtype=np.float32), 64); col = np.tile(np.arange(64, dtype=np.float32), 64)
    ang = np.concatenate([row[:, None] * inv, col[:, None] * inv], -1).astype(np.float32)
    cos, sin = np.cos(ang).astype(np.float32), np.sin(ang).astype(np.float32)
    C = np.ones((96, TT), np.float32); S = np.zeros((96, TT), np.float32)
    C[64:80, 256:] = cos.T; C[80:96, 256:] = cos.T
    S[64:80, 256:] = -sin.T; S[80:96, 256:] = sin.T
    return C, S


def _tok(h):
    return np.concatenate([np.arange(128 * h, 128 * h + 128), 256 + np.arange(2048 * h, 2048 * h + 2048)])


def kernel(x, c, ctx, c_ctx, ada_w, ada_b, w_in, gla_gk_up, gla_gk_b, gla_norm_g, mla_q_norm_g, mla_kv_norm_g,
           mla_w_uq, mla_w_ukv, rwkv_shift_mu, rwkv_w0, rwkv_w_up, rwkv_a0, rwkv_a_up, rwkv_g_up, rwkv_k_k,
           rwkv_k_a, rwkv_r_k, rwkv_ln_g, rwkv_ln_b, w_branch, w_out, ln1_g, ln1_b, mlp_w1, mlp_w2, ln2_g, ln2_b,
           _layers=4):
    f32 = lambda a: np.ascontiguousarray(np.asarray(a, np.float32))
    xf = np.concatenate([f32(ctx), f32(x)], 1)
    C, S = _rope_tables()
    s_ = np.arange(128)[:, None]; t_ = np.arange(128)[None, :]
    masks = np.stack([s_ <= t_, s_ >= t_, s_ > t_, s_ < t_], 1).astype(np.float32)
    toks = [_tok(0), _tok(1)]
    for l in range(_layers):
        ims = []
        for cid in range(8):
            b, h = cid // 2, cid % 2
            c2 = np.stack([f32(c)[b], f32(c_ctx)], -1).reshape(8, 128, 2).transpose(1, 0, 2).copy()
            ims.append({"xT": np.ascontiguousarray(xf[b][toks[h]].T), "c2": c2, "ada_w": f32(ada_w[l]),
                        "ada_bT": _colv(ada_b[l], 48), "w_in": f32(w_in[l])})
        rA = _run("A", ims)
        z = np.empty((4, TT, 7232), NPBF)
        for cid in range(8):
            z[cid // 2][toks[cid % 2]] = rA[cid]["zT"].T
        mods = [rA[2 * b]["mod"] for b in range(4)]
        ims = []
        for cid in range(8):
            b, h0 = cid // 2, 4 * (cid % 2)
            zb = z[b]; kr = zb[:, 2208:2240]
            wq = f32(mla_w_uq[l])[:, h0 * 96:(h0 + 4) * 96]; w3 = wq.reshape(384, 4, 96)
            wqr = np.concatenate([w3[:, :, 0:64], w3[:, :, 80:96], w3[:, :, 64:80]], -1).reshape(384, 384)
            k3 = f32(mla_w_ukv[l]).reshape(256, 8, 128)[:, h0:h0 + 4]
            ims.append({"qdT": np.ascontiguousarray(zb[:, 1568:1952].T), "kvdT": np.ascontiguousarray(zb[:, 1952:2208].T),
                        "krT": np.ascontiguousarray(kr.T), "krrT": np.ascontiguousarray(np.concatenate([kr[:, 16:32], kr[:, 0:16]], 1).T),
                        "wq": np.ascontiguousarray(wq), "wqr": np.ascontiguousarray(wqr),
                        "wkn": np.ascontiguousarray(k3[:, :, 0:64].reshape(256, 256)),
                        "wkv": np.ascontiguousarray(k3[:, :, 64:128].reshape(256, 256)),
                        "qg": _colv(mla_q_norm_g[l], 3), "kg": _colv(mla_kv_norm_g[l], 2), "ropeC": C, "ropeS": S})
        rB1 = _run("B1", ims)
        yb = np.empty((4, TT, 512), NPBF)
        for cid in range(8):
            yb[cid // 2][:, 256 * (cid % 2):256 * (cid % 2) + 256] = rB1[cid]["ybT"].T
        ims = []
        for cid in range(8):
            b, h0 = cid // 2, 2 * (cid % 2)
            zb = z[b]; cs = slice(h0 * 64, h0 * 64 + 128)
            gu = f32(gla_gk_up[l]); gbv = f32(gla_gk_b[l])
            ims.append({"qT": np.ascontiguousarray(zb[:, 0:256][:, cs].T), "kT": np.ascontiguousarray(zb[:, 256:512][:, cs].T),
                        "kM": np.ascontiguousarray(zb[:, 256:512][:, cs]),
                        "vM": np.ascontiguousarray(zb[:, 512 + h0 * 128:512 + h0 * 128 + 256]),
                        "ogM": np.ascontiguousarray(zb[:, 1056 + h0 * 128:1056 + h0 * 128 + 256]),
                        "gdT0": np.ascontiguousarray(zb[:, 1024:1040].T), "gdT1": np.ascontiguousarray(zb[:, 1040:1056].T),
                        "gup0": np.ascontiguousarray(gu[0][:, cs]), "gup1": np.ascontiguousarray(gu[1][:, cs]),
                        "gkb0": np.ascontiguousarray(gbv[0][None, cs]), "gkb1": np.ascontiguousarray(gbv[1][None, cs]),
                        "ng": f32(gla_norm_g[l])[None], "masks": masks})
        rB2 = _run("B2", ims)
        ya = np.empty((4, TT, 512), NPBF)
        for cid in range(8):
            ya[cid // 2][:, 256 * (cid % 2):256 * (cid % 2) + 256] = rB2[cid]["ya"].astype(NPBF)
        ims = []
        mu = f32(rwkv_shift_mu[l]); ident = np.eye(128, dtype=np.float32)
        for cid in range(8):
            b, h0 = cid // 2, 4 * (cid % 2)
            zb = z[b]; base = 2240; cs = slice(h0 * 64, h0 * 64 + 256)
            sel = np.concatenate([np.arange(base + o + h0 * 64, base + o + h0 * 64 + 256) for o in (0, 512, 1024)])
            lr = np.stack([_shift3(zb[:, base + o:base + o + 128]).transpose(1, 2, 0) for o in (1536, 1664, 1792)], 0)
            vecs = np.stack([f32(v_[l])[cs] for v_ in (rwkv_k_k, rwkv_k_a, rwkv_r_k, rwkv_ln_g, rwkv_ln_b)], 0)
            ims.append({"zrkv": np.ascontiguousarray(_shift3(zb[:, sel])), "lr": np.ascontiguousarray(lr),
                        "murkv": np.ascontiguousarray(mu[:, sel - base]),
                        "mulr": np.ascontiguousarray(np.stack([mu[:, o:o + 128].T for o in (1536, 1664, 1792)], 1)),
                        "wup": np.ascontiguousarray(np.concatenate([f32(rwkv_w_up[l])[0][:, cs], f32(rwkv_w_up[l])[1][:, cs]], 0)),
                        "w0": np.ascontiguousarray(f32(rwkv_w0[l])[:, None, cs]),
                        "aup": np.ascontiguousarray(np.concatenate([f32(rwkv_a_up[l])[0][:, cs], f32(rwkv_a_up[l])[1][:, cs]], 0)),
                        "a0": np.ascontiguousarray(f32(rwkv_a0[l])[:, None, cs]),
                        "gup": np.ascontiguousarray(f32(rwkv_g_up[l])[:, cs]), "vecs": np.ascontiguousarray(vecs),
                        "masks": masks, "ident": ident})
        rB3 = _run("B3", ims)
        yr = np.empty((4, TT, 512), NPBF)
        for cid in range(8):
            yr[cid // 2][:, 256 * (cid % 2):256 * (cid % 2) + 256] = rB3[cid]["yr"].astype(NPBF)
        ims = []
        for cid in range(8):
            b, h = cid // 2, cid % 2
            y = np.concatenate([ya[b], yb[b], yr[b]], 1)[toks[h]]
            ims.append({"xT": np.ascontiguousarray(xf[b][toks[h]].T), "yT": np.ascontiguousarray(y.T),
                        "gT": np.ascontiguousarray(rA[cid]["zT"][4160:]), "mod": mods[b],
                        "w_branch": f32(w_branch[l]).reshape(1536, 1024), "w_out": f32(w_out[l]),
                        "ln_g": _colv(ln1_g[l], 8), "ln_b": _colv(ln1_b[l], 8)})
        rC1 = _run("C1", ims)
        ims = [{"xT": rC1[cid]["x1T"], "mod": mods[cid // 2], "w1": f32(mlp_w1[l]), "w2": f32(mlp_w2[l]),
                "ln_g": _colv(ln2_g[l], 8), "ln_b": _colv(ln2_b[l], 8)} for cid in range(8)]
        rC2 = _run("C2", ims)
        for cid in range(8):
            xf[cid // 2][toks[cid % 2]] = rC2[cid]["x2T"].T
    return np.ascontiguousarray(xf[:, 256:, :])
```

```python
import numpy as np
import concourse.bass as bass
import concourse.mybir as mybir

F32 = mybir.dt.float32
BF16 = mybir.dt.bfloat16
AF = mybir.ActivationFunctionType
ALU = mybir.AluOpType
AX = mybir.AxisListType

ENGS = ("pe", "act", "dve", "pool", "sp")


class Buf:
    def __init__(self, prog, t, name, space):
        self.prog = prog
        self.t = t
        self.name = name
        self.space = space
        self.last_write = None
        self.reads = {}
        self.war = {}
        self.dsem = None
        self.dcount = 0

    def __getitem__(self, idx):
        return View(self, self.t.__getitem__(idx))

    @property
    def ap(self):
        return View(self, self.t[:] if self.space != "dram" else self.t.ap())


class View:
    def __init__(self, buf, ap):
        self.buf = buf
        self.ap = ap

    def __getitem__(self, idx):
        return View(self.buf, self.ap.__getitem__(idx))

    def rearrange(self, *a, **k):
        return View(self.buf, self.ap.rearrange(*a, **k))

    def bitcast(self, dt):
        return View(self.buf, self.ap.bitcast(dt))


class Prog:
    def __init__(self, same_engine_sync=True):
        self.nc = bass.Bass("TRN2", target_bir_lowering=False)
        self.ops = {e: [] for e in ENGS}
        self.count = {e: 0 for e in ENGS}
        self.sems = {}
        self.waited = {e: {} for e in ENGS}
        self.same_engine_sync = same_engine_sync
        self.nbuf = 0
        for e in ("pe", "act", "dve", "pool"):
            self.sems[e] = self.nc.alloc_semaphore(name="s_" + e)
        self.final_tokens = []
        self.tag = ""

    def sbuf(self, name, shape, dt=F32):
        t = self.nc.alloc_sbuf_tensor(name, list(shape), dt)
        return Buf(self, t, name, "sbuf")

    def psum(self, name, shape, dt=F32):
        t = self.nc.alloc_psum_tensor(name, list(shape), dt)
        return Buf(self, t, name, "psum")

    def dram(self, name, shape, dt=F32, kind="Internal"):
        t = self.nc.dram_tensor(name, list(shape), dt, kind=kind)
        return Buf(self, t, name, "dram")

    def sub(self, parent, name):
        b = Buf(self, parent.t, name, parent.space)
        return b

    def _need(self, eng, tok, waits):
        if tok is None:
            return
        key, val, teng = tok
        if teng == eng:
            if eng in ("pe", "sp"):
                return
            if not self.same_engine_sync:
                return
        if self.waited[eng].get(key, 0) >= val:
            return
        cur = waits.get(key, 0)
        if val > cur:
            waits[key] = val

    def _deps(self, eng, reads, writes, pe_accum=False):
        waits = {}
        for v in reads:
            self._need(eng, v.buf.last_write, waits)
        for v in writes:
            b = v.buf
            if not (pe_accum and b.last_write is not None and b.last_write[2] == "pe"):
                self._need(eng, b.last_write, waits)
            for src in (b.reads, b.war):
                for reng, tok in src.items():
                    if reng == eng and eng != "sp":
                        continue
                    self._need(eng, tok, waits)
        for k, v in waits.items():
            self.waited[eng][k] = v
        return list(waits.items())

    def _commit(self, eng, tok, reads, writes):
        for v in reads:
            v.buf.reads[eng if eng != "sp" else ("sp", tok[0])] = tok
        for v in writes:
            v.buf.last_write = tok
            if v.buf.reads:
                v.buf.war = v.buf.reads
                v.buf.reads = {}

    def op(self, eng, fn, reads, writes, pe_accum=False):
        waits = self._deps(eng, reads, writes, pe_accum)
        self.count[eng] += 1
        tok = (eng, self.count[eng], eng)
        self.ops[eng].append((waits, fn, (eng, 1), self.tag))
        self._commit(eng, tok, reads, writes)
        return tok

    def dma(self, out, in_, queue="sp", nowaw=False, **kw):
        b = out.buf
        if b.dsem is None:
            b.dsem = "d_" + b.name + str(self.nbuf)
            self.nbuf += 1
            self.sems[b.dsem] = self.nc.alloc_semaphore(name=b.dsem[:30])
        waits = {}
        self._need(queue, in_.buf.last_write, waits)
        if not (nowaw and b.last_write is not None and b.last_write[0] == b.dsem):
            self._need(queue, b.last_write, waits)
        for src in (b.reads, b.war):
            for reng, tok in src.items():
                self._need(queue, tok, waits)
        for k, v in waits.items():
            self.waited[queue][k] = v
        b.dcount += 16
        tok = (b.dsem, b.dcount, "dma")
        oap, iap = out.ap, in_.ap
        fn = lambda e: e.dma_start(out=oap, in_=iap, **kw)
        if queue != "sp":
            self.count[queue] += 0
        self.ops[queue].append((list(waits.items()), fn, (b.dsem, 16), self.tag))
        in_.buf.reads[("dma", b.dsem)] = tok
        b.last_write = tok
        if b.reads:
            b.war = b.reads
            b.reads = {}
        return tok

    def matmul(self, out, lhsT, rhs, start=True, stop=True, **kw):
        o, l, r = out.ap, lhsT.ap, rhs.ap
        return self.op("pe", lambda e: e.matmul(o, l, r, start=start, stop=stop, **kw),
                       [lhsT, rhs], [out], pe_accum=not start)

    def transpose(self, out, in_, ident):
        o, i, d = out.ap, in_.ap, ident.ap
        return self.op("pe", lambda e: e.transpose(o, i, d), [in_, ident], [out])

    def act(self, out, in_, func, bias=None, scale=None, accum_out=None, eng="act"):
        reads = [in_]
        kw = {}
        if bias is not None:
            if isinstance(bias, View):
                reads.append(bias); kw["bias"] = bias.ap
            else:
                kw["bias"] = bias
        if scale is not None:
            if isinstance(scale, View):
                reads.append(scale); kw["scale"] = scale.ap
            else:
                kw["scale"] = scale
        writes = [out]
        if accum_out is not None:
            writes.append(accum_out); kw["accum_out"] = accum_out.ap
        o, i = out.ap, in_.ap
        return self.op(eng, lambda e: e.activation(o, i, func, **kw), reads, writes)

    def tt(self, out, in0, in1, op, eng="dve"):
        o, a, b = out.ap, in0.ap, in1.ap
        return self.op(eng, lambda e: e.tensor_tensor(o, a, b, op), [in0, in1], [out])

    def ts(self, out, in0, s1, s2, op0, op1=None, eng="dve", accum_out=None):
        reads = [in0]
        a1 = s1.ap if isinstance(s1, View) else s1
        a2 = s2.ap if isinstance(s2, View) else s2
        if isinstance(s1, View): reads.append(s1)
        if isinstance(s2, View): reads.append(s2)
        o, a = out.ap, in0.ap
        writes = [out]
        kw = {}
        if accum_out is not None:
            writes.append(accum_out); kw["accum_out"] = accum_out.ap
        if op1 is None:
            return self.op(eng, lambda e: e.tensor_scalar(o, a, a1, None, op0, **kw), reads, writes)
        return self.op(eng, lambda e: e.tensor_scalar(o, a, a1, a2, op0, op1, **kw), reads, writes)

    def stt(self, out, in0, scalar, in1, op0, op1, eng="dve"):
        reads = [in0, in1]
        s = scalar.ap if isinstance(scalar, View) else scalar
        if isinstance(scalar, View): reads.append(scalar)
        o, a, b = out.ap, in0.ap, in1.ap
        return self.op(eng, lambda e: e.scalar_tensor_tensor(o, a, s, b, op0, op1), reads, [out])

    def copy(self, out, in_, eng="dve"):
        o, i = out.ap, in_.ap
        if eng == "act":
            return self.op(eng, lambda e: e.copy(o, i), [in_], [out])
        return self.op(eng, lambda e: e.tensor_copy(o, i), [in_], [out])

    def memset(self, out, val, eng="dve"):
        o = out.ap
        return self.op(eng, lambda e: e.memset(o, val), [], [out])

    def reduce(self, out, in_, op, axis=AX.X, eng="dve"):
        o, i = out.ap, in_.ap
        return self.op(eng, lambda e: e.tensor_reduce(o, i, axis, op), [in_], [out])

    def recip(self, out, in_):
        o, i = out.ap, in_.ap
        return self.op("dve", lambda e: e.reciprocal(o, i), [in_], [out])

    def finish(self, out_bufs):
        for b in out_bufs:
            self.final_tokens.append(b.last_write)
        nc = self.nc
        with nc.Block() as block:
            def emit(eng_name):
                def body(e):
                    for waits, fn, inc, tag in self.ops[eng_name]:
                        for k, v in waits:
                            w = e.wait_ge(self.sems[k], v)
                            if tag:
                                w.annotate("W:" + tag + ":" + str(k))
                        ins = fn(e)
                        ins.then_inc(self.sems[inc[0]], inc[1])
                        if tag:
                            ins.annotate(tag)
                    if eng_name == "sp":
                        for tok in self.final_tokens:
                            e.wait_ge(self.sems[tok[0]], tok[1])
                return body
            block.tensor(emit("pe"))
            block.scalar(emit("act"))
            block.vector(emit("dve"))
            block.gpsimd(emit("pool"))
            block.sync(emit("sp"))
        return nc


from concourse.bass_utils import run_bass_kernel_spmd
import ml_dtypes

NPBF = ml_dtypes.bfloat16
D = 1024
NT = 2176
BLKS = [(0, 128, 1)] + [(128 + 512 * i, 512, 0) for i in range(4)]
ALPHA = 8.0 ** 0.25


def bc(view, shape, axis):
    return View(view.buf, view.ap.unsqueeze(axis).broadcast_to(list(shape)))


class PsPool:
    def __init__(self, P, n=8):
        self.t = [P.psum("psb%d" % i, [128, 512], F32) for i in range(n)]
        self.i = 0

    def get(self):
        b = self.t[self.i % len(self.t)]
        self.i += 1
        return b


def fm_stats(P, pp, xb, n, eps, ones32, sqb, mean, rstd, tmp):
    P.act(sqb[:, :, 0:n], xb, AF.Square)
    ps_s = pp.get(); ps_q = pp.get()
    for c in range(8):
        P.matmul(ps_s[:, 0:n], ones32.ap, xb[:, c, :], start=(c == 0), stop=(c == 7))
    for c in range(8):
        P.matmul(ps_q[:, 0:n], ones32.ap, sqb[:, c, 0:n], start=(c == 0), stop=(c == 7))
    P.ts(mean[:, 0:n], ps_s[:, 0:n], 1.0 / D, None, ALU.mult)
    P.tt(tmp[:, 0:n], mean[:, 0:n], mean[:, 0:n], ALU.mult)
    P.stt(tmp[:, 0:n], ps_q[:, 0:n], 1.0 / D, tmp[:, 0:n], ALU.mult, ALU.subtract)
    P.ts(tmp[:, 0:n], tmp[:, 0:n], eps, None, ALU.add)
    P.act(tmp[:, 0:n], tmp[:, 0:n], AF.Sqrt)
    P.recip(rstd[:, 0:n], tmp[:, 0:n])


def fm_norm_apply(P, xb, n, mean, rstd, s1, s2, out):
    P.tt(xb, xb, bc(mean[:, 0:n], [128, 8, n], 1), ALU.subtract)
    P.tt(xb, xb, bc(rstd[:, 0:n], [128, 8, n], 1), ALU.mult)
    for c in range(8):
        P.ts(out[:, c, :], xb[:, c, :], s1[:, c:c + 1], s2[:, c:c + 1], ALU.mult, ALU.add,
             eng="pool" if c % 2 else "dve")


def fm_norm_gen(P, pp, xb, n, eps, ones32, sqb, mean, rstd, tmp, s1, s2, out):
    P.act(sqb[:, :, 0:n], xb, AF.Square)
    yield
    ps_s = pp.get()
    for c in range(8):
        P.matmul(ps_s[:, 0:n], ones32.ap, xb[:, c, :], start=(c == 0), stop=(c == 7))
    P.ts(mean[:, 0:n], ps_s[:, 0:n], 1.0 / D, None, ALU.mult)
    yield
    ps_q = pp.get()
    for c in range(8):
        P.matmul(ps_q[:, 0:n], ones32.ap, sqb[:, c, 0:n], start=(c == 0), stop=(c == 7))
    P.tt(tmp[:, 0:n], mean[:, 0:n], mean[:, 0:n], ALU.mult)
    P.stt(tmp[:, 0:n], ps_q[:, 0:n], 1.0 / D, tmp[:, 0:n], ALU.mult, ALU.subtract)
    yield
    P.ts(tmp[:, 0:n], tmp[:, 0:n], eps, None, ALU.add)
    P.act(tmp[:, 0:n], tmp[:, 0:n], AF.Sqrt)
    P.recip(rstd[:, 0:n], tmp[:, 0:n])
    yield
    P.tt(xb, xb, bc(mean[:, 0:n], [128, 8, n], 1), ALU.subtract)
    yield
    P.tt(xb, xb, bc(rstd[:, 0:n], [128, 8, n], 1), ALU.mult)
    yield
    for c in range(8):
        P.ts(out[:, c, :], xb[:, c, :], s1[:, c:c + 1], s2[:, c:c + 1], ALU.mult, ALU.add,
             eng="pool" if c % 2 else "dve")
        if c == 3:
            yield


def build_A():
    P = Prog()
    X = lambda n, s, dt=F32: P.dram(n, s, dt, kind="ExternalInput")
    xT = X("xT", [D, NT]); c2d = X("c2", [128, 8, 2]); ada_w = X("ada_w", [D, 6144])
    ada_bT = X("ada_bT", [128, 48]); w_in = X("w_in", [D, 7232])
    zT = P.dram("zT", [7232, NT], BF16, kind="ExternalOutput")
    modo = P.dram("mod", [128, 48, 2], F32, kind="ExternalOutput")
    pp = PsPool(P)
    ones32 = P.sbuf("ones32", [128, 128]); P.memset(ones32.ap, 1.0)
    c2 = P.sbuf("c2s", [128, 8, 2]); P.dma(c2.ap, c2d.ap); P.act(c2.ap, c2.ap, AF.Silu)
    bias = P.sbuf("adab", [128, 48]); P.dma(bias.ap, ada_bT.ap)
    mod = P.sbuf("modsb", [128, 48, 2])
    wA = [P.sbuf("wA%d" % i, [128, 8, 512]) for i in range(2)]
    awv = ada_w.ap.rearrange("(kc p) n -> p kc n", p=128)
    for g in range(12):
        w = wA[g % 2]
        for kc in range(8):
            P.dma(w[:, kc, :], awv[:, kc, g * 512:(g + 1) * 512], queue="sp" if kc % 2 else "pool", nowaw=True)
        ps = pp.get()
        for oc in range(4):
            for kc in range(8):
                P.matmul(ps[:, oc * 16:oc * 16 + 2], w[:, kc, oc * 128:(oc + 1) * 128], c2[:, kc, :],
                         start=(kc == 0), stop=(kc == 7))
        P.tt(mod[:, g * 4:(g + 1) * 4, :], ps[:, 0:64].rearrange("p (a b) -> p a b", b=16)[:, :, 0:2],
             bc(bias[:, g * 4:(g + 1) * 4], [128, 4, 2], 2), ALU.add)
    P.dma(modo.ap, mod.ap, queue="pool")
    opsc = P.sbuf("opsc", [128, 8, 2]); P.ts(opsc.ap, mod[:, 8:16, :], 1.0, None, ALU.add)
    wI = P.sbuf("wI", [128, 8, 7232], BF16)
    wiv = w_in.ap.rearrange("(kc p) n -> p kc n", p=128)
    for kc in range(8):
        P.dma(wI[:, kc, :], wiv[:, kc, :], queue="pool", nowaw=True)
    xv = xT.ap.rearrange("(c p) t -> p c t", p=128)
    xb = P.sbuf("xb0", [128, 8, 512])
    sqb = P.sbuf("sqb", [128, 8, 512]); mean = P.sbuf("mean", [128, 512]); rstd = P.sbuf("rstd", [128, 512])
    tmp = P.sbuf("tmp", [128, 512])
    xms = [P.sbuf("xm%d" % i, [128, 8, 512], BF16) for i in range(2)]
    zst = [P.sbuf("zst%d" % i, [128, 512], BF16) for i in range(4)]

    def pre_gen(bi):
        t0, n, col = BLKS[bi]
        for c in range(8):
            P.dma(xb[:, c, 0:n], xv[:, c, t0:t0 + n], queue="sp", nowaw=True)
        yield
        yield from fm_norm_gen(P, pp, xb[:, :, 0:n], n, 1e-6, ones32, sqb, mean, rstd, tmp,
                               opsc[:, :, col], mod[:, 0:8, col], xms[bi % 2][:, :, 0:n])

    for _ in pre_gen(0):
        pass
    k = 0
    for bi, (t0, n, col) in enumerate(BLKS):
        xm = xms[bi % 2]
        gpre = pre_gen(bi + 1) if bi + 1 < len(BLKS) else None
        for j in range(57):
            m = 128 if j < 56 else 64
            ps = pp.get()
            for kc in range(8):
                P.matmul(ps[0:m, 0:n], wI[:, kc, j * 128:j * 128 + m], xm[:, kc, 0:n], start=(kc == 0), stop=(kc == 7))
            st = zst[k % 4]; k += 1
            if j == 32:
                P.copy(st[0:64, 0:n], ps[0:64, 0:n], eng="dve")
                P.act(st[64:128, 0:n], ps[64:128, 0:n], AF.Sigmoid)
            elif j * 128 >= 4160:
                P.act(st[0:m, 0:n], ps[0:m, 0:n], AF.Sigmoid)
            elif k % 2:
                P.copy(st[0:m, 0:n], ps[0:m, 0:n], eng="act")
            else:
                P.copy(st[0:m, 0:n], ps[0:m, 0:n], eng="dve")
            P.dma(zT[j * 128:j * 128 + m, t0:t0 + n], st[0:m, 0:n], queue="sp", nowaw=True)
            if gpre is not None and j % 5 == 4:
                next(gpre, None)
        if gpre is not None:
            for _ in gpre:
                pass
    return P.finish([zT, modo])


def build_C1():
    P = Prog()
    X = lambda n, s, dt=F32: P.dram(n, s, dt, kind="ExternalInput")
    xT = X("xT", [D, NT]); yT = X("yT", [1536, NT], BF16); gT = X("gT", [3072, NT], BF16)
    modd = X("mod", [128, 48, 2]); w_br = X("w_branch", [1536, D]); w_out = X("w_out", [D, D])
    lng = X("ln_g", [128, 8]); lnb = X("ln_b", [128, 8])
    oT = P.dram("x1T", [D, NT], F32, kind="ExternalOutput")
    pp = PsPool(P)
    ones32 = P.sbuf("ones32", [128, 128]); P.memset(ones32.ap, 1.0)
    mod = P.sbuf("modsb", [128, 48, 2]); P.dma(mod.ap, modd.ap)
    g1 = P.sbuf("lng", [128, 8]); b1 = P.sbuf("lnb", [128, 8]); P.dma(g1.ap, lng.ap); P.dma(b1.ap, lnb.ap)
    wb = P.sbuf("wb", [128, 12, D], BF16); wo = P.sbuf("wo", [128, 8, D], BF16)
    wbv = w_br.ap.rearrange("(kc p) n -> p kc n", p=128); wov = w_out.ap.rearrange("(kc p) n -> p kc n", p=128)
    for kc in range(12):
        P.dma(wb[:, kc, :], wbv[:, kc, :], queue="pool", nowaw=True)
    for kc in range(8):
        P.dma(wo[:, kc, :], wov[:, kc, :], queue="pool", nowaw=True)
    xv = xT.ap.rearrange("(c p) t -> p c t", p=128); yv = yT.ap.rearrange("(c p) t -> p c t", p=128)
    gv = gT.ap.rearrange("(c p) t -> p c t", p=128); ov = oT.ap.rearrange("(c p) t -> p c t", p=128)
    xb = P.sbuf("xb", [128, 8, 512]); yb = P.sbuf("yb", [128, 12, 512], BF16); gb = P.sbuf("gb", [128, 24, 512], BF16)
    mb = P.sbuf("mb", [128, 8, 512], BF16); t1 = P.sbuf("t1", [128, 512]); t2 = P.sbuf("t2", [128, 512])
    sqb = P.sbuf("sqb", [128, 8, 512]); mean = P.sbuf("mean", [128, 512]); rstd = P.sbuf("rstd", [128, 512])
    tmp = P.sbuf("tmp", [128, 512]); ob = P.sbuf("ob", [128, 8, 512])
    for bi, (t0, n, col) in enumerate(BLKS):
        for c in range(8):
            P.dma(xb[:, c, 0:n], xv[:, c, t0:t0 + n], queue="sp", nowaw=True)
        for c in range(12):
            P.dma(yb[:, c, 0:n], yv[:, c, t0:t0 + n], queue="sp", nowaw=True)
        for c in range(24):
            P.dma(gb[:, c, 0:n], gv[:, c, t0:t0 + n], queue="pool", nowaw=True)
        for oc in range(8):
            for nb in range(3):
                ps = pp.get()
                for kc in range(4):
                    P.matmul(ps[:, 0:n], wb[:, nb * 4 + kc, oc * 128:(oc + 1) * 128], yb[:, nb * 4 + kc, 0:n],
                             start=(kc == 0), stop=(kc == 3))
                dst = t1 if nb == 0 else t2
                P.tt(dst[:, 0:n], ps[:, 0:n], gb[:, nb * 8 + oc, 0:n], ALU.mult)
                if nb == 1:
                    P.tt(t1[:, 0:n], t1[:, 0:n], t2[:, 0:n], ALU.add, eng="pool")
                if nb == 2:
                    P.tt(mb[:, oc, 0:n], t1[:, 0:n], t2[:, 0:n], ALU.add, eng="pool")
        for oc in range(8):
            ps = pp.get()
            for kc in range(8):
                P.matmul(ps[:, 0:n], wo[:, kc, oc * 128:(oc + 1) * 128], mb[:, kc, 0:n], start=(kc == 0), stop=(kc == 7))
            P.ts(t1[:, 0:n], ps[:, 0:n], mod[:, 16 + oc, col:col + 1], None, ALU.mult)
            P.stt(xb[:, oc, 0:n], xb[:, oc, 0:n], ALPHA, t1[:, 0:n], ALU.mult, ALU.add)
        fm_stats(P, pp, xb[:, :, 0:n], n, 1e-5, ones32, sqb, mean, rstd, tmp)
        fm_norm_apply(P, xb[:, :, 0:n], n, mean, rstd, g1.ap, b1.ap, ob[:, :, 0:n])
        for c in range(8):
            P.dma(ov[:, c, t0:t0 + n], ob[:, c, 0:n], queue="sp", nowaw=True)
    return P.finish([oT])


def build_C2():
    P = Prog()
    X = lambda n, s, dt=F32: P.dram(n, s, dt, kind="ExternalInput")
    xT = X("xT", [D, NT]); modd = X("mod", [128, 48, 2]); w1d = X("w1", [D, 4096]); w2d = X("w2", [4096, D])
    lng = X("ln_g", [128, 8]); lnb = X("ln_b", [128, 8])
    oT = P.dram("x2T", [D, NT], F32, kind="ExternalOutput")
    pp = PsPool(P)
    ones32 = P.sbuf("ones32", [128, 128]); P.memset(ones32.ap, 1.0)
    mod = P.sbuf("modsb", [128, 48, 2]); P.dma(mod.ap, modd.ap)
    opsc = P.sbuf("opsc", [128, 8, 2]); P.ts(opsc.ap, mod[:, 32:40, :], 1.0, None, ALU.add)
    g1 = P.sbuf("lng", [128, 8]); b1 = P.sbuf("lnb", [128, 8]); P.dma(g1.ap, lng.ap); P.dma(b1.ap, lnb.ap)
    w1 = P.sbuf("w1s", [128, 8, 4096], BF16); w2 = P.sbuf("w2s", [128, 32, D], BF16)
    w1v = w1d.ap.rearrange("(kc p) n -> p kc n", p=128); w2v = w2d.ap.rearrange("(kc p) n -> p kc n", p=128)
    for kc in range(8):
        P.dma(w1[:, kc, :], w1v[:, kc, :], queue="pool", nowaw=True)
    for kc in range(32):
        P.dma(w2[:, kc, :], w2v[:, kc, :], queue="pool", nowaw=True)
    xv = xT.ap.rearrange("(c p) t -> p c t", p=128); ov = oT.ap.rearrange("(c p) t -> p c t", p=128)
    NB = 256
    xb = P.sbuf("xb", [128, 8, NB]); xrs = [P.sbuf("xr%d" % i, [128, 8, NB]) for i in range(2)]
    xms = [P.sbuf("xm%d" % i, [128, 8, NB], BF16) for i in range(2)]
    hb = P.sbuf("hb", [128, 32, NB], BF16); hrs = [P.sbuf("hr%d" % i, [128, NB]) for i in range(2)]
    t1s = [P.sbuf("t1_%d" % i, [128, NB]) for i in range(2)]
    sqb = P.sbuf("sqb", [128, 8, NB]); mean = P.sbuf("mean", [128, NB]); rstd = P.sbuf("rstd", [128, NB])
    tmp = P.sbuf("tmp", [128, NB])
    blks = []
    for (t0, n, col) in BLKS:
        for s in range(0, n, NB):
            blks.append((t0 + s, min(NB, n - s), col))

    def pre_gen(bi):
        t0, n, col = blks[bi]
        xr = xrs[bi % 2]
        for c in range(8):
            P.dma(xb[:, c, 0:n], xv[:, c, t0:t0 + n], queue="sp", nowaw=True)
        P.copy(xr[:, :, 0:n], xb[:, :, 0:n], eng="pool")
        yield
        yield from fm_norm_gen(P, pp, xb[:, :, 0:n], n, 1e-6, ones32, sqb, mean, rstd, tmp,
                               opsc[:, :, col], mod[:, 24:32, col], xms[bi % 2][:, :, 0:n])

    def post_gen(bi):
        t0, n, col = blks[bi]
        xr = xrs[bi % 2]
        yield from fm_norm_gen(P, pp, xr[:, :, 0:n], n, 1e-5, ones32, sqb, mean, rstd, tmp, g1.ap, b1.ap, xr[:, :, 0:n])
        yield
        for c in range(8):
            P.dma(ov[:, c, t0:t0 + n], xr[:, c, 0:n], queue="sp", nowaw=True)

    def drain(g):
        if g is not None:
            for _ in g:
                pass

    drain(pre_gen(0))
    gpost = None
    for bi, (t0, n, col) in enumerate(blks):
        xm = xms[bi % 2]; xr = xrs[bi % 2]
        gpre = pre_gen(bi + 1) if bi + 1 < len(blks) else None
        for fc in range(32):
            ps = pp.get()
            for kc in range(8):
                P.matmul(ps[:, 0:n], w1[:, kc, fc * 128:(fc + 1) * 128], xm[:, kc, 0:n], start=(kc == 0), stop=(kc == 7))
            hr = hrs[fc % 2]
            P.act(hr[:, 0:n], ps[:, 0:n], AF.Relu)
            P.tt(hb[:, fc, 0:n], hr[:, 0:n], hr[:, 0:n], ALU.mult)
            if fc % 2 == 1:
                if fc < 20:
                    if gpost is not None:
                        next(gpost, None)
                else:
                    if fc == 21:
                        drain(gpost); gpost = None
                    if gpre is not None:
                        next(gpre, None)
        for oc in range(8):
            ps = pp.get()
            for fc in range(32):
                P.matmul(ps[:, 0:n], w2[:, fc, oc * 128:(oc + 1) * 128], hb[:, fc, 0:n], start=(fc == 0), stop=(fc == 31))
            t1 = t1s[oc % 2]
            P.ts(t1[:, 0:n], ps[:, 0:n], mod[:, 40 + oc, col:col + 1], None, ALU.mult)
            P.stt(xr[:, oc, 0:n], xr[:, oc, 0:n], ALPHA, t1[:, 0:n], ALU.mult, ALU.add)
            if gpre is not None:
                next(gpre, None)
        drain(gpre)
        gpost = post_gen(bi)
    drain(gpost)
    return P.finish([oT])


TT = 4352
QBLK = [(0, 256, 2)] + [(256 + 512 * i, 512, 34) for i in range(8)]


def build_B1():
    P = Prog()
    X = lambda n, s, dt=F32: P.dram(n, s, dt, kind="ExternalInput")
    qdT = X("qdT", [384, TT], BF16); kvdT = X("kvdT", [256, TT], BF16)
    krT = X("krT", [32, TT], BF16); krrT = X("krrT", [32, TT], BF16)
    wq = X("wq", [384, 384]); wqr = X("wqr", [384, 384]); wkn = X("wkn", [256, 256]); wkv = X("wkv", [256, 256])
    qg = X("qg", [128, 3]); kg = X("kg", [128, 2]); Cd = X("ropeC", [96, TT]); Sd = X("ropeS", [96, TT])
    yo = P.dram("ybT", [256, TT], BF16, kind="ExternalOutput")
    pss = [P.psum("pS%d" % i, [128, 512]) for i in range(6)]
    pso = [P.psum("pO%d" % i, [128, 512]) for i in range(2)]
    psi = [0]
    def gps():
        psi[0] += 1
        return pss[psi[0] % 6]
    ones16 = P.sbuf("ones16", [128, 128], BF16); P.memset(ones16.ap, 1.0)
    qgs = P.sbuf("qgs", [128, 3]); kgs = P.sbuf("kgs", [128, 2]); P.dma(qgs.ap, qg.ap); P.dma(kgs.ap, kg.ap)
    Ct = P.sbuf("Ct", [96, TT]); St = P.sbuf("St", [96, TT]); P.dma(Ct.ap, Cd.ap); P.dma(St.ap, Sd.ap, queue="pool")
    qd = P.sbuf("qd", [128, 3, TT], BF16); kvd = P.sbuf("kvd", [128, 2, TT], BF16)
    for c in range(3):
        P.dma(qd[:, c, :], qdT[c * 128:(c + 1) * 128, :], nowaw=True)
    for c in range(2):
        P.dma(kvd[:, c, :], kvdT[c * 128:(c + 1) * 128, :], queue="pool", nowaw=True)
    kr = P.sbuf("kr", [96, TT], BF16); krr = P.sbuf("krr", [96, TT], BF16)
    P.dma(kr[64:96, :], krT.ap); P.dma(krr[64:96, :], krrT.ap, queue="pool")
    wqs = P.sbuf("wqs", [128, 3, 384], BF16); wqrs = P.sbuf("wqrs", [128, 3, 384], BF16)
    wkns = P.sbuf("wkns", [128, 2, 256], BF16); wkvs = P.sbuf("wkvs", [128, 2, 256], BF16)
    P.dma(wqs.ap, wq.ap.rearrange("(c p) n -> p c n", p=128), queue="pool")
    P.dma(wqrs.ap, wqr.ap.rearrange("(c p) n -> p c n", p=128), queue="pool")
    P.dma(wkns.ap, wkn.ap.rearrange("(c p) n -> p c n", p=128), queue="pool")
    P.dma(wkvs.ap, wkv.ap.rearrange("(c p) n -> p c n", p=128), queue="pool")
    rq = P.sbuf("rq", [128, TT]); rk = P.sbuf("rk", [128, TT]); rkc = P.sbuf("rkc", [128, 34])
    sq = P.sbuf("sq", [128, 3, 512], BF16)
    for (t0, n, _) in QBLK:
        for (src, nc_, dst, dim) in ((qd, 3, rq, 384.0), (kvd, 2, rk, 256.0)):
            P.tt(sq[:, 0:nc_, 0:n], src[:, 0:nc_, t0:t0 + n], src[:, 0:nc_, t0:t0 + n], ALU.mult)
            ps = gps()
            for c in range(nc_):
                P.matmul(ps[:, 0:n], ones16.ap, sq[:, c, 0:n], start=(c == 0), stop=(c == nc_ - 1))
            P.ts(dst[:, t0:t0 + n], ps[:, 0:n], 1.0 / dim, 1e-6, ALU.mult, ALU.add)
            P.act(dst[:, t0:t0 + n], dst[:, t0:t0 + n], AF.Sqrt)
            P.recip(dst[:, t0:t0 + n], dst[:, t0:t0 + n])
            if src is kvd:
                for tl in range(n // 128):
                    kt = t0 // 128 + tl
                    ps2 = gps()
                    for c in range(2):
                        P.matmul(ps2[:, 0:1], sq[:, c, tl * 128:(tl + 1) * 128], ones16[:, 0:1], start=(c == 0), stop=(c == 1))
                    P.ts(rkc[:, kt:kt + 1], ps2[:, 0:1], 1.0 / 256.0, 1e-6, ALU.mult, ALU.add)
    P.act(rkc.ap, rkc.ap, AF.Sqrt); P.recip(rkc.ap, rkc.ap)
    P.ts(rq.ap, rq.ap, 96.0 ** -0.5, None, ALU.mult)
    for c in range(3):
        P.ts(qd[:, c, :], qd[:, c, :], qgs[:, c:c + 1], None, ALU.mult, eng="pool")
    for c in range(2):
        P.ts(kvd[:, c, :], kvd[:, c, :], kgs[:, c:c + 1], None, ALU.mult, eng="pool")
    Rk = P.sbuf("Rk", [96, TT], BF16); tk = P.sbuf("tk", [96, TT])
    P.tt(tk[64:96, :], kr[64:96, :], Ct[64:96, :], ALU.mult)
    P.tt(kr[64:96, :], krr[64:96, :], St[64:96, :], ALU.mult)
    P.tt(Rk[64:96, :], tk[64:96, :], kr[64:96, :], ALU.add)
    Qh = P.sbuf("Qh", [96, TT], BF16); Kh = P.sbuf("Kh", [96, TT], BF16); Vh = P.sbuf("Vh", [128, 34, 128], BF16)
    P.memset(Vh.ap, 1.0)
    t1 = P.sbuf("t1", [96, 512]); t2 = P.sbuf("t2", [96, 512])
    es = [P.sbuf("es%d" % i, [128, 512], BF16) for i in range(6)]
    den = P.sbuf("den", [64, 512]); ost = [P.sbuf("ost%d" % i, [64, 512], BF16) for i in range(2)]
    ei = 0
    for h in range(4):
        for (t0, n, _) in QBLK:
            ps1 = gps(); ps2 = gps()
            for c in range(3):
                P.matmul(ps1[0:96, 0:n], wqs[:, c, h * 96:(h + 1) * 96], qd[:, c, t0:t0 + n], start=(c == 0), stop=(c == 2))
            for c in range(3):
                P.matmul(ps2[0:96, 0:n], wqrs[:, c, h * 96:(h + 1) * 96], qd[:, c, t0:t0 + n], start=(c == 0), stop=(c == 2))
            P.tt(t1[:, 0:n], ps1[0:96, 0:n], Ct[:, t0:t0 + n], ALU.mult)
            P.tt(t2[:, 0:n], ps2[0:96, 0:n], St[:, t0:t0 + n], ALU.mult)
            P.tt(t1[:, 0:n], t1[:, 0:n], t2[:, 0:n], ALU.add, eng="pool")
            P.tt(Qh[:, t0:t0 + n], t1[:, 0:n], rq[0:96, t0:t0 + n], ALU.mult)
            ps3 = gps()
            for c in range(2):
                P.matmul(ps3[0:64, 0:n], wkns[:, c, h * 64:(h + 1) * 64], kvd[:, c, t0:t0 + n], start=(c == 0), stop=(c == 1))
            P.tt(Kh[0:64, t0:t0 + n], ps3[0:64, 0:n], rk[0:64, t0:t0 + n], ALU.mult)
            P.copy(Kh[64:96, t0:t0 + n], Rk[64:96, t0:t0 + n], eng="pool")
        for kt in range(34):
            ps4 = gps()
            for c in range(2):
                P.matmul(ps4[:, 0:64], kvd[:, c, kt * 128:(kt + 1) * 128], wkvs[:, c, h * 64:(h + 1) * 64], start=(c == 0), stop=(c == 1))
            P.ts(Vh[:, kt, 0:64], ps4[:, 0:64], rkc[:, kt:kt + 1], None, ALU.mult)
        for qi, (q0, nq, nk) in enumerate(QBLK):
            po = pso[qi % 2]
            pend = []
            for kt in range(nk):
                ps = gps()
                P.matmul(ps[:, 0:nq], Kh[:, kt * 128:(kt + 1) * 128], Qh[:, q0:q0 + nq])
                e = es[ei % 6]; ei += 1
                P.act(e[:, 0:nq], ps[:, 0:nq], AF.Exp)
                pend.append((kt, e))
                if len(pend) > 2:
                    k0, e0 = pend.pop(0)
                    P.matmul(po[:, 0:nq], Vh[:, k0, :], e0[:, 0:nq], start=(k0 == 0), stop=(k0 == nk - 1))
            for (k0, e0) in pend:
                P.matmul(po[:, 0:nq], Vh[:, k0, :], e0[:, 0:nq], start=(k0 == 0), stop=(k0 == nk - 1))
            P.copy(den[:, 0:nq], po[64:128, 0:nq])
            P.recip(den[:, 0:nq], den[:, 0:nq])
            o = ost[qi % 2]
            P.tt(o[:, 0:nq], po[0:64, 0:nq], den[:, 0:nq], ALU.mult)
            P.dma(yo[h * 64:(h + 1) * 64, q0:q0 + nq], o[:, 0:nq], queue="sp", nowaw=True)
    return P.finish([yo])


def build_B2():
    P = Prog()
    X = lambda n, s, dt=F32: P.dram(n, s, dt, kind="ExternalInput")
    qT = X("qT", [128, TT], BF16); kT = X("kT", [128, TT], BF16); kM = X("kM", [TT, 128], BF16)
    vM = X("vM", [TT, 256], BF16); ogM = X("ogM", [TT, 256], BF16)
    gdT = [X("gdT%d" % i, [16, TT], BF16) for i in range(2)]
    gup = [X("gup%d" % i, [16, 128]) for i in range(2)]; gkb = [X("gkb%d" % i, [1, 128]) for i in range(2)]
    ngd = X("ng", [1, 128]); mk = X("masks", [128, 4, 128])
    yo = P.dram("ya", [TT, 256], F32, kind="ExternalOutput")
    pp = PsPool(P)
    masks = P.sbuf("masksb", [128, 4, 128]); P.dma(masks.ap, mk.ap)
    m16 = P.sbuf("m16", [128, 2, 128], BF16); P.copy(m16.ap, masks[:, 0:2, :])
    ones1 = P.sbuf("ones1", [1, 128], BF16); P.memset(ones1.ap, 1.0)
    q = P.sbuf("q", [128, TT], BF16); k = P.sbuf("k", [128, TT], BF16)
    P.dma(q.ap, qT.ap); P.dma(k.ap, kT.ap, queue="pool")
    km = P.sbuf("km", [128, 34, 128], BF16); vm = P.sbuf("vm", [128, 34, 256], BF16)
    P.dma(km.ap, kM.ap.rearrange("(n p) c -> p n c", p=128)); P.dma(vm.ap, vM.ap.rearrange("(n p) c -> p n c", p=128), queue="pool")
    gd = [P.sbuf("gd%d" % i, [16, TT], BF16) for i in range(2)]
    gu = [P.sbuf("gu%d" % i, [16, 128], BF16) for i in range(2)]; gb = [P.sbuf("gb%d" % i, [1, 128], BF16) for i in range(2)]
    for i in range(2):
        P.dma(gd[i].ap, gdT[i].ap); P.dma(gu[i].ap, gup[i].ap, queue="pool"); P.dma(gb[i].ap, gkb[i].ap, queue="pool")
    ng = P.sbuf("ngsb", [128, 128]); P.dma(ng.ap, View(ngd, ngd.t.ap()[0, :].partition_broadcast(128)))
    oacc = P.sbuf("oacc", [128, 34, 256]); P.memset(oacc.ap, 0.0, eng="pool")
    S32 = P.sbuf("S32", [128, 2, 128]); S16 = P.sbuf("S16", [128, 2, 128], BF16)
    P.memset(S32.ap, 0.0); P.memset(S16.ap, 0.0)
    TD = []
    for dr in range(2):
        f = lambda nm, sh, dt=F32: P.sbuf("%s_d%d" % (nm, dr), sh, dt)
        TD.append(dict(sp=f("sp", [128, 128]), E1=f("E1", [128, 128]), E2=f("E2", [128, 128]), E3=f("E3", [128, 128]),
                       dec=f("dec", [128, 1]), qin=f("qin", [128, 128], BF16), kin=f("kin", [128, 128], BF16),
                       kst=f("kst", [128, 128], BF16), attm=[f("attm%d" % i, [128, 128], BF16) for i in range(2)]))
    orders = [list(range(34)), [1, 0] + list(range(33, 1, -1))]
    for step in range(34):
        ctx = []
        for dr in range(2):
            T_ = TD[dr]
            ctx.append(dict(dr=dr, ti=orders[dr][step], last=127 if dr == 0 else 0, T=T_,
                            Sf=S32[:, dr, :], Sb=S16[:, dr, :]))
        for c in ctx:
            dr, ti, T_ = c["dr"], c["ti"], c["T"]
            tsl = slice(ti * 128, (ti + 1) * 128); c["tsl"] = tsl
            ps = pp.get()
            P.matmul(ps[:, 0:128], gd[dr][:, tsl], gu[dr].ap, start=True, stop=False)
            P.matmul(ps[:, 0:128], ones1.ap, gb[dr].ap, start=False, stop=True)
            P.act(T_["sp"].ap, ps[:, 0:128], AF.Exp, scale=-1.0)
        for c in ctx:
            P.act(c["T"]["sp"].ap, c["T"]["sp"].ap, AF.Ln, bias=1.0)
        for c in ctx:
            dr, T_ = c["dr"], c["T"]
            pb = pp.get(); pf = pp.get(); c["pb"] = pb; c["pf"] = pf
            P.matmul(pb[:, 0:128], T_["sp"].ap, masks[:, dr, :])
            P.matmul(pf[:, 0:128], masks[:, 2 + dr, :], T_["sp"].ap)
        for c in ctx:
            T_ = c["T"]
            P.act(T_["E1"].ap, c["pb"][:, 0:128], AF.Exp, scale=-1.0 / 16.0)
            P.act(T_["E2"].ap, c["pb"][:, 0:128], AF.Exp, scale=1.0 / 16.0)
            P.act(T_["E3"].ap, c["pf"][:, 0:128], AF.Exp, scale=-1.0 / 16.0)
        for c in ctx:
            T_, ti, tsl, last = c["T"], c["ti"], c["tsl"], c["last"]
            P.copy(T_["dec"].ap, T_["E1"][:, last:last + 1])
            P.stt(T_["qin"].ap, q[:, tsl], 0.125, T_["E1"].ap, ALU.mult, ALU.mult)
            P.tt(T_["kin"].ap, k[:, tsl], T_["E2"].ap, ALU.mult, eng="pool")
            P.tt(T_["kst"].ap, km[:, ti, :], T_["E3"].ap, ALU.mult, eng="pool")
        for hh in range(2):
            hs = slice(hh * 64, (hh + 1) * 64)
            for c in ctx:
                T_ = c["T"]
                pa = pp.get()
                P.matmul(pa[:, 0:128], T_["kin"][hs, :], T_["qin"][hs, :])
                P.tt(T_["attm"][hh].ap, pa[:, 0:128], m16[:, c["dr"], :], ALU.mult)
        for hh in range(2):
            hs = slice(hh * 64, (hh + 1) * 64); vs = slice(hh * 128, (hh + 1) * 128)
            for c in ctx:
                T_, ti, Sf, Sb = c["T"], c["ti"], c["Sf"], c["Sb"]
                po = pp.get()
                P.matmul(po[:, 0:128], T_["attm"][hh].ap, vm[:, ti, vs], start=True, stop=False)
                P.matmul(po[:, 0:128], T_["qin"][hs, :], Sb[hs, :], start=False, stop=True)
                pu = pp.get()
                P.matmul(pu[:, 0:128], T_["kst"].ap, vm[:, ti, vs])
                P.tt(oacc[:, ti, vs], oacc[:, ti, vs], po[:, 0:128], ALU.add)
                P.stt(Sf[hs, :], Sf[hs, :], T_["dec"][hs, 0:1], pu[hs, 0:128], ALU.mult, ALU.add)
                P.copy(Sb[hs, :], Sf[hs, :], eng="act")

    og = P.sbuf("og", [128, 256], BF16); sg = P.sbuf("sg", [128, 256]); sq = P.sbuf("sq2", [128, 256])
    ss = P.sbuf("ss", [128, 2]); yt = [P.sbuf("yt%d" % i, [128, 256]) for i in range(2)]
    for ti in range(34):
        P.dma(og.ap, ogM[ti * 128:(ti + 1) * 128, :])
        P.act(sg.ap, og.ap, AF.Silu)
        o = oacc[:, ti, :]
        P.tt(sq.ap, o, o, ALU.mult)
        P.reduce(ss.ap, sq.ap.rearrange("p (h v) -> p h v", h=2), ALU.add)
        P.ts(ss.ap, ss.ap, 1.0 / 128.0, 1e-6, ALU.mult, ALU.add)
        P.act(ss.ap, ss.ap, AF.Sqrt); P.recip(ss.ap, ss.ap)
        y = yt[ti % 2]
        P.tt(y.ap.rearrange("p (h v) -> p h v", h=2), o.rearrange("p (h v) -> p h v", h=2), bc(ss.ap, [128, 2, 128], 2), ALU.mult)
        P.tt(y.ap.rearrange("p (h v) -> p h v", h=2), y.ap.rearrange("p (h v) -> p h v", h=2), bc(ng.ap, [128, 2, 128], 1), ALU.mult, eng="pool")
        P.tt(y.ap, y.ap, sg.ap, ALU.mult)
        P.dma(yo[ti * 128:(ti + 1) * 128, :], y.ap, queue="pool", nowaw=True)
    return P.finish([yo])


CW = 0.6065306597126334


def build_B3():
    P = Prog()
    X = lambda n, s, dt=F32: P.dram(n, s, dt, kind="ExternalInput")
    zrkv = X("zrkv", [TT, 3, 768], BF16)
    lrd = X("lr", [3, 3, 128, TT], BF16)
    murkv = X("murkv", [2, 768]); mulrd = X("mulr", [128, 3, 2])
    wupd = X("wup", [128, 256]); w0d = X("w0", [2, 1, 256]); aupd = X("aup", [128, 256]); a0d = X("a0", [2, 1, 256])
    gupd = X("gup", [128, 256]); vecsd = X("vecs", [5, 256]); mk = X("masks", [128, 4, 128]); idd = X("ident", [128, 128])
    yo = P.dram("yr", [TT, 256], F32, kind="ExternalOutput")
    pp = PsPool(P)
    masks = P.sbuf("masksb", [128, 4, 128]); P.dma(masks.ap, mk.ap)
    id32 = P.sbuf("id32", [128, 128]); P.dma(id32.ap, idd.ap)
    id16 = P.sbuf("id16", [128, 128], BF16); P.copy(id16.ap, id32.ap)
    ones1 = P.sbuf("ones1", [1, 128], BF16); P.memset(ones1.ap, 1.0)
    onec = P.sbuf("onec", [128, 1]); P.memset(onec.ap, 1.0)
    M_I = (2, 3); QS_I = (3, 2); QN_I = (0, 1); INC_I = (0, 1); EXC_I = (3, 2); SUF_I = (2, 3)
    maskM = P.sbuf("maskM", [128, 2, 128], BF16); maskQN = P.sbuf("maskQN", [128, 2, 256], BF16)
    for d in range(2):
        P.copy(maskM[:, d, :], masks[:, M_I[d], :])
        P.copy(maskQN[:, d, 0:128], masks[:, QS_I[d], :]); P.copy(maskQN[:, d, 128:256], masks[:, QN_I[d], :])
    murep = P.sbuf("murep", [128, 2, 768])
    P.dma(murep.ap, View(murkv, murkv.t.ap().partition_broadcast(128)))
    c0rep = P.sbuf("c0rep", [128, 768])
    P.tt(c0rep.ap, murep[:, 0, :], murep[:, 1, :], ALU.add)
    P.ts(c0rep.ap, c0rep.ap, -1.0, 1.0, ALU.mult, ALU.add)
    vrep = P.sbuf("vrep", [128, 5, 256]); P.dma(vrep.ap, View(vecsd, vecsd.t.ap().partition_broadcast(128)), queue="pool")
    wup = P.sbuf("wup16", [128, 256], BF16); aup = P.sbuf("aup16", [128, 256], BF16)
    w0 = P.sbuf("w016", [1, 2, 256], BF16); a0 = P.sbuf("a016", [1, 2, 256], BF16); gup = P.sbuf("gup16", [128, 256], BF16)
    P.dma(wup.ap, wupd.ap, queue="pool"); P.dma(aup.ap, aupd.ap, queue="pool")
    P.dma(w0.ap, w0d.ap.rearrange("d r c -> r d c"), queue="pool"); P.dma(a0.ap, a0d.ap.rearrange("d r c -> r d c"), queue="pool")
    P.dma(gup.ap, gupd.ap, queue="pool")
    P.tag = "lowrank"
    mulr = P.sbuf("mulrs", [128, 3, 2]); P.dma(mulr.ap, mulrd.ap)
    clr = P.sbuf("clr", [128, 3])
    P.tt(clr.ap, mulr[:, :, 0], mulr[:, :, 1], ALU.add); P.ts(clr.ap, clr.ap, -1.0, 1.0, ALU.mult, ALU.add)
    st = [P.sbuf("lrst%d" % i, [128, 544], BF16) for i in range(6)]
    LRs = [P.sbuf("LRs%d" % i, [128, TT], BF16) for i in range(3)]
    LRg = LRs[2]
    tl = P.sbuf("lrt", [128, 544])
    kk_ = 0
    for i in range(3):
        fn = (AF.Tanh, AF.Identity, AF.Sigmoid)[i]
        for b0 in range(0, TT, 544):
            bs = slice(b0, b0 + 544)
            sb = [st[(kk_ % 2) * 3 + j] for j in range(3)]; kk_ += 1
            for j in range(3):
                P.dma(sb[j].ap, lrd[i, j][:, bs], queue="sp" if j % 2 else "pool")
            P.ts(tl.ap, sb[0].ap, clr[:, i:i + 1], None, ALU.mult)
            P.stt(tl.ap, sb[1].ap, mulr[:, i, 0:1], tl.ap, ALU.mult, ALU.add)
            P.stt(tl.ap, sb[2].ap, mulr[:, i, 1:2], tl.ap, ALU.mult, ALU.add)
            P.act(LRs[i][:, bs], tl.ap, fn)
    yacc = P.sbuf("yacc", [128, 34, 256]); racc = P.sbuf("racc", [128, 34, 4])
    P.memset(yacc.ap, 0.0, eng="pool"); P.memset(racc.ap, 0.0, eng="pool")
    S16 = P.sbuf("S16", [64, 8, 64], BF16); P.memset(S16.ap, 0.0)
    zb = [P.sbuf("zb%d" % i, [128, 3, 768], BF16) for i in range(2)]
    zv = zrkv.ap.rearrange("(n p) j c -> n p j c", p=128)

    def mk_tmp(tag):
        f = lambda nm, sh, dt=F32: P.sbuf(nm + tag, sh, dt)
        return dict(m1=f("m1", [128, 768]), m2=f("m2", [128, 768]),
                    sw=f("sw", [128, 256]), aa=f("aa", [128, 256]), kk=f("kk", [128, 256]), t1=f("t1", [128, 256]),
                    keff=f("keff", [128, 256]), bb=f("bb", [128, 256]), ss=f("ss", [128, 4]),
                    EA=f("EA", [128, 256]), ER=f("ER", [128, 256]), EI=f("EI", [128, 256]), ES=f("ES", [128, 256]),
                    Ap16=f("Ap16", [128, 256], BF16), Bm16=f("Bm16", [128, 256], BF16), Km16=f("Km16", [128, 256], BF16))

    def mk_out(tag):
        f = lambda nm, sh, dt=F32: P.sbuf(nm + tag, sh, dt)
        return dict(V16=f("V16", [128, 256], BF16), Ap32=f("Ap32", [128, 256]), Rp16=f("Rp16", [128, 256], BF16),
                    Bc16=f("Bc16", [128, 256], BF16), Kc16=f("Kc16", [128, 256], BF16), Wc=f("Wc", [64, 4]),
                    FTAR=f("FTAR", [128, 2, 256], BF16), FTBK=f("FTBK", [128, 2, 256], BF16))
    PBt = {d: mk_tmp("_%d" % d) for d in range(2)}
    PBo = {(d, par): mk_out("_%d%d" % (d, par)) for d in range(2) for par in range(2)}
    PB = {(d, par): {**PBt[d], **PBo[(d, par)]} for d in range(2) for par in range(2)}

    def prep_gen(d, ti, par):
        T_ = PB[(d, par)]
        P.tag = "prep0"
        zin = zb[(d + par) % 2]
        P.dma(zin.ap, zv[ti])
        m1, m2 = T_["m1"], T_["m2"]
        P.tt(m1.ap, zin[:, 0, :], c0rep.ap, ALU.mult, eng="pool")
        P.tt(m2.ap, zin[:, 1, :], murep[:, 0, :], ALU.mult, eng="pool")
        P.tt(m1.ap, m1.ap, m2.ap, ALU.add, eng="pool")
        P.tt(m2.ap, zin[:, 2, :], murep[:, 1, :], ALU.mult, eng="pool")
        P.tt(m1.ap, m1.ap, m2.ap, ALU.add, eng="pool")
        r = m1[:, 0:256]; k = m1[:, 256:512]; v = m1[:, 512:768]
        P.copy(T_["V16"].ap, v, eng="pool")
        tsl = slice(ti * 128, (ti + 1) * 128)
        pw = pp.get()
        ds = slice(64 * d, 64 * d + 64)
        P.matmul(pw[:, 0:256], LRs[0][ds, tsl], wup[ds, :], start=True, stop=False)
        P.matmul(pw[:, 0:256], ones1.ap, w0[:, d, :], start=False, stop=True)
        P.act(T_["sw"].ap, pw[:, 0:256], AF.Sigmoid)
        pa = pp.get()
        P.matmul(pa[:, 0:256], LRs[1][ds, tsl], aup[ds, :], start=True, stop=False)
        P.matmul(pa[:, 0:256], ones1.ap, a0[:, d, :], start=False, stop=True)
        P.act(T_["aa"].ap, pa[:, 0:256], AF.Sigmoid)
        yield
        P.tag = "prep1"
        sw = T_["sw"]
        pci = pp.get(); pce = pp.get(); pcs = pp.get()
        P.matmul(pci[:, 0:256], masks[:, INC_I[d], :], sw.ap)
        P.matmul(pce[:, 0:256], masks[:, EXC_I[d], :], sw.ap)
        P.matmul(pcs[:, 0:256], masks[:, SUF_I[d], :], sw.ap)
        P.act(T_["EA"].ap, pce[:, 0:256], AF.Exp, scale=-CW)
        P.act(T_["ER"].ap, pci[:, 0:256], AF.Exp, scale=-CW)
        P.act(T_["EI"].ap, pci[:, 0:256], AF.Exp, scale=CW)
        P.act(T_["ES"].ap, pcs[:, 0:256], AF.Exp, scale=-CW)
        pwc = pp.get()
        for h in range(4):
            P.matmul(pwc[0:64, h * 8:h * 8 + 1], sw[:, h * 64:(h + 1) * 64], onec.ap)
        P.act(T_["Wc"].ap, pwc[0:64, 0:32].rearrange("p (h e) -> p h e", e=8)[:, :, 0], AF.Exp, scale=-CW)
        kk, t1, ss = T_["kk"], T_["t1"], T_["ss"]
        P.tt(kk.ap, k, vrep[:, 0, :], ALU.mult)
        P.tt(t1.ap, kk.ap, kk.ap, ALU.mult)
        P.reduce(ss.ap, t1.ap.rearrange("p (h c) -> p h c", h=4), ALU.add)
        P.ts(ss.ap, ss.ap, 1e-12, None, ALU.add)
        P.act(ss.ap, ss.ap, AF.Sqrt)
        yield
        P.tag = "prep2"
        P.recip(ss.ap, ss.ap)
        P.tt(kk.ap.rearrange("p (h c) -> p h c", h=4), kk.ap.rearrange("p (h c) -> p h c", h=4), bc(ss.ap, [128, 4, 64], 2), ALU.mult)
        keff, bb = T_["keff"], T_["bb"]
        P.stt(t1.ap, T_["aa"].ap, -1.0, vrep[:, 1, :], ALU.add, ALU.mult)
        P.stt(keff.ap, t1.ap, 1.0, k, ALU.add, ALU.mult)
        P.tt(bb.ap, kk.ap, T_["aa"].ap, ALU.mult)
        yield
        P.tag = "prep3"
        P.stt(T_["Ap32"].ap, kk.ap, -1.0, T_["EA"].ap, ALU.mult, ALU.mult)
        P.copy(T_["Ap16"].ap, T_["Ap32"].ap, eng="pool")
        P.tt(T_["Rp16"].ap, r, T_["ER"].ap, ALU.mult)
        P.tt(T_["Bm16"].ap, bb.ap, T_["EI"].ap, ALU.mult, eng="pool")
        P.tt(T_["Km16"].ap, keff.ap, T_["EI"].ap, ALU.mult)
        yield
        P.tag = "prep4"
        P.tt(T_["Bc16"].ap, bb.ap, T_["ES"].ap, ALU.mult, eng="pool")
        P.tt(T_["Kc16"].ap, keff.ap, T_["ES"].ap, ALU.mult)
        P.tt(t1.ap, r, keff.ap, ALU.mult)
        P.tt(t1.ap, t1.ap, vrep[:, 2, :], ALU.mult)
        P.reduce(ss.ap, t1.ap.rearrange("p (h c) -> p h c", h=4), ALU.add)
        P.tt(racc[:, ti, :], racc[:, ti, :], ss.ap, ALU.add)
        yield
        P.tag = "prep5"
        for (dstn, srcs) in (("FTAR", ("Ap16", "Rp16")), ("FTBK", ("Bm16", "Km16"))):
            pt = pp.get(); ptv = pt.ap.bitcast(BF16)
            for pair in range(2):
                for qi in range(2):
                    c0 = (pair * 2 + qi) * 128
                    P.transpose(ptv[:, c0:c0 + 128], T_[srcs[qi]][:, pair * 128:(pair + 1) * 128], id16.ap)
            P.copy(T_[dstn].ap.rearrange("p a c -> p (a c)"), ptv[:, 0:512], eng="act")

    UB = []
    for u in range(8):
        f = lambda nm, sh, dt=F32: P.sbuf("%s_u%d" % (nm, u), sh, dt)
        UB.append(dict(Mab=f("Mab", [128, 128]), Q32=f("Q32", [128, 128]), Nrb=f("Nrb", [128, 128], BF16),
                       T3=f("T3", [128, 256], BF16), Z32=f("Z32", [128, 128]), Z16=f("Z16", [128, 128], BF16),
                       MQ=[f("MQ%d" % i, [128, 256]) for i in range(2)],
                       G16=f("G16", [64, 64], BF16), RT=f("RT", [64, 128], BF16)))
    e1s = [P.sbuf("e1_%d" % i, [128, 256]) for i in range(2)]; e2 = P.sbuf("e2_", [128, 256])
    es = P.sbuf("es_", [128, 4]); er = P.sbuf("er_", [128, 4])
    yt = [P.sbuf("yrt%d" % i, [128, 256]) for i in range(2)]
    zv2 = [P.sbuf("zv2_%d" % i, [128, 3, 256], BF16) for i in range(2)]
    H4 = lambda vw: vw.rearrange("p (h c) -> p h c", h=4)
    epi_n = [0]

    def epilogue(ti):
        P.tag = "epi"
        i2 = epi_n[0] % 2; epi_n[0] += 1
        zin = zv2[i2]; e1 = e1s[i2]
        P.dma(zin.ap, zv[ti][:, :, 512:768])
        P.tt(e1.ap, zin[:, 0, :], c0rep[:, 512:768], ALU.mult, eng="pool")
        P.tt(e2.ap, zin[:, 1, :], murep[:, 0, 512:768], ALU.mult, eng="pool")
        P.tt(e1.ap, e1.ap, e2.ap, ALU.add, eng="pool")
        P.tt(e2.ap, zin[:, 2, :], murep[:, 1, 512:768], ALU.mult, eng="pool")
        P.tt(e1.ap, e1.ap, e2.ap, ALU.add, eng="pool")
        P.tt(H4(e1.ap), H4(e1.ap), bc(racc[:, ti, :], [128, 4, 64], 2), ALU.mult, eng="pool")
        y = yacc[:, ti, :]
        P.reduce(es.ap, H4(y), ALU.add)
        P.ts(es.ap, es.ap, 1.0 / 64.0, None, ALU.mult)
        P.tt(H4(y), H4(y), bc(es.ap, [128, 4, 64], 2), ALU.subtract)
        o = yt[i2]
        P.tt(o.ap, y, y, ALU.mult)
        P.reduce(er.ap, H4(o.ap), ALU.add)
        P.ts(er.ap, er.ap, 1.0 / 64.0, 64e-5, ALU.mult, ALU.add)
        P.act(er.ap, er.ap, AF.Sqrt); P.recip(er.ap, er.ap)
        P.tt(H4(o.ap), H4(y), bc(er.ap, [128, 4, 64], 2), ALU.mult)
        P.tt(o.ap, o.ap, vrep[:, 3, :], ALU.mult)
        P.tt(o.ap, o.ap, vrep[:, 4, :], ALU.add)
        P.tt(o.ap, o.ap, e1.ap, ALU.add)
        pg = pp.get()
        P.matmul(pg[:, 0:256], LRg[:, ti * 128:(ti + 1) * 128], gup.ap)
        P.tt(o.ap, o.ap, pg[:, 0:256], ALU.mult)
        P.dma(yo[ti * 128:(ti + 1) * 128, :], o.ap, queue="sp", nowaw=True)

    orders = [list(range(34)), [1, 0] + list(range(33, 1, -1))]

    def start_prep(step):
        return [prep_gen(d, orders[d][step], step % 2) for d in range(2)]

    def advance(gens):
        for g in gens:
            next(g, None)
    for g in start_prep(0):
        for _ in g:
            pass
    for step in range(34):
        par = step % 2
        units = []
        for d in range(2):
            ti = orders[d][step]
            for h in range(4):
                units.append((d, ti, h, PB[(d, par)], UB[d * 4 + h]))
        gens = start_prep(step + 1) if step + 1 < 34 else []
        P.tag = "s1"
        for (d, ti, h, T_, U) in units:
            pair, hs = h // 2, slice(64 * (h % 2), 64 * (h % 2) + 64)
            AR, BK = T_["FTAR"], T_["FTBK"]
            p1 = pp.get(); p2 = pp.get(); p3 = pp.get()
            P.matmul(p1[:, 0:128], AR[hs, pair, 0:128], BK[hs, pair, 0:128])
            P.matmul(p2[:, 0:256], BK[hs, pair, 0:128], AR[hs, pair, 0:256])
            P.matmul(p3[:, 0:256], BK[hs, pair, 128:256], AR[hs, pair, 0:256])
            P.tt(U["Mab"].ap, p1[:, 0:128], masks[:, M_I[d], :], ALU.mult)
            P.tt(U["Q32"].ap, p2[:, 0:128], masks[:, QS_I[d], :], ALU.mult)
            P.tt(U["Nrb"].ap, p2[:, 128:256], maskQN[:, d, 128:256], ALU.mult)
            P.tt(U["T3"].ap, p3[:, 0:256], maskQN[:, d, :], ALU.mult)
        P.tag = "s2"
        for (d, ti, h, T_, U) in units:
            hc = slice(h * 64, (h + 1) * 64)
            px = pp.get()
            P.matmul(px[:, 0:64], U["T3"][:, 0:128], T_["V16"][:, hc])
            P.copy(U["Z32"][:, 0:64], px[:, 0:64], eng="act")
            P.copy(U["Z32"][:, 64:128], T_["Ap32"][:, hc], eng="pool")
        for lv in range(7):
            P.tag = "lv%d" % lv
            for (d, ti, h, T_, U) in units:
                if lv == 0:
                    Mv, Qv = U["Mab"].ap, U["Q32"].ap
                else:
                    mq = U["MQ"][(lv - 1) % 2]
                    Mv, Qv = mq[:, 0:128], mq[:, 128:256]
                pa = pp.get()
                P.matmul(pa[:, 0:128], Qv, U["Z32"].ap)
                if lv < 6:
                    pq = pp.get()
                    P.matmul(pq[:, 0:128], Qv, Mv)
                    P.matmul(pq[:, 128:256], Mv, Qv)
                P.tt(U["Z32"].ap, U["Z32"].ap, pa[:, 0:128], ALU.add)
                if lv < 6:
                    P.copy(U["MQ"][lv % 2].ap, pq[:, 0:256], eng="act")
            advance(gens)
        P.tag = "z16"
        for (d, ti, h, T_, U) in units:
            P.copy(U["Z16"].ap, U["Z32"].ap, eng="pool")
        P.tag = "s3"
        for (d, ti, h, T_, U) in units:
            hc = slice(h * 64, (h + 1) * 64)
            pg = pp.get(); pr = pp.get()
            P.matmul(pg[0:64, 0:64], U["Z16"][:, 64:128], T_["Bc16"][:, hc])
            P.matmul(pr[0:64, 0:128], U["Z16"][:, 64:128], U["Nrb"].ap, start=True, stop=False)
            P.matmul(pr[0:64, 0:128], T_["Rp16"][:, hc], id16.ap, start=False, stop=True)
            P.stt(U["G16"].ap, id32[0:64, 0:64], T_["Wc"][:, h:h + 1], pg[0:64, 0:64], ALU.mult, ALU.add)
            P.copy(U["RT"].ap, pr[0:64, 0:128], eng="act")
        P.tag = "post"
        for (d, ti, h, T_, U) in units:
            hc = slice(h * 64, (h + 1) * 64); ch = d * 4 + h
            ph = pp.get(); py = pp.get()
            P.matmul(ph[0:64, 0:64], T_["Bc16"][:, hc], U["Z16"][:, 0:64], start=True, stop=False)
            P.matmul(ph[0:64, 0:64], T_["Kc16"][:, hc], T_["V16"][:, hc], start=False, stop=False)
            P.matmul(ph[0:64, 0:64], U["G16"].ap, S16[:, ch, :], start=False, stop=True)
            P.matmul(py[:, 0:64], U["Nrb"].ap, U["Z16"][:, 0:64], start=True, stop=False)
            P.matmul(py[:, 0:64], U["T3"][:, 128:256], T_["V16"][:, hc], start=False, stop=False)
            P.matmul(py[:, 0:64], U["RT"].ap, S16[:, ch, :], start=False, stop=True)
            P.copy(S16[:, ch, :], ph[0:64, 0:64], eng="act")
            P.tt(yacc[:, ti, hc], yacc[:, ti, hc], py[:, 0:64], ALU.add)
        for t_done in range(34):
            bpos = (1, 0)[t_done] if t_done < 2 else 35 - t_done
            if max(t_done, bpos) == step:
                epilogue(t_done)
    return P.finish([yo])


_PROGS = {}


_BUILDERS = {"A": build_A, "B1": build_B1, "B2": build_B2, "B3": build_B3, "C1": build_C1, "C2": build_C2}
_TIMES = []


def _run(name, in_maps):
    import time as _t
    t0 = _t.time()
    nc = _BUILDERS[name]()
    t1 = _t.time()
    res = run_bass_kernel_spmd(nc, in_maps, core_ids=list(range(8)))
    _TIMES.append((name, round(t1 - t0, 1), round(_t.time() - t1, 1)))
    return res.results


def _shift3(a):
    p = np.zeros_like(a); n = np.zeros_like(a)
    p[1:256] = a[0:255]; p[257:] = a[256:-1]
    n[0:255] = a[1:256]; n[256:-1] = a[257:]
    return np.stack([a, p, n], 1)


def _colv(v, n):
    return np.ascontiguousarray(np.asarray(v, np.float32).reshape(n, 128).T)


def _rope_tables():
    inv = (10000.0 ** (-np.arange(8, dtype=np.float32) / 8)).astype(np.float32)
    row = np.repeat(np.arange(64, dtype=np.float32), 64); col = np.tile(np.arange(64, dtype=np.float32), 64)
    ang = np.concatenate([row[:, None] * inv, col[:, None] * inv], -1).astype(np.float32)
    cos, sin = np.cos(ang).astype(np.float32), np.sin(ang).astype(np.float32)
    C = np.ones((96, TT), np.float32); S = np.zeros((96, TT), np.float32)
    C[64:80, 256:] = cos.T; C[80:96, 256:] = cos.T
    S[64:80, 256:] = -sin.T; S[80:96, 256:] = sin.T
    return C, S


def _tok(h):
    return np.concatenate([np.arange(128 * h, 128 * h + 128), 256 + np.arange(2048 * h, 2048 * h + 2048)])


def kernel(x, c, ctx, c_ctx, ada_w, ada_b, w_in, gla_gk_up, gla_gk_b, gla_norm_g, mla_q_norm_g, mla_kv_norm_g,
           mla_w_uq, mla_w_ukv, rwkv_shift_mu, rwkv_w0, rwkv_w_up, rwkv_a0, rwkv_a_up, rwkv_g_up, rwkv_k_k,
           rwkv_k_a, rwkv_r_k, rwkv_ln_g, rwkv_ln_b, w_branch, w_out, ln1_g, ln1_b, mlp_w1, mlp_w2, ln2_g, ln2_b,
           _layers=4):
    f32 = lambda a: np.ascontiguousarray(np.asarray(a, np.float32))
    xf = np.concatenate([f32(ctx), f32(x)], 1)
    C, S = _rope_tables()
    s_ = np.arange(128)[:, None]; t_ = np.arange(128)[None, :]
    masks = np.stack([s_ <= t_, s_ >= t_, s_ > t_, s_ < t_], 1).astype(np.float32)
    toks = [_tok(0), _tok(1)]
    for l in range(_layers):
        ims = []
        for cid in range(8):
            b, h = cid // 2, cid % 2
            c2 = np.stack([f32(c)[b], f32(c_ctx)], -1).reshape(8, 128, 2).transpose(1, 0, 2).copy()
            ims.append({"xT": np.ascontiguousarray(xf[b][toks[h]].T), "c2": c2, "ada_w": f32(ada_w[l]),
                        "ada_bT": _colv(ada_b[l], 48), "w_in": f32(w_in[l])})
        rA = _run("A", ims)
        z = np.empty((4, TT, 7232), NPBF)
        for cid in range(8):
            z[cid // 2][toks[cid % 2]] = rA[cid]["zT"].T
        mods = [rA[2 * b]["mod"] for b in range(4)]
        ims = []
        for cid in range(8):
            b, h0 = cid // 2, 4 * (cid % 2)
            zb = z[b]; kr = zb[:, 2208:2240]
            wq = f32(mla_w_uq[l])[:, h0 * 96:(h0 + 4) * 96]; w3 = wq.reshape(384, 4, 96)
            wqr = np.concatenate([w3[:, :, 0:64], w3[:, :, 80:96], w3[:, :, 64:80]], -1).reshape(384, 384)
            k3 = f32(mla_w_ukv[l]).reshape(256, 8, 128)[:, h0:h0 + 4]
            ims.append({"qdT": np.ascontiguousarray(zb[:, 1568:1952].T), "kvdT": np.ascontiguousarray(zb[:, 1952:2208].T),
                        "krT": np.ascontiguousarray(kr.T), "krrT": np.ascontiguousarray(np.concatenate([kr[:, 16:32], kr[:, 0:16]], 1).T),
                        "wq": np.ascontiguousarray(wq), "wqr": np.ascontiguousarray(wqr),
                        "wkn": np.ascontiguousarray(k3[:, :, 0:64].reshape(256, 256)),
                        "wkv": np.ascontiguousarray(k3[:, :, 64:128].reshape(256, 256)),
                        "qg": _colv(mla_q_norm_g[l], 3), "kg": _colv(mla_kv_norm_g[l], 2), "ropeC": C, "ropeS": S})
        rB1 = _run("B1", ims)
        yb = np.empty((4, TT, 512), NPBF)
        for cid in range(8):
            yb[cid // 2][:, 256 * (cid % 2):256 * (cid % 2) + 256] = rB1[cid]["ybT"].T
        ims = []
        for cid in range(8):
            b, h0 = cid // 2, 2 * (cid % 2)
            zb = z[b]; cs = slice(h0 * 64, h0 * 64 + 128)
            gu = f32(gla_gk_up[l]); gbv = f32(gla_gk_b[l])
            ims.append({"qT": np.ascontiguousarray(zb[:, 0:256][:, cs].T), "kT": np.ascontiguousarray(zb[:, 256:512][:, cs].T),
                        "kM": np.ascontiguousarray(zb[:, 256:512][:, cs]),
                        "vM": np.ascontiguousarray(zb[:, 512 + h0 * 128:512 + h0 * 128 + 256]),
                        "ogM": np.ascontiguousarray(zb[:, 1056 + h0 * 128:1056 + h0 * 128 + 256]),
                        "gdT0": np.ascontiguousarray(zb[:, 1024:1040].T), "gdT1": np.ascontiguousarray(zb[:, 1040:1056].T),
                        "gup0": np.ascontiguousarray(gu[0][:, cs]), "gup1": np.ascontiguousarray(gu[1][:, cs]),
                        "gkb0": np.ascontiguousarray(gbv[0][None, cs]), "gkb1": np.ascontiguousarray(gbv[1][None, cs]),
                        "ng": f32(gla_norm_g[l])[None], "masks": masks})
        rB2 = _run("B2", ims)
        ya = np.empty((4, TT, 512), NPBF)
        for cid in range(8):
            ya[cid // 2][:, 256 * (cid % 2):256 * (cid % 2) + 256] = rB2[cid]["ya"].astype(NPBF)
        ims = []
        mu = f32(rwkv_shift_mu[l]); ident = np.eye(128, dtype=np.float32)
        for cid in range(8):
            b, h0 = cid // 2, 4 * (cid % 2)
            zb = z[b]; base = 2240; cs = slice(h0 * 64, h0 * 64 + 256)
            sel = np.concatenate([np.arange(base + o + h0 * 64, base + o + h0 * 64 + 256) for o in (0, 512, 1024)])
            lr = np.stack([_shift3(zb[:, base + o:base + o + 128]).transpose(1, 2, 0) for o in (1536, 1664, 1792)], 0)
            vecs = np.stack([f32(v_[l])[cs] for v_ in (rwkv_k_k, rwkv_k_a, rwkv_r_k, rwkv_ln_g, rwkv_ln_b)], 0)
            ims.append({"zrkv": np.ascontiguousarray(_shift3(zb[:, sel])), "lr": np.ascontiguousarray(lr),
                        "murkv": np.ascontiguousarray(mu[:, sel - base]),
                        "mulr": np.ascontiguousarray(np.stack([mu[:, o:o + 128].T for o in (1536, 1664, 1792)], 1)),
                        "wup": np.ascontiguousarray(np.concatenate([f32(rwkv_w_up[l])[0][:, cs], f32(rwkv_w_up[l])[1][:, cs]], 0)),
                        "w0": np.ascontiguousarray(f32(rwkv_w0[l])[:, None, cs]),
                        "aup": np.ascontiguousarray(np.concatenate([f32(rwkv_a_up[l])[0][:, cs], f32(rwkv_a_up[l])[1][:, cs]], 0)),
                        "a0": np.ascontiguousarray(f32(rwkv_a0[l])[:, None, cs]),
                        "gup": np.ascontiguousarray(f32(rwkv_g_up[l])[:, cs]), "vecs": np.ascontiguousarray(vecs),
                        "masks": masks, "ident": ident})
        rB3 = _run("B3", ims)
        yr = np.empty((4, TT, 512), NPBF)
        for cid in range(8):
            yr[cid // 2][:, 256 * (cid % 2):256 * (cid % 2) + 256] = rB3[cid]["yr"].astype(NPBF)
        ims = []
        for cid in range(8):
            b, h = cid // 2, cid % 2
            y = np.concatenate([ya[b], yb[b], yr[b]], 1)[toks[h]]
            ims.append({"xT": np.ascontiguousarray(xf[b][toks[h]].T), "yT": np.ascontiguousarray(y.T),
                        "gT": np.ascontiguousarray(rA[cid]["zT"][4160:]), "mod": mods[b],
                        "w_branch": f32(w_branch[l]).reshape(1536, 1024), "w_out": f32(w_out[l]),
                        "ln_g": _colv(ln1_g[l], 8), "ln_b": _colv(ln1_b[l], 8)})
        rC1 = _run("C1", ims)
        ims = [{"xT": rC1[cid]["x1T"], "mod": mods[cid // 2], "w1": f32(mlp_w1[l]), "w2": f32(mlp_w2[l]),
                "ln_g": _colv(ln2_g[l], 8), "ln_b": _colv(ln2_b[l], 8)} for cid in range(8)]
        rC2 = _run("C2", ims)
        for cid in range(8):
            xf[cid // 2][toks[cid % 2]] = rC2[cid]["x2T"].T
    return np.ascontiguousarray(xf[:, 256:, :])
```

```python
import numpy as np
import concourse.bass as bass
import concourse.mybir as mybir

F32 = mybir.dt.float32
BF16 = mybir.dt.bfloat16
AF = mybir.ActivationFunctionType
ALU = mybir.AluOpType
AX = mybir.AxisListType

ENGS = ("pe", "act", "dve", "pool", "sp")


class Buf:
    def __init__(self, prog, t, name, space):
        self.prog = prog
        self.t = t
        self.name = name
        self.space = space
        self.last_write = None
        self.reads = {}
        self.war = {}
        self.dsem = None
        self.dcount = 0

    def __getitem__(self, idx):
        return View(self, self.t.__getitem__(idx))

    @property
    def ap(self):
        return View(self, self.t[:] if self.space != "dram" else self.t.ap())


class View:
    def __init__(self, buf, ap):
        self.buf = buf
        self.ap = ap

    def __getitem__(self, idx):
        return View(self.buf, self.ap.__getitem__(idx))

    def rearrange(self, *a, **k):
        return View(self.buf, self.ap.rearrange(*a, **k))

    def bitcast(self, dt):
        return View(self.buf, self.ap.bitcast(dt))


class Prog:
    def __init__(self, same_engine_sync=True):
        self.nc = bass.Bass("TRN2", target_bir_lowering=False)
        self.ops = {e: [] for e in ENGS}
        self.count = {e: 0 for e in ENGS}
        self.sems = {}
        self.waited = {e: {} for e in ENGS}
        self.same_engine_sync = same_engine_sync
        self.nbuf = 0
        for e in ("pe", "act", "dve", "pool"):
            self.sems[e] = self.nc.alloc_semaphore(name="s_" + e)
        self.final_tokens = []
        self.tag = ""
        self.fuse_waits = True

    def sbuf(self, name, shape, dt=F32):
        t = self.nc.alloc_sbuf_tensor(name, list(shape), dt)
        return Buf(self, t, name, "sbuf")

    def psum(self, name, shape, dt=F32):
        t = self.nc.alloc_psum_tensor(name, list(shape), dt)
        return Buf(self, t, name, "psum")

    def dram(self, name, shape, dt=F32, kind="Internal"):
        t = self.nc.dram_tensor(name, list(shape), dt, kind=kind)
        return Buf(self, t, name, "dram")

    def sub(self, parent, name):
        b = Buf(self, parent.t, name, parent.space)
        return b

    def _need(self, eng, tok, waits):
        if tok is None:
            return
        key, val, teng = tok
        if teng == eng:
            if eng in ("pe", "sp"):
                return
            if not self.same_engine_sync:
                return
        if self.waited[eng].get(key, 0) >= val:
            return
        cur = waits.get(key, 0)
        if val > cur:
            waits[key] = val

    def _deps(self, eng, reads, writes, pe_accum=False):
        waits = {}
        for v in reads:
            self._need(eng, v.buf.last_write, waits)
        for v in writes:
            b = v.buf
            if not (pe_accum and b.last_write is not None and b.last_write[2] == "pe"):
                self._need(eng, b.last_write, waits)
            for src in (b.reads, b.war):
                for reng, tok in src.items():
                    if reng == eng and eng != "sp":
                        continue
                    self._need(eng, tok, waits)
        for k, v in waits.items():
            self.waited[eng][k] = v
        return list(waits.items())

    def _commit(self, eng, tok, reads, writes):
        for v in reads:
            v.buf.reads[eng if eng != "sp" else ("sp", tok[0])] = tok
        for v in writes:
            v.buf.last_write = tok
            if v.buf.reads:
                v.buf.war = v.buf.reads
                v.buf.reads = {}

    def op(self, eng, fn, reads, writes, pe_accum=False):
        waits = self._deps(eng, reads, writes, pe_accum)
        self.count[eng] += 1
        tok = (eng, self.count[eng], eng)
        self.ops[eng].append((waits, fn, (eng, 1), self.tag))
        self._commit(eng, tok, reads, writes)
        return tok

    def dma(self, out, in_, queue="sp", nowaw=False, **kw):
        b = out.buf
        if b.dsem is None:
            b.dsem = "d_" + b.name + str(self.nbuf)
            self.nbuf += 1
            self.sems[b.dsem] = self.nc.alloc_semaphore(name=b.dsem[:30])
        waits = {}
        self._need(queue, in_.buf.last_write, waits)
        if not (nowaw and b.last_write is not None and b.last_write[0] == b.dsem):
            self._need(queue, b.last_write, waits)
        for src in (b.reads, b.war):
            for reng, tok in src.items():
                self._need(queue, tok, waits)
        for k, v in waits.items():
            self.waited[queue][k] = v
        b.dcount += 16
        tok = (b.dsem, b.dcount, "dma")
        oap, iap = out.ap, in_.ap
        fn = lambda e: e.dma_start(out=oap, in_=iap, **kw)
        if queue != "sp":
            self.count[queue] += 0
        self.ops[queue].append((list(waits.items()), fn, (b.dsem, 16), self.tag))
        in_.buf.reads[("dma", b.dsem)] = tok
        b.last_write = tok
        if b.reads:
            b.war = b.reads
            b.reads = {}
        return tok

    def matmul(self, out, lhsT, rhs, start=True, stop=True, **kw):
        o, l, r = out.ap, lhsT.ap, rhs.ap
        return self.op("pe", lambda e: e.matmul(o, l, r, start=start, stop=stop, **kw),
                       [lhsT, rhs], [out], pe_accum=not start)

    def transpose(self, out, in_, ident):
        o, i, d = out.ap, in_.ap, ident.ap
        return self.op("pe", lambda e: e.transpose(o, i, d), [in_, ident], [out])

    def act(self, out, in_, func, bias=None, scale=None, accum_out=None, eng="act"):
        reads = [in_]
        kw = {}
        if bias is not None:
            if isinstance(bias, View):
                reads.append(bias); kw["bias"] = bias.ap
            else:
                kw["bias"] = bias
        if scale is not None:
            if isinstance(scale, View):
                reads.append(scale); kw["scale"] = scale.ap
            else:
                kw["scale"] = scale
        writes = [out]
        if accum_out is not None:
            writes.append(accum_out); kw["accum_out"] = accum_out.ap
        o, i = out.ap, in_.ap
        return self.op(eng, lambda e: e.activation(o, i, func, **kw), reads, writes)

    def tt(self, out, in0, in1, op, eng="dve"):
        o, a, b = out.ap, in0.ap, in1.ap
        return self.op(eng, lambda e: e.tensor_tensor(o, a, b, op), [in0, in1], [out])

    def ts(self, out, in0, s1, s2, op0, op1=None, eng="dve", accum_out=None):
        reads = [in0]
        a1 = s1.ap if isinstance(s1, View) else s1
        a2 = s2.ap if isinstance(s2, View) else s2
        if isinstance(s1, View): reads.append(s1)
        if isinstance(s2, View): reads.append(s2)
        o, a = out.ap, in0.ap
        writes = [out]
        kw = {}
        if accum_out is not None:
            writes.append(accum_out); kw["accum_out"] = accum_out.ap
        if op1 is None:
            return self.op(eng, lambda e: e.tensor_scalar(o, a, a1, None, op0, **kw), reads, writes)
        return self.op(eng, lambda e: e.tensor_scalar(o, a, a1, a2, op0, op1, **kw), reads, writes)

    def stt(self, out, in0, scalar, in1, op0, op1, eng="dve"):
        reads = [in0, in1]
        s = scalar.ap if isinstance(scalar, View) else scalar
        if isinstance(scalar, View): reads.append(scalar)
        o, a, b = out.ap, in0.ap, in1.ap
        return self.op(eng, lambda e: e.scalar_tensor_tensor(o, a, s, b, op0, op1), reads, [out])

    def copy(self, out, in_, eng="dve"):
        o, i = out.ap, in_.ap
        if eng == "act":
            return self.op(eng, lambda e: e.copy(o, i), [in_], [out])
        return self.op(eng, lambda e: e.tensor_copy(o, i), [in_], [out])

    def memset(self, out, val, eng="dve"):
        o = out.ap
        return self.op(eng, lambda e: e.memset(o, val), [], [out])

    def reduce(self, out, in_, op, axis=AX.X, eng="dve"):
        o, i = out.ap, in_.ap
        return self.op(eng, lambda e: e.tensor_reduce(o, i, axis, op), [in_], [out])

    def recip(self, out, in_):
        o, i = out.ap, in_.ap
        return self.op("dve", lambda e: e.reciprocal(o, i), [in_], [out])

    def finish(self, out_bufs):
        for b in out_bufs:
            self.final_tokens.append(b.last_write)
        nc = self.nc
        with nc.Block() as block:
            def emit(eng_name):
                def body(e):
                    for waits, fn, inc, tag in self.ops[eng_name]:
                        fused = waits[-1] if (self.fuse_waits and waits) else None
                        for k, v in (waits[:-1] if fused else waits):
                            w = e.wait_ge(self.sems[k], v)
                            if tag:
                                w.annotate("W:" + tag + ":" + str(k))
                        ins = fn(e)
                        if fused:
                            ins._wait_ge(self.sems[fused[0]], fused[1])
                        ins.then_inc(self.sems[inc[0]], inc[1])
                        if tag:
                            ins.annotate(tag)
                    if eng_name == "sp":
                        for tok in self.final_tokens:
                            e.wait_ge(self.sems[tok[0]], tok[1])
                return body
            block.tensor(emit("pe"))
            block.scalar(emit("act"))
            block.vector(emit("dve"))
            block.gpsimd(emit("pool"))
            block.sync(emit("sp"))
        return nc


from concourse.bass_utils import run_bass_kernel_spmd
import ml_dtypes

NPBF = ml_dtypes.bfloat16
D = 1024
NT = 2176
BLKS = [(0, 128, 1)] + [(128 + 512 * i, 512, 0) for i in range(4)]
ALPHA = 8.0 ** 0.25


def bc(view, shape, axis):
    return View(view.buf, view.ap.unsqueeze(axis).broadcast_to(list(shape)))


class PsPool:
    def __init__(self, P, n=8):
        self.t = [P.psum("psb%d" % i, [128, 512], F32) for i in range(n)]
        self.i = 0

    def get(self):
        b = self.t[self.i % len(self.t)]
        self.i += 1
        return b


def fm_stats(P, pp, xb, n, eps, ones32, sqb, mean, rstd, tmp):
    P.act(sqb[:, :, 0:n], xb, AF.Square)
    ps_s = pp.get(); ps_q = pp.get()
    for c in range(8):
        P.matmul(ps_s[:, 0:n], ones32.ap, xb[:, c, :], start=(c == 0), stop=(c == 7))
    for c in range(8):
        P.matmul(ps_q[:, 0:n], ones32.ap, sqb[:, c, 0:n], start=(c == 0), stop=(c == 7))
    P.ts(mean[:, 0:n], ps_s[:, 0:n], 1.0 / D, None, ALU.mult)
    P.tt(tmp[:, 0:n], mean[:, 0:n], mean[:, 0:n], ALU.mult)
    P.stt(tmp[:, 0:n], ps_q[:, 0:n], 1.0 / D, tmp[:, 0:n], ALU.mult, ALU.subtract)
    P.ts(tmp[:, 0:n], tmp[:, 0:n], eps, None, ALU.add)
    P.act(tmp[:, 0:n], tmp[:, 0:n], AF.Sqrt)
    P.recip(rstd[:, 0:n], tmp[:, 0:n])


def fm_norm_apply(P, xb, n, mean, rstd, s1, s2, out):
    P.tt(xb, xb, bc(mean[:, 0:n], [128, 8, n], 1), ALU.subtract)
    P.tt(xb, xb, bc(rstd[:, 0:n], [128, 8, n], 1), ALU.mult)
    for c in range(8):
        P.ts(out[:, c, :], xb[:, c, :], s1[:, c:c + 1], s2[:, c:c + 1], ALU.mult, ALU.add,
             eng="pool" if c % 2 else "dve")


def fm_norm_gen(P, pp, xb, n, eps, ones32, sqb, mean, rstd, tmp, s1, s2, out):
    P.act(sqb[:, :, 0:n], xb, AF.Square)
    yield
    ps_s = pp.get()
    for c in range(8):
        P.matmul(ps_s[:, 0:n], ones32.ap, xb[:, c, :], start=(c == 0), stop=(c == 7))
    P.ts(mean[:, 0:n], ps_s[:, 0:n], 1.0 / D, None, ALU.mult)
    yield
    ps_q = pp.get()
    for c in range(8):
        P.matmul(ps_q[:, 0:n], ones32.ap, sqb[:, c, 0:n], start=(c == 0), stop=(c == 7))
    P.tt(tmp[:, 0:n], mean[:, 0:n], mean[:, 0:n], ALU.mult)
    P.stt(tmp[:, 0:n], ps_q[:, 0:n], 1.0 / D, tmp[:, 0:n], ALU.mult, ALU.subtract)
    yield
    P.ts(tmp[:, 0:n], tmp[:, 0:n], eps, None, ALU.add)
    P.act(tmp[:, 0:n], tmp[:, 0:n], AF.Sqrt)
    P.recip(rstd[:, 0:n], tmp[:, 0:n])
    yield
    P.tt(xb, xb, bc(mean[:, 0:n], [128, 8, n], 1), ALU.subtract)
    yield
    P.tt(xb, xb, bc(rstd[:, 0:n], [128, 8, n], 1), ALU.mult)
    yield
    for c in range(8):
        P.ts(out[:, c, :], xb[:, c, :], s1[:, c:c + 1], s2[:, c:c + 1], ALU.mult, ALU.add,
             eng="pool" if c % 2 else "dve")
        if c == 3:
            yield


def build_A():
    P = Prog()
    X = lambda n, s, dt=F32: P.dram(n, s, dt, kind="ExternalInput")
    xT = X("xT", [D, NT]); c2d = X("c2", [128, 8, 2]); ada_w = X("ada_w", [D, 6144])
    ada_bT = X("ada_bT", [128, 48]); w_in = X("w_in", [D, 7232])
    zT = P.dram("zT", [7232, NT], BF16, kind="ExternalOutput")
    modo = P.dram("mod", [128, 48, 2], F32, kind="ExternalOutput")
    pp = PsPool(P)
    ones32 = P.sbuf("ones32", [128, 128]); P.memset(ones32.ap, 1.0)
    c2 = P.sbuf("c2s", [128, 8, 2]); P.dma(c2.ap, c2d.ap); P.act(c2.ap, c2.ap, AF.Silu)
    bias = P.sbuf("adab", [128, 48]); P.dma(bias.ap, ada_bT.ap)
    mod = P.sbuf("modsb", [128, 48, 2])
    wA = [P.sbuf("wA%d" % i, [128, 8, 512]) for i in range(2)]
    awv = ada_w.ap.rearrange("(kc p) n -> p kc n", p=128)
    for g in range(12):
        w = wA[g % 2]
        for kc in range(8):
            P.dma(w[:, kc, :], awv[:, kc, g * 512:(g + 1) * 512], queue="sp" if kc % 2 else "pool", nowaw=True)
        ps = pp.get()
        for oc in range(4):
            for kc in range(8):
                P.matmul(ps[:, oc * 16:oc * 16 + 2], w[:, kc, oc * 128:(oc + 1) * 128], c2[:, kc, :],
                         start=(kc == 0), stop=(kc == 7))
        P.tt(mod[:, g * 4:(g + 1) * 4, :], ps[:, 0:64].rearrange("p (a b) -> p a b", b=16)[:, :, 0:2],
             bc(bias[:, g * 4:(g + 1) * 4], [128, 4, 2], 2), ALU.add)
    P.dma(modo.ap, mod.ap, queue="pool")
    opsc = P.sbuf("opsc", [128, 8, 2]); P.ts(opsc.ap, mod[:, 8:16, :], 1.0, None, ALU.add)
    wI = P.sbuf("wI", [128, 8, 7232], BF16)
    wiv = w_in.ap.rearrange("(kc p) n -> p kc n", p=128)
    for kc in range(8):
        P.dma(wI[:, kc, :], wiv[:, kc, :], queue="pool", nowaw=True)
    xv = xT.ap.rearrange("(c p) t -> p c t", p=128)
    xb = P.sbuf("xb0", [128, 8, 512])
    sqb = P.sbuf("sqb", [128, 8, 512]); mean = P.sbuf("mean", [128, 512]); rstd = P.sbuf("rstd", [128, 512])
    tmp = P.sbuf("tmp", [128, 512])
    xms = [P.sbuf("xm%d" % i, [128, 8, 512], BF16) for i in range(2)]
    zst = [P.sbuf("zst%d" % i, [128, 512], BF16) for i in range(4)]

    def pre_gen(bi):
        t0, n, col = BLKS[bi]
        for c in range(8):
            P.dma(xb[:, c, 0:n], xv[:, c, t0:t0 + n], queue="sp", nowaw=True)
        yield
        yield from fm_norm_gen(P, pp, xb[:, :, 0:n], n, 1e-6, ones32, sqb, mean, rstd, tmp,
                               opsc[:, :, col], mod[:, 0:8, col], xms[bi % 2][:, :, 0:n])

    for _ in pre_gen(0):
        pass
    k = 0
    for bi, (t0, n, col) in enumerate(BLKS):
        xm = xms[bi % 2]
        gpre = pre_gen(bi + 1) if bi + 1 < len(BLKS) else None
        for j in range(57):
            m = 128 if j < 56 else 64
            ps = pp.get()
            for kc in range(8):
                P.matmul(ps[0:m, 0:n], wI[:, kc, j * 128:j * 128 + m], xm[:, kc, 0:n], start=(kc == 0), stop=(kc == 7))
            st = zst[k % 4]; k += 1
            if j == 32:
                P.copy(st[0:64, 0:n], ps[0:64, 0:n], eng="dve")
                P.act(st[64:128, 0:n], ps[64:128, 0:n], AF.Sigmoid)
            elif j * 128 >= 4160:
                P.act(st[0:m, 0:n], ps[0:m, 0:n], AF.Sigmoid)
            elif k % 2:
                P.copy(st[0:m, 0:n], ps[0:m, 0:n], eng="act")
            else:
                P.copy(st[0:m, 0:n], ps[0:m, 0:n], eng="dve")
            P.dma(zT[j * 128:j * 128 + m, t0:t0 + n], st[0:m, 0:n], queue="sp", nowaw=True)
            if gpre is not None and j % 5 == 4:
                next(gpre, None)
        if gpre is not None:
            for _ in gpre:
                pass
    return P.finish([zT, modo])


def build_C1():
    P = Prog()
    X = lambda n, s, dt=F32: P.dram(n, s, dt, kind="ExternalInput")
    xT = X("xT", [D, NT]); yT = X("yT", [1536, NT], BF16); gT = X("gT", [3072, NT], BF16)
    modd = X("mod", [128, 48, 2]); w_br = X("w_branch", [1536, D]); w_out = X("w_out", [D, D])
    lng = X("ln_g", [128, 8]); lnb = X("ln_b", [128, 8])
    oT = P.dram("x1T", [D, NT], F32, kind="ExternalOutput")
    pp = PsPool(P)
    ones32 = P.sbuf("ones32", [128, 128]); P.memset(ones32.ap, 1.0)
    mod = P.sbuf("modsb", [128, 48, 2]); P.dma(mod.ap, modd.ap)
    g1 = P.sbuf("lng", [128, 8]); b1 = P.sbuf("lnb", [128, 8]); P.dma(g1.ap, lng.ap); P.dma(b1.ap, lnb.ap)
    wb = P.sbuf("wb", [128, 12, D], BF16); wo = P.sbuf("wo", [128, 8, D], BF16)
    wbv = w_br.ap.rearrange("(kc p) n -> p kc n", p=128); wov = w_out.ap.rearrange("(kc p) n -> p kc n", p=128)
    for kc in range(12):
        P.dma(wb[:, kc, :], wbv[:, kc, :], queue="pool", nowaw=True)
    for kc in range(8):
        P.dma(wo[:, kc, :], wov[:, kc, :], queue="pool", nowaw=True)
    xv = xT.ap.rearrange("(c p) t -> p c t", p=128); yv = yT.ap.rearrange("(c p) t -> p c t", p=128)
    gv = gT.ap.rearrange("(c p) t -> p c t", p=128); ov = oT.ap.rearrange("(c p) t -> p c t", p=128)
    xb = P.sbuf("xb", [128, 8, 512]); yb = P.sbuf("yb", [128, 12, 512], BF16); gb = P.sbuf("gb", [128, 24, 512], BF16)
    mb = P.sbuf("mb", [128, 8, 512], BF16); t1 = P.sbuf("t1", [128, 512]); t2 = P.sbuf("t2", [128, 512])
    sqb = P.sbuf("sqb", [128, 8, 512]); mean = P.sbuf("mean", [128, 512]); rstd = P.sbuf("rstd", [128, 512])
    tmp = P.sbuf("tmp", [128, 512]); ob = P.sbuf("ob", [128, 8, 512])
    for bi, (t0, n, col) in enumerate(BLKS):
        for c in range(8):
            P.dma(xb[:, c, 0:n], xv[:, c, t0:t0 + n], queue="sp", nowaw=True)
        for c in range(12):
            P.dma(yb[:, c, 0:n], yv[:, c, t0:t0 + n], queue="sp", nowaw=True)
        for c in range(24):
            P.dma(gb[:, c, 0:n], gv[:, c, t0:t0 + n], queue="pool", nowaw=True)
        for oc in range(8):
            for nb in range(3):
                ps = pp.get()
                for kc in range(4):
                    P.matmul(ps[:, 0:n], wb[:, nb * 4 + kc, oc * 128:(oc + 1) * 128], yb[:, nb * 4 + kc, 0:n],
                             start=(kc == 0), stop=(kc == 3))
                dst = t1 if nb == 0 else t2
                P.tt(dst[:, 0:n], ps[:, 0:n], gb[:, nb * 8 + oc, 0:n], ALU.mult)
                if nb == 1:
                    P.tt(t1[:, 0:n], t1[:, 0:n], t2[:, 0:n], ALU.add, eng="pool")
                if nb == 2:
                    P.tt(mb[:, oc, 0:n], t1[:, 0:n], t2[:, 0:n], ALU.add, eng="pool")
        for oc in range(8):
            ps = pp.get()
            for kc in range(8):
                P.matmul(ps[:, 0:n], wo[:, kc, oc * 128:(oc + 1) * 128], mb[:, kc, 0:n], start=(kc == 0), stop=(kc == 7))
            P.ts(t1[:, 0:n], ps[:, 0:n], mod[:, 16 + oc, col:col + 1], None, ALU.mult)
            P.stt(xb[:, oc, 0:n], xb[:, oc, 0:n], ALPHA, t1[:, 0:n], ALU.mult, ALU.add)
        fm_stats(P, pp, xb[:, :, 0:n], n, 1e-5, ones32, sqb, mean, rstd, tmp)
        fm_norm_apply(P, xb[:, :, 0:n], n, mean, rstd, g1.ap, b1.ap, ob[:, :, 0:n])
        for c in range(8):
            P.dma(ov[:, c, t0:t0 + n], ob[:, c, 0:n], queue="sp", nowaw=True)
    return P.finish([oT])


def build_C2():
    P = Prog()
    X = lambda n, s, dt=F32: P.dram(n, s, dt, kind="ExternalInput")
    xT = X("xT", [D, NT]); modd = X("mod", [128, 48, 2]); w1d = X("w1", [D, 4096]); w2d = X("w2", [4096, D])
    lng = X("ln_g", [128, 8]); lnb = X("ln_b", [128, 8])
    oT = P.dram("x2T", [D, NT], F32, kind="ExternalOutput")
    pp = PsPool(P)
    ones32 = P.sbuf("ones32", [128, 128]); P.memset(ones32.ap, 1.0)
    mod = P.sbuf("modsb", [128, 48, 2]); P.dma(mod.ap, modd.ap)
    opsc = P.sbuf("opsc", [128, 8, 2]); P.ts(opsc.ap, mod[:, 32:40, :], 1.0, None, ALU.add)
    g1 = P.sbuf("lng", [128, 8]); b1 = P.sbuf("lnb", [128, 8]); P.dma(g1.ap, lng.ap); P.dma(b1.ap, lnb.ap)
    w1 = P.sbuf("w1s", [128, 8, 4096], BF16); w2 = P.sbuf("w2s", [128, 32, D], BF16)
    w1v = w1d.ap.rearrange("(kc p) n -> p kc n", p=128); w2v = w2d.ap.rearrange("(kc p) n -> p kc n", p=128)
    w1g = [P.sub(w1, "w1g%d" % g) for g in range(4)]
    for g in range(4):
        for kc in range(8):
            P.dma(View(w1g[g], w1.t[:, kc, g * 1024:(g + 1) * 1024]), w1v[:, kc, g * 1024:(g + 1) * 1024], queue="pool", nowaw=True)
    for kc in range(32):
        P.dma(w2[:, kc, :], w2v[:, kc, :], queue="pool", nowaw=True)
    xv = xT.ap.rearrange("(c p) t -> p c t", p=128); ov = oT.ap.rearrange("(c p) t -> p c t", p=128)
    NB = 256
    xb = P.sbuf("xb", [128, 8, NB]); xrs = [P.sbuf("xr%d" % i, [128, 8, NB]) for i in range(2)]
    xms = [P.sbuf("xm%d" % i, [128, 8, NB], BF16) for i in range(2)]
    hb = P.sbuf("hb", [128, 32, NB], BF16); hrs = [P.sbuf("hr%d" % i, [128, NB]) for i in range(2)]
    t1s = [P.sbuf("t1_%d" % i, [128, NB]) for i in range(2)]
    sqb = P.sbuf("sqb", [128, 8, NB]); mean = P.sbuf("mean", [128, NB]); rstd = P.sbuf("rstd", [128, NB])
    tmp = P.sbuf("tmp", [128, NB])
    blks = []
    for (t0, n, col) in BLKS:
        for s in range(0, n, NB):
            blks.append((t0 + s, min(NB, n - s), col))

    def pre_gen(bi):
        t0, n, col = blks[bi]
        xr = xrs[bi % 2]
        for c in range(8):
            P.dma(xb[:, c, 0:n], xv[:, c, t0:t0 + n], queue="sp", nowaw=True)
        P.copy(xr[:, :, 0:n], xb[:, :, 0:n], eng="pool")
        yield
        yield from fm_norm_gen(P, pp, xb[:, :, 0:n], n, 1e-6, ones32, sqb, mean, rstd, tmp,
                               opsc[:, :, col], mod[:, 24:32, col], xms[bi % 2][:, :, 0:n])

    def post_gen(bi):
        t0, n, col = blks[bi]
        xr = xrs[bi % 2]
        yield from fm_norm_gen(P, pp, xr[:, :, 0:n], n, 1e-5, ones32, sqb, mean, rstd, tmp, g1.ap, b1.ap, xr[:, :, 0:n])
        yield
        for c in range(8):
            P.dma(ov[:, c, t0:t0 + n], xr[:, c, 0:n], queue="sp", nowaw=True)

    def drain(g):
        if g is not None:
            for _ in g:
                pass

    drain(pre_gen(0))
    gpost = None
    for bi, (t0, n, col) in enumerate(blks):
        xm = xms[bi % 2]; xr = xrs[bi % 2]
        gpre = pre_gen(bi + 1) if bi + 1 < len(blks) else None
        for fc in range(32):
            ps = pp.get()
            for kc in range(8):
                P.matmul(ps[:, 0:n], View(w1g[fc // 8], w1.t[:, kc, fc * 128:(fc + 1) * 128]), xm[:, kc, 0:n], start=(kc == 0), stop=(kc == 7))
            hr = hrs[fc % 2]
            P.act(hr[:, 0:n], ps[:, 0:n], AF.Relu)
            P.tt(hb[:, fc, 0:n], hr[:, 0:n], hr[:, 0:n], ALU.mult)
            if fc % 2 == 1:
                if fc < 20:
                    if gpost is not None:
                        next(gpost, None)
                else:
                    if fc == 21:
                        drain(gpost); gpost = None
                    if gpre is not None:
                        next(gpre, None)
        for oc in range(8):
            ps = pp.get()
            for fc in range(32):
                P.matmul(ps[:, 0:n], w2[:, fc, oc * 128:(oc + 1) * 128], hb[:, fc, 0:n], start=(fc == 0), stop=(fc == 31))
            t1 = t1s[oc % 2]
            P.ts(t1[:, 0:n], ps[:, 0:n], mod[:, 40 + oc, col:col + 1], None, ALU.mult)
            P.stt(xr[:, oc, 0:n], xr[:, oc, 0:n], ALPHA, t1[:, 0:n], ALU.mult, ALU.add)
            if gpre is not None:
                next(gpre, None)
        drain(gpre)
        gpost = post_gen(bi)
    drain(gpost)
    return P.finish([oT])


TT = 4352
QBLK = [(0, 256, 2)] + [(256 + 512 * i, 512, 34) for i in range(8)]


def build_B1():
    P = Prog()
    X = lambda n, s, dt=F32: P.dram(n, s, dt, kind="ExternalInput")
    qdT = X("qdT", [384, TT], BF16); kvdT = X("kvdT", [256, TT], BF16)
    krT = X("krT", [32, TT], BF16); krrT = X("krrT", [32, TT], BF16)
    wq = X("wq", [384, 384]); wqr = X("wqr", [384, 384]); wkn = X("wkn", [256, 256]); wkv = X("wkv", [256, 256])
    qg = X("qg", [128, 3]); kg = X("kg", [128, 2]); Cd = X("ropeC", [96, TT]); Sd = X("ropeS", [96, TT])
    yo = P.dram("ybT", [256, TT], BF16, kind="ExternalOutput")
    pss = [P.psum("pS%d" % i, [128, 512]) for i in range(6)]
    pso = [P.psum("pO%d" % i, [128, 512]) for i in range(2)]
    psi = [0]
    def gps():
        psi[0] += 1
        return pss[psi[0] % 6]
    ones16 = P.sbuf("ones16", [128, 128], BF16); P.memset(ones16.ap, 1.0)
    qgs = P.sbuf("qgs", [128, 3]); kgs = P.sbuf("kgs", [128, 2]); P.dma(qgs.ap, qg.ap); P.dma(kgs.ap, kg.ap)
    Ct = P.sbuf("Ct", [96, TT]); St = P.sbuf("St", [96, TT]); P.dma(Ct.ap, Cd.ap); P.dma(St.ap, Sd.ap, queue="pool")
    qd = P.sbuf("qd", [128, 3, TT], BF16); kvd = P.sbuf("kvd", [128, 2, TT], BF16)
    for c in range(3):
        P.dma(qd[:, c, :], qdT[c * 128:(c + 1) * 128, :], nowaw=True)
    for c in range(2):
        P.dma(kvd[:, c, :], kvdT[c * 128:(c + 1) * 128, :], queue="pool", nowaw=True)
    kr = P.sbuf("kr", [96, TT], BF16); krr = P.sbuf("krr", [96, TT], BF16)
    P.dma(kr[64:96, :], krT.ap); P.dma(krr[64:96, :], krrT.ap, queue="pool")
    wqs = P.sbuf("wqs", [128, 3, 384], BF16); wqrs = P.sbuf("wqrs", [128, 3, 384], BF16)
    wkns = P.sbuf("wkns", [128, 2, 256], BF16); wkvs = P.sbuf("wkvs", [128, 2, 256], BF16)
    P.dma(wqs.ap, wq.ap.rearrange("(c p) n -> p c n", p=128), queue="pool")
    P.dma(wqrs.ap, wqr.ap.rearrange("(c p) n -> p c n", p=128), queue="pool")
    P.dma(wkns.ap, wkn.ap.rearrange("(c p) n -> p c n", p=128), queue="pool")
    P.dma(wkvs.ap, wkv.ap.rearrange("(c p) n -> p c n", p=128), queue="pool")
    rq = P.sbuf("rq", [128, TT]); rk = P.sbuf("rk", [128, TT]); rkc = P.sbuf("rkc", [128, 34])
    sq = P.sbuf("sq", [128, 3, 512], BF16)
    for (t0, n, _) in QBLK:
        for (src, nc_, dst, dim) in ((qd, 3, rq, 384.0), (kvd, 2, rk, 256.0)):
            P.tt(sq[:, 0:nc_, 0:n], src[:, 0:nc_, t0:t0 + n], src[:, 0:nc_, t0:t0 + n], ALU.mult)
            ps = gps()
            for c in range(nc_):
                P.matmul(ps[:, 0:n], ones16.ap, sq[:, c, 0:n], start=(c == 0), stop=(c == nc_ - 1))
            P.ts(dst[:, t0:t0 + n], ps[:, 0:n], 1.0 / dim, 1e-6, ALU.mult, ALU.add)
            P.act(dst[:, t0:t0 + n], dst[:, t0:t0 + n], AF.Sqrt)
            P.recip(dst[:, t0:t0 + n], dst[:, t0:t0 + n])
            if src is kvd:
                for tl in range(n // 128):
                    kt = t0 // 128 + tl
                    ps2 = gps()
                    for c in range(2):
                        P.matmul(ps2[:, 0:1], sq[:, c, tl * 128:(tl + 1) * 128], ones16[:, 0:1], start=(c == 0), stop=(c == 1))
                    P.ts(rkc[:, kt:kt + 1], ps2[:, 0:1], 1.0 / 256.0, 1e-6, ALU.mult, ALU.add)
    P.act(rkc.ap, rkc.ap, AF.Sqrt); P.recip(rkc.ap, rkc.ap)
    P.ts(rq.ap, rq.ap, 96.0 ** -0.5, None, ALU.mult)
    for c in range(3):
        P.ts(qd[:, c, :], qd[:, c, :], qgs[:, c:c + 1], None, ALU.mult, eng="pool")
    for c in range(2):
        P.ts(kvd[:, c, :], kvd[:, c, :], kgs[:, c:c + 1], None, ALU.mult, eng="pool")
    Rk = P.sbuf("Rk", [96, TT], BF16)
    t1 = P.sbuf("t1", [96, 512]); t2 = P.sbuf("t2", [96, 512])
    for (t0, n, _) in QBLK:
        P.tt(t1[64:96, 0:n], kr[64:96, t0:t0 + n], Ct[64:96, t0:t0 + n], ALU.mult)
        P.tt(t2[64:96, 0:n], krr[64:96, t0:t0 + n], St[64:96, t0:t0 + n], ALU.mult, eng="pool")
        P.tt(Rk[64:96, t0:t0 + n], t1[64:96, 0:n], t2[64:96, 0:n], ALU.add)
    Qhs = [P.sbuf("Qh%d" % i, [96, TT], BF16) for i in range(2)]
    Khs = [P.sbuf("Kh%d" % i, [96, TT], BF16) for i in range(2)]
    Vh = P.sbuf("Vh", [128, 34, 128], BF16)
    P.memset(Vh.ap, 1.0)
    es = [P.sbuf("es%d" % i, [128, 512], BF16) for i in range(6)]
    den = P.sbuf("den", [64, 512]); ost = [P.sbuf("ost%d" % i, [64, 512], BF16) for i in range(2)]
    ei = 0

    def proj_qk_gen(h):
        Qh, Kh = Qhs[h % 2], Khs[h % 2]
        for (t0, n, _) in QBLK:
            ps1 = gps(); ps2 = gps()
            for c in range(3):
                P.matmul(ps1[0:96, 0:n], wqs[:, c, h * 96:(h + 1) * 96], qd[:, c, t0:t0 + n], start=(c == 0), stop=(c == 2))
            for c in range(3):
                P.matmul(ps2[0:96, 0:n], wqrs[:, c, h * 96:(h + 1) * 96], qd[:, c, t0:t0 + n], start=(c == 0), stop=(c == 2))
            P.tt(t1[:, 0:n], ps1[0:96, 0:n], Ct[:, t0:t0 + n], ALU.mult)
            P.tt(t2[:, 0:n], ps2[0:96, 0:n], St[:, t0:t0 + n], ALU.mult)
            P.tt(t1[:, 0:n], t1[:, 0:n], t2[:, 0:n], ALU.add, eng="pool")
            P.tt(Qh[:, t0:t0 + n], t1[:, 0:n], rq[0:96, t0:t0 + n], ALU.mult)
            ps3 = gps()
            for c in range(2):
                P.matmul(ps3[0:64, 0:n], wkns[:, c, h * 64:(h + 1) * 64], kvd[:, c, t0:t0 + n], start=(c == 0), stop=(c == 1))
            P.tt(Kh[0:64, t0:t0 + n], ps3[0:64, 0:n], rk[0:64, t0:t0 + n], ALU.mult)
            P.copy(Kh[64:96, t0:t0 + n], Rk[64:96, t0:t0 + n], eng="pool")
            yield

    def proj_v(h):
        for kt in range(34):
            ps4 = gps()
            for c in range(2):
                P.matmul(ps4[:, 0:64], kvd[:, c, kt * 128:(kt + 1) * 128], wkvs[:, c, h * 64:(h + 1) * 64], start=(c == 0), stop=(c == 1))
            P.ts(Vh[:, kt, 0:64], ps4[:, 0:64], rkc[:, kt:kt + 1], None, ALU.mult)

    for _ in proj_qk_gen(0):
        pass
    proj_v(0)
    for h in range(4):
        Qh, Kh = Qhs[h % 2], Khs[h % 2]
        gnext = proj_qk_gen(h + 1) if h < 3 else None
        for qi, (q0, nq, nk) in enumerate(QBLK):
            po = pso[qi % 2]
            pend = []
            for kt in range(nk):
                ps = gps()
                P.matmul(ps[:, 0:nq], Kh[:, kt * 128:(kt + 1) * 128], Qh[:, q0:q0 + nq])
                e = es[ei % 6]; ei += 1
                P.act(e[:, 0:nq], ps[:, 0:nq], AF.Exp)
                pend.append((kt, e))
                if len(pend) > 2:
                    k0, e0 = pend.pop(0)
                    P.matmul(po[:, 0:nq], Vh[:, k0, :], e0[:, 0:nq], start=(k0 == 0), stop=(k0 == nk - 1))
                if gnext is not None and kt == 16:
                    next(gnext, None)
            for (k0, e0) in pend:
                P.matmul(po[:, 0:nq], Vh[:, k0, :], e0[:, 0:nq], start=(k0 == 0), stop=(k0 == nk - 1))
            P.copy(den[:, 0:nq], po[64:128, 0:nq])
            P.recip(den[:, 0:nq], den[:, 0:nq])
            o = ost[qi % 2]
            P.tt(o[:, 0:nq], po[0:64, 0:nq], den[:, 0:nq], ALU.mult)
            P.dma(yo[h * 64:(h + 1) * 64, q0:q0 + nq], o[:, 0:nq], queue="sp", nowaw=True)
        if gnext is not None:
            for _ in gnext:
                pass
            proj_v(h + 1)
    return P.finish([yo])


def build_B2():
    P = Prog()
    X = lambda n, s, dt=F32: P.dram(n, s, dt, kind="ExternalInput")
    qT = X("qT", [128, TT], BF16); kT = X("kT", [128, TT], BF16); kM = X("kM", [TT, 128], BF16)
    vM = X("vM", [TT, 256], BF16); ogM = X("ogM", [TT, 256], BF16)
    gdT = [X("gdT%d" % i, [16, TT], BF16) for i in range(2)]
    gup = [X("gup%d" % i, [16, 128]) for i in range(2)]; gkb = [X("gkb%d" % i, [1, 128]) for i in range(2)]
    ngd = X("ng", [1, 128]); mk = X("masks", [128, 4, 128])
    yo = P.dram("ya", [TT, 256], F32, kind="ExternalOutput")
    pp = PsPool(P)
    masks = P.sbuf("masksb", [128, 4, 128]); P.dma(masks.ap, mk.ap)
    m16 = P.sbuf("m16", [128, 2, 128], BF16); P.copy(m16.ap, masks[:, 0:2, :])
    ones1 = P.sbuf("ones1", [1, 128], BF16); P.memset(ones1.ap, 1.0)
    q = P.sbuf("q", [128, TT], BF16); k = P.sbuf("k", [128, TT], BF16)
    P.dma(q.ap, qT.ap); P.dma(k.ap, kT.ap, queue="pool")
    km = P.sbuf("km", [128, 34, 128], BF16); vm = P.sbuf("vm", [128, 34, 256], BF16)
    P.dma(km.ap, kM.ap.rearrange("(n p) c -> p n c", p=128)); P.dma(vm.ap, vM.ap.rearrange("(n p) c -> p n c", p=128), queue="pool")
    gd = [P.sbuf("gd%d" % i, [16, TT], BF16) for i in range(2)]
    gu = [P.sbuf("gu%d" % i, [16, 128], BF16) for i in range(2)]; gb = [P.sbuf("gb%d" % i, [1, 128], BF16) for i in range(2)]
    for i in range(2):
        P.dma(gd[i].ap, gdT[i].ap); P.dma(gu[i].ap, gup[i].ap, queue="pool"); P.dma(gb[i].ap, gkb[i].ap, queue="pool")
    ng = P.sbuf("ngsb", [128, 128]); P.dma(ng.ap, View(ngd, ngd.t.ap()[0, :].partition_broadcast(128)))
    oacc = P.sbuf("oacc", [128, 34, 256]); P.memset(oacc.ap, 0.0, eng="pool")
    S32 = P.sbuf("S32", [128, 2, 128]); S16 = P.sbuf("S16", [128, 2, 128], BF16)
    P.memset(S32.ap, 0.0); P.memset(S16.ap, 0.0)
    TD = []
    for dr in range(2):
        f = lambda nm, sh, dt=F32: P.sbuf("%s_d%d" % (nm, dr), sh, dt)
        TD.append(dict(sp=f("sp", [128, 128]), E1=f("E1", [128, 128]), E2=f("E2", [128, 128]), E3=f("E3", [128, 128]),
                       dec=f("dec", [128, 1]), qin=f("qin", [128, 128], BF16), kin=f("kin", [128, 128], BF16),
                       kst=f("kst", [128, 128], BF16), attm=[f("attm%d" % i, [128, 128], BF16) for i in range(2)]))
    orders = [list(range(34)), [1, 0] + list(range(33, 1, -1))]
    for step in range(34):
        ctx = []
        for dr in range(2):
            T_ = TD[dr]
            ctx.append(dict(dr=dr, ti=orders[dr][step], last=127 if dr == 0 else 0, T=T_,
                            Sf=S32[:, dr, :], Sb=S16[:, dr, :]))
        for c in ctx:
            dr, ti, T_ = c["dr"], c["ti"], c["T"]
            tsl = slice(ti * 128, (ti + 1) * 128); c["tsl"] = tsl
            ps = pp.get()
            P.matmul(ps[:, 0:128], gd[dr][:, tsl], gu[dr].ap, start=True, stop=False)
            P.matmul(ps[:, 0:128], ones1.ap, gb[dr].ap, start=False, stop=True)
            P.act(T_["sp"].ap, ps[:, 0:128], AF.Exp, scale=-1.0)
        for c in ctx:
            P.act(c["T"]["sp"].ap, c["T"]["sp"].ap, AF.Ln, bias=1.0)
        for c in ctx:
            dr, T_ = c["dr"], c["T"]
            pb = pp.get(); pf = pp.get(); c["pb"] = pb; c["pf"] = pf
            P.matmul(pb[:, 0:128], T_["sp"].ap, masks[:, dr, :])
            P.matmul(pf[:, 0:128], masks[:, 2 + dr, :], T_["sp"].ap)
        for c in ctx:
            T_ = c["T"]
            P.act(T_["E1"].ap, c["pb"][:, 0:128], AF.Exp, scale=-1.0 / 16.0)
            P.act(T_["E2"].ap, c["pb"][:, 0:128], AF.Exp, scale=1.0 / 16.0)
            P.act(T_["E3"].ap, c["pf"][:, 0:128], AF.Exp, scale=-1.0 / 16.0)
        for c in ctx:
            T_, ti, tsl, last = c["T"], c["ti"], c["tsl"], c["last"]
            P.copy(T_["dec"].ap, T_["E1"][:, last:last + 1])
            P.stt(T_["qin"].ap, q[:, tsl], 0.125, T_["E1"].ap, ALU.mult, ALU.mult)
            P.tt(T_["kin"].ap, k[:, tsl], T_["E2"].ap, ALU.mult, eng="pool")
            P.tt(T_["kst"].ap, km[:, ti, :], T_["E3"].ap, ALU.mult, eng="pool")
        for hh in range(2):
            hs = slice(hh * 64, (hh + 1) * 64)
            for c in ctx:
                T_ = c["T"]
                pa = pp.get()
                P.matmul(pa[:, 0:128], T_["kin"][hs, :], T_["qin"][hs, :])
                P.tt(T_["attm"][hh].ap, pa[:, 0:128], m16[:, c["dr"], :], ALU.mult)
        for hh in range(2):
            hs = slice(hh * 64, (hh + 1) * 64); vs = slice(hh * 128, (hh + 1) * 128)
            for c in ctx:
                T_, ti, Sf, Sb = c["T"], c["ti"], c["Sf"], c["Sb"]
                po = pp.get()
                P.matmul(po[:, 0:128], T_["attm"][hh].ap, vm[:, ti, vs], start=True, stop=False)
                P.matmul(po[:, 0:128], T_["qin"][hs, :], Sb[hs, :], start=False, stop=True)
                pu = pp.get()
                P.matmul(pu[:, 0:128], T_["kst"].ap, vm[:, ti, vs])
                P.tt(oacc[:, ti, vs], oacc[:, ti, vs], po[:, 0:128], ALU.add)
                P.stt(Sf[hs, :], Sf[hs, :], T_["dec"][hs, 0:1], pu[hs, 0:128], ALU.mult, ALU.add)
                P.copy(Sb[hs, :], Sf[hs, :], eng="act")

    og = P.sbuf("og", [128, 256], BF16); sg = P.sbuf("sg", [128, 256]); sq = P.sbuf("sq2", [128, 256])
    ss = P.sbuf("ss", [128, 2]); yt = [P.sbuf("yt%d" % i, [128, 256]) for i in range(2)]
    for ti in range(34):
        P.dma(og.ap, ogM[ti * 128:(ti + 1) * 128, :])
        P.act(sg.ap, og.ap, AF.Silu)
        o = oacc[:, ti, :]
        P.tt(sq.ap, o, o, ALU.mult)
        P.reduce(ss.ap, sq.ap.rearrange("p (h v) -> p h v", h=2), ALU.add)
        P.ts(ss.ap, ss.ap, 1.0 / 128.0, 1e-6, ALU.mult, ALU.add)
        P.act(ss.ap, ss.ap, AF.Sqrt); P.recip(ss.ap, ss.ap)
        y = yt[ti % 2]
        P.tt(y.ap.rearrange("p (h v) -> p h v", h=2), o.rearrange("p (h v) -> p h v", h=2), bc(ss.ap, [128, 2, 128], 2), ALU.mult)
        P.tt(y.ap.rearrange("p (h v) -> p h v", h=2), y.ap.rearrange("p (h v) -> p h v", h=2), bc(ng.ap, [128, 2, 128], 1), ALU.mult, eng="pool")
        P.tt(y.ap, y.ap, sg.ap, ALU.mult)
        P.dma(yo[ti * 128:(ti + 1) * 128, :], y.ap, queue="pool", nowaw=True)
    return P.finish([yo])


CW = 0.6065306597126334


def build_B3():
    P = Prog()
    X = lambda n, s, dt=F32: P.dram(n, s, dt, kind="ExternalInput")
    zrkv = X("zrkv", [TT, 3, 768], BF16)
    lrd = X("lr", [3, 3, 128, TT], BF16)
    murkv = X("murkv", [2, 768]); mulrd = X("mulr", [128, 3, 2])
    wupd = X("wup", [128, 256]); w0d = X("w0", [2, 1, 256]); aupd = X("aup", [128, 256]); a0d = X("a0", [2, 1, 256])
    gupd = X("gup", [128, 256]); vecsd = X("vecs", [5, 256]); mk = X("masks", [128, 4, 128]); idd = X("ident", [128, 128])
    yo = P.dram("yr", [TT, 256], F32, kind="ExternalOutput")
    pp = PsPool(P)
    masks = P.sbuf("masksb", [128, 4, 128]); P.dma(masks.ap, mk.ap)
    id32 = P.sbuf("id32", [128, 128]); P.dma(id32.ap, idd.ap)
    id16 = P.sbuf("id16", [128, 128], BF16); P.copy(id16.ap, id32.ap)
    ones1 = P.sbuf("ones1", [1, 128], BF16); P.memset(ones1.ap, 1.0)
    onec = P.sbuf("onec", [128, 1]); P.memset(onec.ap, 1.0)
    M_I = (2, 3); QS_I = (3, 2); QN_I = (0, 1); INC_I = (0, 1); EXC_I = (3, 2); SUF_I = (2, 3)
    maskM = P.sbuf("maskM", [128, 2, 128], BF16); maskQN = P.sbuf("maskQN", [128, 2, 256], BF16)
    for d in range(2):
        P.copy(maskM[:, d, :], masks[:, M_I[d], :])
        P.copy(maskQN[:, d, 0:128], masks[:, QS_I[d], :]); P.copy(maskQN[:, d, 128:256], masks[:, QN_I[d], :])
    murep = P.sbuf("murep", [128, 2, 768])
    P.dma(murep.ap, View(murkv, murkv.t.ap().partition_broadcast(128)))
    c0rep = P.sbuf("c0rep", [128, 768])
    P.tt(c0rep.ap, murep[:, 0, :], murep[:, 1, :], ALU.add)
    P.ts(c0rep.ap, c0rep.ap, -1.0, 1.0, ALU.mult, ALU.add)
    vrep = P.sbuf("vrep", [128, 5, 256]); P.dma(vrep.ap, View(vecsd, vecsd.t.ap().partition_broadcast(128)), queue="pool")
    wup = P.sbuf("wup16", [128, 256], BF16); aup = P.sbuf("aup16", [128, 256], BF16)
    w0 = P.sbuf("w016", [1, 2, 256], BF16); a0 = P.sbuf("a016", [1, 2, 256], BF16); gup = P.sbuf("gup16", [128, 256], BF16)
    P.dma(wup.ap, wupd.ap, queue="pool"); P.dma(aup.ap, aupd.ap, queue="pool")
    P.dma(w0.ap, w0d.ap.rearrange("d r c -> r d c"), queue="pool"); P.dma(a0.ap, a0d.ap.rearrange("d r c -> r d c"), queue="pool")
    P.dma(gup.ap, gupd.ap, queue="pool")
    P.tag = "lowrank"
    mulr = P.sbuf("mulrs", [128, 3, 2]); P.dma(mulr.ap, mulrd.ap)
    clr = P.sbuf("clr", [128, 3])
    P.tt(clr.ap, mulr[:, :, 0], mulr[:, :, 1], ALU.add); P.ts(clr.ap, clr.ap, -1.0, 1.0, ALU.mult, ALU.add)
    st = [P.sbuf("lrst%d" % i, [128, 544], BF16) for i in range(6)]
    LRs = [P.sbuf("LRs%d" % i, [128, TT], BF16) for i in range(3)]
    LRg = LRs[2]
    tl = P.sbuf("lrt", [128, 544])
    kk_ = 0
    for i in range(3):
        fn = (AF.Tanh, AF.Identity, AF.Sigmoid)[i]
        for b0 in range(0, TT, 544):
            bs = slice(b0, b0 + 544)
            sb = [st[(kk_ % 2) * 3 + j] for j in range(3)]; kk_ += 1
            for j in range(3):
                P.dma(sb[j].ap, lrd[i, j][:, bs], queue="sp" if j % 2 else "pool")
            P.ts(tl.ap, sb[0].ap, clr[:, i:i + 1], None, ALU.mult)
            P.stt(tl.ap, sb[1].ap, mulr[:, i, 0:1], tl.ap, ALU.mult, ALU.add)
            P.stt(tl.ap, sb[2].ap, mulr[:, i, 1:2], tl.ap, ALU.mult, ALU.add)
            P.act(LRs[i][:, bs], tl.ap, fn)
    yacc = P.sbuf("yacc", [128, 34, 256]); racc = P.sbuf("racc", [128, 34, 4])
    P.memset(yacc.ap, 0.0, eng="pool"); P.memset(racc.ap, 0.0, eng="pool")
    S16 = P.sbuf("S16", [64, 8, 64], BF16); P.memset(S16.ap, 0.0)
    zb = [P.sbuf("zb%d" % i, [128, 3, 768], BF16) for i in range(2)]
    zv = zrkv.ap.rearrange("(n p) j c -> n p j c", p=128)

    def mk_tmp(tag):
        f = lambda nm, sh, dt=F32: P.sbuf(nm + tag, sh, dt)
        return dict(m1=f("m1", [128, 768]), m2=f("m2", [128, 768]),
                    sw=f("sw", [128, 256]), aa=f("aa", [128, 256]), kk=f("kk", [128, 256]), t1=f("t1", [128, 256]),
                    keff=f("keff", [128, 256]), bb=f("bb", [128, 256]), ss=f("ss", [128, 4]),
                    EA=f("EA", [128, 256]), ER=f("ER", [128, 256]), EI=f("EI", [128, 256]), ES=f("ES", [128, 256]),
                    Ap16=f("Ap16", [128, 256], BF16), Bm16=f("Bm16", [128, 256], BF16), Km16=f("Km16", [128, 256], BF16))

    def mk_out(tag):
        f = lambda nm, sh, dt=F32: P.sbuf(nm + tag, sh, dt)
        return dict(V16=f("V16", [128, 256], BF16), Ap32=f("Ap32", [128, 256]), Rp16=f("Rp16", [128, 256], BF16),
                    Bc16=f("Bc16", [128, 256], BF16), Kc16=f("Kc16", [128, 256], BF16), Wc=f("Wc", [64, 4]),
                    FTAR=f("FTAR", [128, 2, 256], BF16), FTBK=f("FTBK", [128, 2, 256], BF16))
    PBt = {d: mk_tmp("_%d" % d) for d in range(2)}
    PBo = {(d, par): mk_out("_%d%d" % (d, par)) for d in range(2) for par in range(2)}
    PB = {(d, par): {**PBt[d], **PBo[(d, par)]} for d in range(2) for par in range(2)}

    def prep_gen(d, ti, par):
        T_ = PB[(d, par)]
        P.tag = "prep0"
        zin = zb[(d + par) % 2]
        P.dma(zin.ap, zv[ti])
        m1, m2 = T_["m1"], T_["m2"]
        P.tt(m1.ap, zin[:, 0, :], c0rep.ap, ALU.mult, eng="pool")
        P.tt(m2.ap, zin[:, 1, :], murep[:, 0, :], ALU.mult, eng="pool")
        P.tt(m1.ap, m1.ap, m2.ap, ALU.add, eng="pool")
        P.tt(m2.ap, zin[:, 2, :], murep[:, 1, :], ALU.mult, eng="pool")
        P.tt(m1.ap, m1.ap, m2.ap, ALU.add, eng="pool")
        r = m1[:, 0:256]; k = m1[:, 256:512]; v = m1[:, 512:768]
        P.copy(T_["V16"].ap, v, eng="pool")
        tsl = slice(ti * 128, (ti + 1) * 128)
        pw = pp.get()
        ds = slice(64 * d, 64 * d + 64)
        P.matmul(pw[:, 0:256], LRs[0][ds, tsl], wup[ds, :], start=True, stop=False)
        P.matmul(pw[:, 0:256], ones1.ap, w0[:, d, :], start=False, stop=True)
        P.act(T_["sw"].ap, pw[:, 0:256], AF.Sigmoid)
        pa = pp.get()
        P.matmul(pa[:, 0:256], LRs[1][ds, tsl], aup[ds, :], start=True, stop=False)
        P.matmul(pa[:, 0:256], ones1.ap, a0[:, d, :], start=False, stop=True)
        P.act(T_["aa"].ap, pa[:, 0:256], AF.Sigmoid)
        yield
        P.tag = "prep1"
        sw = T_["sw"]
        pci = pp.get(); pce = pp.get(); pcs = pp.get()
        P.matmul(pci[:, 0:256], masks[:, INC_I[d], :], sw.ap)
        P.matmul(pce[:, 0:256], masks[:, EXC_I[d], :], sw.ap)
        P.matmul(pcs[:, 0:256], masks[:, SUF_I[d], :], sw.ap)
        P.act(T_["EA"].ap, pce[:, 0:256], AF.Exp, scale=-CW)
        P.act(T_["ER"].ap, pci[:, 0:256], AF.Exp, scale=-CW)
        P.act(T_["EI"].ap, pci[:, 0:256], AF.Exp, scale=CW)
        P.act(T_["ES"].ap, pcs[:, 0:256], AF.Exp, scale=-CW)
        pwc = pp.get()
        for h in range(4):
            P.matmul(pwc[0:64, h * 8:h * 8 + 1], sw[:, h * 64:(h + 1) * 64], onec.ap)
        P.act(T_["Wc"].ap, pwc[0:64, 0:32].rearrange("p (h e) -> p h e", e=8)[:, :, 0], AF.Exp, scale=-CW)
        kk, t1, ss = T_["kk"], T_["t1"], T_["ss"]
        P.tt(kk.ap, k, vrep[:, 0, :], ALU.mult)
        P.tt(t1.ap, kk.ap, kk.ap, ALU.mult)
        P.reduce(ss.ap, t1.ap.rearrange("p (h c) -> p h c", h=4), ALU.add)
        P.ts(ss.ap, ss.ap, 1e-12, None, ALU.add)
        P.act(ss.ap, ss.ap, AF.Sqrt)
        yield
        P.tag = "prep2"
        P.recip(ss.ap, ss.ap)
        P.tt(kk.ap.rearrange("p (h c) -> p h c", h=4), kk.ap.rearrange("p (h c) -> p h c", h=4), bc(ss.ap, [128, 4, 64], 2), ALU.mult)
        keff, bb = T_["keff"], T_["bb"]
        P.stt(t1.ap, T_["aa"].ap, -1.0, vrep[:, 1, :], ALU.add, ALU.mult)
        P.stt(keff.ap, t1.ap, 1.0, k, ALU.add, ALU.mult)
        P.tt(bb.ap, kk.ap, T_["aa"].ap, ALU.mult)
        yield
        P.tag = "prep3"
        P.stt(T_["Ap32"].ap, kk.ap, -1.0, T_["EA"].ap, ALU.mult, ALU.mult)
        P.copy(T_["Ap16"].ap, T_["Ap32"].ap, eng="pool")
        P.tt(T_["Rp16"].ap, r, T_["ER"].ap, ALU.mult)
        P.tt(T_["Bm16"].ap, bb.ap, T_["EI"].ap, ALU.mult, eng="pool")
        P.tt(T_["Km16"].ap, keff.ap, T_["EI"].ap, ALU.mult)
        yield
        P.tag = "prep4"
        P.tt(T_["Bc16"].ap, bb.ap, T_["ES"].ap, ALU.mult, eng="pool")
        P.tt(T_["Kc16"].ap, keff.ap, T_["ES"].ap, ALU.mult)
        P.tt(t1.ap, r, keff.ap, ALU.mult)
        P.tt(t1.ap, t1.ap, vrep[:, 2, :], ALU.mult)
        P.reduce(ss.ap, t1.ap.rearrange("p (h c) -> p h c", h=4), ALU.add)
        P.tt(racc[:, ti, :], racc[:, ti, :], ss.ap, ALU.add)
        yield
        P.tag = "prep5"
        for (dstn, srcs) in (("FTAR", ("Ap16", "Rp16")), ("FTBK", ("Bm16", "Km16"))):
            pt = pp.get(); ptv = pt.ap.bitcast(BF16)
            for pair in range(2):
                for qi in range(2):
                    c0 = (pair * 2 + qi) * 128
                    P.transpose(ptv[:, c0:c0 + 128], T_[srcs[qi]][:, pair * 128:(pair + 1) * 128], id16.ap)
            P.copy(T_[dstn].ap.rearrange("p a c -> p (a c)"), ptv[:, 0:512], eng="act")

    UB = []
    for u in range(8):
        f = lambda nm, sh, dt=F32: P.sbuf("%s_u%d" % (nm, u), sh, dt)
        UB.append(dict(Mab=f("Mab", [128, 128]), Q32=f("Q32", [128, 128]), Nrb=f("Nrb", [128, 128], BF16),
                       T3=f("T3", [128, 256], BF16), Z32=f("Z32", [128, 128]), Z16=f("Z16", [128, 128], BF16),
                       MQ=[f("MQ%d" % i, [128, 256]) for i in range(2)],
                       G16=f("G16", [64, 64], BF16), RT=f("RT", [64, 128], BF16)))
    e1s = [P.sbuf("e1_%d" % i, [128, 256]) for i in range(2)]; e2 = P.sbuf("e2_", [128, 256])
    es = P.sbuf("es_", [128, 4]); er = P.sbuf("er_", [128, 4])
    yt = [P.sbuf("yrt%d" % i, [128, 256]) for i in range(2)]
    zv2 = [P.sbuf("zv2_%d" % i, [128, 3, 256], BF16) for i in range(2)]
    H4 = lambda vw: vw.rearrange("p (h c) -> p h c", h=4)
    epi_n = [0]

    def epilogue(ti):
        P.tag = "epi"
        i2 = epi_n[0] % 2; epi_n[0] += 1
        zin = zv2[i2]; e1 = e1s[i2]
        P.dma(zin.ap, zv[ti][:, :, 512:768])
        P.tt(e1.ap, zin[:, 0, :], c0rep[:, 512:768], ALU.mult, eng="pool")
        P.tt(e2.ap, zin[:, 1, :], murep[:, 0, 512:768], ALU.mult, eng="pool")
        P.tt(e1.ap, e1.ap, e2.ap, ALU.add, eng="pool")
        P.tt(e2.ap, zin[:, 2, :], murep[:, 1, 512:768], ALU.mult, eng="pool")
        P.tt(e1.ap, e1.ap, e2.ap, ALU.add, eng="pool")
        P.tt(H4(e1.ap), H4(e1.ap), bc(racc[:, ti, :], [128, 4, 64], 2), ALU.mult, eng="pool")
        y = yacc[:, ti, :]
        P.reduce(es.ap, H4(y), ALU.add)
        P.ts(es.ap, es.ap, 1.0 / 64.0, None, ALU.mult)
        P.tt(H4(y), H4(y), bc(es.ap, [128, 4, 64], 2), ALU.subtract)
        o = yt[i2]
        P.tt(o.ap, y, y, ALU.mult)
        P.reduce(er.ap, H4(o.ap), ALU.add)
        P.ts(er.ap, er.ap, 1.0 / 64.0, 64e-5, ALU.mult, ALU.add)
        P.act(er.ap, er.ap, AF.Sqrt); P.recip(er.ap, er.ap)
        P.tt(H4(o.ap), H4(y), bc(er.ap, [128, 4, 64], 2), ALU.mult)
        P.tt(o.ap, o.ap, vrep[:, 3, :], ALU.mult)
        P.tt(o.ap, o.ap, vrep[:, 4, :], ALU.add)
        P.tt(o.ap, o.ap, e1.ap, ALU.add)
        pg = pp.get()
        P.matmul(pg[:, 0:256], LRg[:, ti * 128:(ti + 1) * 128], gup.ap)
        P.tt(o.ap, o.ap, pg[:, 0:256], ALU.mult)
        P.dma(yo[ti * 128:(ti + 1) * 128, :], o.ap, queue="sp", nowaw=True)

    orders = [list(range(34)), [1, 0] + list(range(33, 1, -1))]

    def start_prep(step):
        return [prep_gen(d, orders[d][step], step % 2) for d in range(2)]

    def advance(gens):
        for g in gens:
            next(g, None)
    for g in start_prep(0):
        for _ in g:
            pass
    for step in range(34):
        par = step % 2
        units = []
        for d in range(2):
            ti = orders[d][step]
            for h in range(4):
                units.append((d, ti, h, PB[(d, par)], UB[d * 4 + h]))
        gens = start_prep(step + 1) if step + 1 < 34 else []
        P.tag = "s1"
        for (d, ti, h, T_, U) in units:
            pair, hs = h // 2, slice(64 * (h % 2), 64 * (h % 2) + 64)
            AR, BK = T_["FTAR"], T_["FTBK"]
            p1 = pp.get(); p2 = pp.get(); p3 = pp.get()
            P.matmul(p1[:, 0:128], AR[hs, pair, 0:128], BK[hs, pair, 0:128])
            P.matmul(p2[:, 0:256], BK[hs, pair, 0:128], AR[hs, pair, 0:256])
            P.matmul(p3[:, 0:256], BK[hs, pair, 128:256], AR[hs, pair, 0:256])
            P.tt(U["Mab"].ap, p1[:, 0:128], masks[:, M_I[d], :], ALU.mult)
            P.tt(U["Q32"].ap, p2[:, 0:128], masks[:, QS_I[d], :], ALU.mult)
            P.tt(U["Nrb"].ap, p2[:, 128:256], maskQN[:, d, 128:256], ALU.mult)
            P.tt(U["T3"].ap, p3[:, 0:256], maskQN[:, d, :], ALU.mult)
        P.tag = "s2"
        for (d, ti, h, T_, U) in units:
            hc = slice(h * 64, (h + 1) * 64)
            px = pp.get()
            P.matmul(px[:, 0:64], U["T3"][:, 0:128], T_["V16"][:, hc])
            P.copy(U["Z32"][:, 0:64], px[:, 0:64], eng="act")
            P.copy(U["Z32"][:, 64:128], T_["Ap32"][:, hc], eng="pool")
        for lv in range(7):
            P.tag = "lv%d" % lv
            for (d, ti, h, T_, U) in units:
                if lv == 0:
                    Mv, Qv = U["Mab"].ap, U["Q32"].ap
                else:
                    mq = U["MQ"][(lv - 1) % 2]
                    Mv, Qv = mq[:, 0:128], mq[:, 128:256]
                pa = pp.get()
                P.matmul(pa[:, 0:128], Qv, U["Z32"].ap)
                if lv < 6:
                    pq = pp.get()
                    P.matmul(pq[:, 0:128], Qv, Mv)
                    P.matmul(pq[:, 128:256], Mv, Qv)
                P.tt(U["Z32"].ap, U["Z32"].ap, pa[:, 0:128], ALU.add)
                if lv < 6:
                    P.copy(U["MQ"][lv % 2].ap, pq[:, 0:256], eng="act")
            advance(gens)
        P.tag = "z16"
        for (d, ti, h, T_, U) in units:
            P.copy(U["Z16"].ap, U["Z32"].ap, eng="pool")
        P.tag = "s3"
        for (d, ti, h, T_, U) in units:
            hc = slice(h * 64, (h + 1) * 64)
            pg = pp.get(); pr = pp.get()
            P.matmul(pg[0:64, 0:64], U["Z16"][:, 64:128], T_["Bc16"][:, hc])
            P.matmul(pr[0:64, 0:128], U["Z16"][:, 64:128], U["Nrb"].ap, start=True, stop=False)
            P.matmul(pr[0:64, 0:128], T_["Rp16"][:, hc], id16.ap, start=False, stop=True)
            P.stt(U["G16"].ap, id32[0:64, 0:64], T_["Wc"][:, h:h + 1], pg[0:64, 0:64], ALU.mult, ALU.add)
            P.copy(U["RT"].ap, pr[0:64, 0:128], eng="act")
        P.tag = "post"
        for (d, ti, h, T_, U) in units:
            hc = slice(h * 64, (h + 1) * 64); ch = d * 4 + h
            ph = pp.get(); py = pp.get()
            P.matmul(ph[0:64, 0:64], T_["Bc16"][:, hc], U["Z16"][:, 0:64], start=True, stop=False)
            P.matmul(ph[0:64, 0:64], T_["Kc16"][:, hc], T_["V16"][:, hc], start=False, stop=False)
            P.matmul(ph[0:64, 0:64], U["G16"].ap, S16[:, ch, :], start=False, stop=True)
            P.matmul(py[:, 0:64], U["Nrb"].ap, U["Z16"][:, 0:64], start=True, stop=False)
            P.matmul(py[:, 0:64], U["T3"][:, 128:256], T_["V16"][:, hc], start=False, stop=False)
            P.matmul(py[:, 0:64], U["RT"].ap, S16[:, ch, :], start=False, stop=True)
            P.copy(S16[:, ch, :], ph[0:64, 0:64], eng="act")
            P.tt(yacc[:, ti, hc], yacc[:, ti, hc], py[:, 0:64], ALU.add)
        for t_done in range(34):
            bpos = (1, 0)[t_done] if t_done < 2 else 35 - t_done
            if max(t_done, bpos) == step:
                epilogue(t_done)
    return P.finish([yo])


_PROGS = {}


_BUILDERS = {"A": build_A, "B1": build_B1, "B2": build_B2, "B3": build_B3, "C1": build_C1, "C2": build_C2}
_TIMES = []


def _run(name, in_maps):
    import time as _t
    t0 = _t.time()
    nc = _BUILDERS[name]()
    t1 = _t.time()
    res = run_bass_kernel_spmd(nc, in_maps, core_ids=list(range(8)))
    _TIMES.append((name, round(t1 - t0, 1), round(_t.time() - t1, 1)))
    return res.results


def _shift3(a):
    p = np.zeros_like(a); n = np.zeros_like(a)
    p[1:256] = a[0:255]; p[257:] = a[256:-1]
    n[0:255] = a[1:256]; n[256:-1] = a[257:]
    return np.stack([a, p, n], 1)


def _colv(v, n):
    return np.ascontiguousarray(np.asarray(v, np.float32).reshape(n, 128).T)


def _rope_tables():
    inv = (10000.0 ** (-np.arange(8, dtype=np.float32) / 8)).astype(np.float32)
    row = np.repeat(np.arange(64, dtype=np.float32), 64); col = np.tile(np.arange(64, dtype=np.float32), 64)
    ang = np.concatenate([row[:, None] * inv, col[:, None] * inv], -1).astype(np.float32)
    cos, sin = np.cos(ang).astype(np.float32), np.sin(ang).astype(np.float32)
    C = np.ones((96, TT), np.float32); S = np.zeros((96, TT), np.float32)
    C[64:80, 256:] = cos.T; C[80:96, 256:] = cos.T
    S[64:80, 256:] = -sin.T; S[80:96, 256:] = sin.T
    return C, S


def _tok(h):
    return np.concatenate([np.arange(128 * h, 128 * h + 128), 256 + np.arange(2048 * h, 2048 * h + 2048)])


def kernel(x, c, ctx, c_ctx, ada_w, ada_b, w_in, gla_gk_up, gla_gk_b, gla_norm_g, mla_q_norm_g, mla_kv_norm_g,
           mla_w_uq, mla_w_ukv, rwkv_shift_mu, rwkv_w0, rwkv_w_up, rwkv_a0, rwkv_a_up, rwkv_g_up, rwkv_k_k,
           rwkv_k_a, rwkv_r_k, rwkv_ln_g, rwkv_ln_b, w_branch, w_out, ln1_g, ln1_b, mlp_w1, mlp_w2, ln2_g, ln2_b,
           _layers=4):
    f32 = lambda a: np.ascontiguousarray(np.asarray(a, np.float32))
    xf = np.concatenate([f32(ctx), f32(x)], 1)
    C, S = _rope_tables()
    s_ = np.arange(128)[:, None]; t_ = np.arange(128)[None, :]
    masks = np.stack([s_ <= t_, s_ >= t_, s_ > t_, s_ < t_], 1).astype(np.float32)
    toks = [_tok(0), _tok(1)]
    for l in range(_layers):
        ims = []
        for cid in range(8):
            b, h = cid // 2, cid % 2
            c2 = np.stack([f32(c)[b], f32(c_ctx)], -1).reshape(8, 128, 2).transpose(1, 0, 2).copy()
            ims.append({"xT": np.ascontiguousarray(xf[b][toks[h]].T), "c2": c2, "ada_w": f32(ada_w[l]),
                        "ada_bT": _colv(ada_b[l], 48), "w_in": f32(w_in[l])})
        rA = _run("A", ims)
        z = np.empty((4, TT, 7232), NPBF)
        for cid in range(8):
            z[cid // 2][toks[cid % 2]] = rA[cid]["zT"].T
        mods = [rA[2 * b]["mod"] for b in range(4)]
        ims = []
        for cid in range(8):
            b, h0 = cid // 2, 4 * (cid % 2)
            zb = z[b]; kr = zb[:, 2208:2240]
            wq = f32(mla_w_uq[l])[:, h0 * 96:(h0 + 4) * 96]; w3 = wq.reshape(384, 4, 96)
            wqr = np.concatenate([w3[:, :, 0:64], w3[:, :, 80:96], w3[:, :, 64:80]], -1).reshape(384, 384)
            k3 = f32(mla_w_ukv[l]).reshape(256, 8, 128)[:, h0:h0 + 4]
            ims.append({"qdT": np.ascontiguousarray(zb[:, 1568:1952].T), "kvdT": np.ascontiguousarray(zb[:, 1952:2208].T),
                        "krT": np.ascontiguousarray(kr.T), "krrT": np.ascontiguousarray(np.concatenate([kr[:, 16:32], kr[:, 0:16]], 1).T),
                        "wq": np.ascontiguousarray(wq), "wqr": np.ascontiguousarray(wqr),
                        "wkn": np.ascontiguousarray(k3[:, :, 0:64].reshape(256, 256)),
                        "wkv": np.ascontiguousarray(k3[:, :, 64:128].reshape(256, 256)),
                        "qg": _colv(mla_q_norm_g[l], 3), "kg": _colv(mla_kv_norm_g[l], 2), "ropeC": C, "ropeS": S})
        rB1 = _run("B1", ims)
        yb = np.empty((4, TT, 512), NPBF)
        for cid in range(8):
            yb[cid // 2][:, 256 * (cid % 2):256 * (cid % 2) + 256] = rB1[cid]["ybT"].T
        ims = []
        for cid in range(8):
            b, h0 = cid // 2, 2 * (cid % 2)
            zb = z[b]; cs = slice(h0 * 64, h0 * 64 + 128)
            gu = f32(gla_gk_up[l]); gbv = f32(gla_gk_b[l])
            ims.append({"qT": np.ascontiguousarray(zb[:, 0:256][:, cs].T), "kT": np.ascontiguousarray(zb[:, 256:512][:, cs].T),
                        "kM": np.ascontiguousarray(zb[:, 256:512][:, cs]),
                        "vM": np.ascontiguousarray(zb[:, 512 + h0 * 128:512 + h0 * 128 + 256]),
                        "ogM": np.ascontiguousarray(zb[:, 1056 + h0 * 128:1056 + h0 * 128 + 256]),
                        "gdT0": np.ascontiguousarray(zb[:, 1024:1040].T), "gdT1": np.ascontiguousarray(zb[:, 1040:1056].T),
                        "gup0": np.ascontiguousarray(gu[0][:, cs]), "gup1": np.ascontiguousarray(gu[1][:, cs]),
                        "gkb0": np.ascontiguousarray(gbv[0][None, cs]), "gkb1": np.ascontiguousarray(gbv[1][None, cs]),
                        "ng": f32(gla_norm_g[l])[None], "masks": masks})
        rB2 = _run("B2", ims)
        ya = np.empty((4, TT, 512), NPBF)
        for cid in range(8):
            ya[cid // 2][:, 256 * (cid % 2):256 * (cid % 2) + 256] = rB2[cid]["ya"].astype(NPBF)
        ims = []
        mu = f32(rwkv_shift_mu[l]); ident = np.eye(128, dtype=np.float32)
        for cid in range(8):
            b, h0 = cid // 2, 4 * (cid % 2)
            zb = z[b]; base = 2240; cs = slice(h0 * 64, h0 * 64 + 256)
            sel = np.concatenate([np.arange(base + o + h0 * 64, base + o + h0 * 64 + 256) for o in (0, 512, 1024)])
            lr = np.stack([_shift3(zb[:, base + o:base + o + 128]).transpose(1, 2, 0) for o in (1536, 1664, 1792)], 0)
            vecs = np.stack([f32(v_[l])[cs] for v_ in (rwkv_k_k, rwkv_k_a, rwkv_r_k, rwkv_ln_g, rwkv_ln_b)], 0)
            ims.append({"zrkv": np.ascontiguousarray(_shift3(zb[:, sel])), "lr": np.ascontiguousarray(lr),
                        "murkv": np.ascontiguousarray(mu[:, sel - base]),
                        "mulr": np.ascontiguousarray(np.stack([mu[:, o:o + 128].T for o in (1536, 1664, 1792)], 1)),
                        "wup": np.ascontiguousarray(np.concatenate([f32(rwkv_w_up[l])[0][:, cs], f32(rwkv_w_up[l])[1][:, cs]], 0)),
                        "w0": np.ascontiguousarray(f32(rwkv_w0[l])[:, None, cs]),
                        "aup": np.ascontiguousarray(np.concatenate([f32(rwkv_a_up[l])[0][:, cs], f32(rwkv_a_up[l])[1][:, cs]], 0)),
                        "a0": np.ascontiguousarray(f32(rwkv_a0[l])[:, None, cs]),
                        "gup": np.ascontiguousarray(f32(rwkv_g_up[l])[:, cs]), "vecs": np.ascontiguousarray(vecs),
                        "masks": masks, "ident": ident})
        rB3 = _run("B3", ims)
        yr = np.empty((4, TT, 512), NPBF)
        for cid in range(8):
            yr[cid // 2][:, 256 * (cid % 2):256 * (cid % 2) + 256] = rB3[cid]["yr"].astype(NPBF)
        ims = []
        for cid in range(8):
            b, h = cid // 2, cid % 2
            y = np.concatenate([ya[b], yb[b], yr[b]], 1)[toks[h]]
            ims.append({"xT": np.ascontiguousarray(xf[b][toks[h]].T), "yT": np.ascontiguousarray(y.T),
                        "gT": np.ascontiguousarray(rA[cid]["zT"][4160:]), "mod": mods[b],
                        "w_branch": f32(w_branch[l]).reshape(1536, 1024), "w_out": f32(w_out[l]),
                        "ln_g": _colv(ln1_g[l], 8), "ln_b": _colv(ln1_b[l], 8)})
        rC1 = _run("C1", ims)
        ims = [{"xT": rC1[cid]["x1T"], "mod": mods[cid // 2], "w1": f32(mlp_w1[l]), "w2": f32(mlp_w2[l]),
                "ln_g": _colv(ln2_g[l], 8), "ln_b": _colv(ln2_b[l], 8)} for cid in range(8)]
        rC2 = _run("C2", ims)
        for cid in range(8):
            xf[cid // 2][toks[cid % 2]] = rC2[cid]["x2T"].T
    return np.ascontiguousarray(xf[:, 256:, :])
```
